# Optimizing a Trainium2 kernel written in Bass

```python
import jax
import jax.numpy as jnp
from jax import lax
import numpy as np

D_MODEL = 2048
BATCH = 4
SEQ = 4096
DEPTH = 4

N_MEM = 256
CHUNK = 64
EPS = 1e-6
A_HEADS = 8
A_DK = 128
A_DV = 128
A_CONV = 4
B_HEADS = 8
B_DK = 128
B_DV = 128
C_HEADS = 4
C_DH = 256
N_BRANCH = 3
D_FF = 4 * D_MODEL
A_QK_W = A_HEADS * A_DK
A_V_W = A_HEADS * A_DV
B_K_W = B_HEADS * B_DK
B_V_W = B_HEADS * B_DV
C_W = C_HEADS * C_DH
IN_SIZES = (A_QK_W, A_QK_W, A_V_W, A_V_W, A_HEADS, A_HEADS, B_K_W, B_K_W, B_V_W, B_V_W, C_W, N_BRANCH * D_MODEL)
IN_TOTAL = sum(IN_SIZES)

kernel_name = 'hybrid_deltanet_hgrn2_memxattn_block'


def rmsnorm(x, g):
    xf = x.astype(jnp.float32)
    y = xf * lax.rsqrt(jnp.mean(xf * xf, axis=-1, keepdims=True) + EPS)
    return (y * g.astype(jnp.float32)).astype(x.dtype)


def gated_head_norm(o, z, g):
    of = o.astype(jnp.float32)
    y = of * lax.rsqrt(jnp.mean(of * of, axis=-1, keepdims=True) + EPS)
    return y * g.astype(jnp.float32) * jax.nn.silu(z.astype(jnp.float32))


def l2norm(u):
    return u * lax.rsqrt(jnp.sum(u * u, axis=-1, keepdims=True) + EPS)


def causal_dwconv(u, w):
    k = w.shape[0]
    return lax.conv_general_dilated(u, w[:, None, :], window_strides=(1,), padding=[(k - 1, 0)], dimension_numbers=('NWC', 'WIO', 'NWC'), feature_group_count=u.shape[-1])


def to_chunks(u):
    b, s, h, d = u.shape
    return u.reshape(b, s // CHUNK, CHUNK, h, d).transpose(0, 3, 1, 2, 4)


def from_chunks(o):
    n, b, h, c, d = o.shape
    return o.transpose(1, 0, 3, 2, 4).reshape(b, n * c, h, d)


def gated_deltanet(q, k, v, beta, logdecay):
    dk = q.shape[-1]
    dv = v.shape[-1]
    qc = to_chunks(l2norm(q) * (dk ** -0.5))
    kc = to_chunks(l2norm(k))
    vc = to_chunks(v)
    bc = to_chunks(beta[..., None])[..., 0]
    gam = jnp.cumsum(to_chunks(logdecay[..., None])[..., 0], axis=-1)
    causal = jnp.tril(jnp.ones((CHUNK, CHUNK), dtype=bool))
    strict = jnp.tril(jnp.ones((CHUNK, CHUNK), dtype=bool), k=-1)
    dec = jnp.exp(jnp.where(causal, gam[..., :, None] - gam[..., None, :], -jnp.inf))
    lower = jnp.where(strict, bc[..., :, None] * jnp.einsum('bhnid,bhnjd->bhnij', kc, kc) * dec, 0.0)
    a_mat = lower + jnp.eye(CHUNK, dtype=lower.dtype)
    rhs = jnp.concatenate([vc * bc[..., None], kc * (bc * jnp.exp(gam))[..., None]], axis=-1)
    sol = lax.linalg.triangular_solve(a_mat, rhs, left_side=True, lower=True, unit_diagonal=True)
    u_c = sol[..., :dv]
    w_c = sol[..., dv:]
    a_qk = jnp.einsum('bhnid,bhnjd->bhnij', qc, kc) * dec
    q_dec = qc * jnp.exp(gam)[..., None]
    k_dec = kc * jnp.exp(gam[..., -1:] - gam)[..., None]
    g_last = jnp.exp(gam[..., -1])

    def step(state, inp):
        q_i, k_i, u_i, w_i, a_i, g_i = inp
        u_new = u_i - w_i @ state
        o = q_i @ state + a_i @ u_new
        state = g_i[..., None, None] * state + jnp.swapaxes(k_i, -1, -2) @ u_new
        return state, o

    xs = tuple(jnp.moveaxis(t, 2, 0) for t in (q_dec, k_dec, u_c, w_c, a_qk, g_last))
    s0 = jnp.zeros((q_dec.shape[0], q_dec.shape[1], dk, dv), jnp.float32)
    _, o = lax.scan(step, s0, xs)
    return from_chunks(o)


def hgrn2(q, f, i, lb):
    h, dk = q.shape[-2], q.shape[-1]
    lb = lb.reshape(h, dk)
    log_f = jnp.logaddexp(jnp.log(lb), jnp.log1p(-lb) + jax.nn.log_sigmoid(f))
    k = -jnp.expm1(log_f)
    qc = to_chunks(jax.nn.silu(q) * (dk ** -0.5))
    kc = to_chunks(k)
    vc = to_chunks(i)
    bcum = jnp.cumsum(to_chunks(log_f), axis=-2)
    causal = jnp.tril(jnp.ones((CHUNK, CHUNK), dtype=bool))

    def step(state, inp):
        q_i, k_i, v_i, b_i = inp
        dec = jnp.exp(jnp.where(causal[:, :, None], b_i[..., :, None, :] - b_i[..., None, :, :], -jnp.inf))
        a_i = jnp.einsum('bhtd,bhjd,bhtjd->bhtj', q_i, k_i, dec)
        o = (q_i * jnp.exp(b_i)) @ state + a_i @ v_i
        b_last = b_i[..., -1:, :]
        state = jnp.exp(b_last[..., 0, :])[..., None] * state + jnp.einsum('bhjd,bhje->bhde', k_i * jnp.exp(b_last - b_i), v_i)
        return state, o

    xs = tuple(jnp.moveaxis(t, 2, 0) for t in (qc, kc, vc, bcum))
    s0 = jnp.zeros((qc.shape[0], qc.shape[1], dk, i.shape[-1]), jnp.float32)
    _, o = lax.scan(step, s0, xs)
    return from_chunks(o)


def memory_attention(q, mem_n, w_kv):
    kv = mem_n @ w_kv
    k, v = jnp.split(kv, 2, axis=-1)
    b, m = k.shape[0], k.shape[1]
    k = k.reshape(b, m, C_HEADS, C_DH)
    v = v.reshape(b, m, C_HEADS, C_DH)
    s = jnp.einsum('bshd,bmhd->bhsm', q, k).astype(jnp.float32) * (C_DH ** -0.5)
    p = jax.nn.softmax(s, axis=-1)
    return jnp.einsum('bhsm,bmhd->bshd', p.astype(v.dtype), v)


def setup_inputs(seed: int = 0) -> dict:
    key = jax.random.key(seed)
    ks = jax.random.split(key, 24)
    f32 = jnp.float32

    def nrm(k, shape, scale):
        return jax.random.normal(k, shape, f32) * scale

    def gain(k, shape):
        return 1.0 + 0.02 * jax.random.normal(k, shape, f32)

    x = nrm(ks[0], (BATCH, SEQ, D_MODEL), 1.0)
    mem = nrm(ks[1], (BATCH, N_MEM, D_MODEL), 1.0)
    g_pre_mix = gain(ks[2], (DEPTH, D_MODEL))
    w_in = nrm(ks[3], (DEPTH, D_MODEL, IN_TOTAL), D_MODEL ** -0.5)
    conv_a = nrm(ks[4], (DEPTH, A_CONV, 2 * A_QK_W + A_V_W), A_CONV ** -0.5)
    a_log = jnp.log(jax.random.uniform(ks[5], (DEPTH, A_HEADS), f32, 1.0, 16.0))
    dt = jnp.exp(jax.random.uniform(ks[6], (DEPTH, A_HEADS), f32, float(np.log(1e-3)), float(np.log(1e-1))))
    dt_bias = dt + jnp.log(-jnp.expm1(-dt))
    gn_a = gain(ks[7], (DEPTH, A_DV))
    lb_raw = nrm(ks[8], (DEPTH, B_K_W), 0.1)
    gn_b = gain(ks[9], (DEPTH, B_DV))
    g_mem = gain(ks[10], (DEPTH, D_MODEL))
    w_kv_mem = nrm(ks[11], (DEPTH, D_MODEL, 2 * C_W), D_MODEL ** -0.5)
    w_br_a = nrm(ks[12], (DEPTH, A_V_W, D_MODEL), A_V_W ** -0.5)
    w_br_b = nrm(ks[13], (DEPTH, B_V_W, D_MODEL), B_V_W ** -0.5)
    w_br_c = nrm(ks[14], (DEPTH, C_W, D_MODEL), C_W ** -0.5)
    w_out = nrm(ks[15], (DEPTH, D_MODEL, D_MODEL), D_MODEL ** -0.5)
    g_post_mix = gain(ks[16], (DEPTH, D_MODEL))
    g_pre_mlp = gain(ks[17], (DEPTH, D_MODEL))
    w_mlp_in = nrm(ks[18], (DEPTH, D_MODEL, D_FF), D_MODEL ** -0.5)
    w_mlp_out = nrm(ks[19], (DEPTH, D_FF, D_MODEL), D_FF ** -0.5)
    g_post_mlp = gain(ks[20], (DEPTH, D_MODEL))
    return {'x': x, 'mem': mem, 'g_pre_mix': g_pre_mix, 'w_in': w_in, 'conv_a': conv_a, 'a_log': a_log, 'dt_bias': dt_bias, 'gn_a': gn_a, 'lb_raw': lb_raw, 'gn_b': gn_b, 'g_mem': g_mem, 'w_kv_mem': w_kv_mem, 'w_br_a': w_br_a, 'w_br_b': w_br_b, 'w_br_c': w_br_c, 'w_out': w_out, 'g_post_mix': g_post_mix, 'g_pre_mlp': g_pre_mlp, 'w_mlp_in': w_mlp_in, 'w_mlp_out': w_mlp_out, 'g_post_mlp': g_post_mlp}


def reference(x, mem, g_pre_mix, w_in, conv_a, a_log, dt_bias, gn_a, lb_raw, gn_b, g_mem, w_kv_mem, w_br_a, w_br_b, w_br_c, w_out, g_post_mix, g_pre_mlp, w_mlp_in, w_mlp_out, g_post_mlp):
    b, s, _ = x.shape
    split_idx = np.cumsum(IN_SIZES)[:-1].tolist()
    lb_all = jnp.cumsum(jax.nn.softmax(lb_raw.astype(jnp.float32), axis=0), axis=0)
    lb_all = lb_all - lb_all[0]
    for l in range(DEPTH):
        h = rmsnorm(x, g_pre_mix[l])
        proj = h @ w_in[l]
        aq, ak, av, az, ab, aa, bq, bf, bi, bz, cq, gates = jnp.split(proj, split_idx, axis=-1)
        qkv = jax.nn.silu(causal_dwconv(jnp.concatenate([aq, ak, av], axis=-1), conv_a[l]))
        aq, ak, av = jnp.split(qkv, [A_QK_W, 2 * A_QK_W], axis=-1)
        beta = jax.nn.sigmoid(ab.astype(jnp.float32))
        logdecay = -jnp.exp(a_log[l].astype(jnp.float32)) * jax.nn.softplus(aa.astype(jnp.float32) + dt_bias[l].astype(jnp.float32))
        oa = gated_deltanet(aq.reshape(b, s, A_HEADS, A_DK).astype(jnp.float32), ak.reshape(b, s, A_HEADS, A_DK).astype(jnp.float32), av.reshape(b, s, A_HEADS, A_DV).astype(jnp.float32), beta, logdecay)
        oa = gated_head_norm(oa, az.reshape(b, s, A_HEADS, A_DV), gn_a[l]).reshape(b, s, A_V_W).astype(x.dtype)
        ob = hgrn2(bq.reshape(b, s, B_HEADS, B_DK).astype(jnp.float32), bf.reshape(b, s, B_HEADS, B_DK).astype(jnp.float32), bi.reshape(b, s, B_HEADS, B_DV).astype(jnp.float32), lb_all[l])
        ob = gated_head_norm(ob, bz.reshape(b, s, B_HEADS, B_DV), gn_b[l]).reshape(b, s, B_V_W).astype(x.dtype)
        mem_n = rmsnorm(mem, g_mem[l])
        oc = memory_attention(cq.reshape(b, s, C_HEADS, C_DH), mem_n, w_kv_mem[l]).reshape(b, s, C_W).astype(x.dtype)
        ga, gb, gc = jnp.split(jax.nn.sigmoid(gates), N_BRANCH, axis=-1)
        merged = ga * (oa @ w_br_a[l]) + gb * (ob @ w_br_b[l]) + gc * (oc @ w_br_c[l])
        x = x + rmsnorm(merged @ w_out[l], g_post_mix[l])
        h2 = rmsnorm(x, g_pre_mlp[l])
        m = jnp.square(jax.nn.relu(h2 @ w_mlp_in[l])) @ w_mlp_out[l]
        x = x + rmsnorm(m, g_post_mlp[l])
    return x
```

```python
import numpy as np
import concourse.bass as bass
import concourse.mybir as mybir
from contextlib import ExitStack

F32 = mybir.dt.float32
BF16 = mybir.dt.bfloat16
I32 = mybir.dt.int32
AF = mybir.ActivationFunctionType
ALU = mybir.AluOpType
AX = mybir.AxisListType

COMPUTE = ("pe", "dve", "act", "pool")
DMAQ = ("sp", "pq")
NDMASEM = {"sp": 24, "pq": 12}


class T:
    __slots__ = ("t", "name", "w", "r", "rd")

    def __init__(self, t, name=""):
        self.t = t
        self.name = name
        self.w = None
        self.r = {}
        self.rd = []

    def __getitem__(self, idx):
        return self.t[idx]


class Ins:
    __slots__ = ("eng", "fn", "inc", "idx", "deps", "dma_sem", "dma_val", "ticket")

    def __init__(self, eng, fn, inc):
        self.eng = eng
        self.fn = fn
        self.inc = inc
        self.deps = []
        self.dma_sem = None
        self.dma_val = 0
        self.ticket = None


class Prog:
    def __init__(self, nc):
        self.nc = nc
        self.stack = ExitStack()
        self.all = []
        self.per = {e: [] for e in COMPUTE + DMAQ}
        self.nalloc = 0

    def sb(self, shape, dt=F32, name=None):
        self.nalloc += 1
        name = name or f"sb{self.nalloc}"
        t = self.stack.enter_context(self.nc.sbuf_tensor(name, list(shape), dt))
        return T(t, name)

    def ps(self, shape, dt=F32, name=None):
        self.nalloc += 1
        name = name or f"ps{self.nalloc}"
        t = self.stack.enter_context(self.nc.psum_tensor(name, list(shape), dt))
        return T(t, name)

    def dram(self, name, shape, dt=F32, kind="Internal"):
        t = self.nc.dram_tensor(name, list(shape), dt, kind=kind)
        return T(t, name)

    def view(self, t, name=""):
        return T(t.t if isinstance(t, T) else t, name)

    def op(self, eng, fn, reads=(), writes=(), inc=True):
        ins = Ins(eng, fn, inc)
        isdma = eng in DMAQ
        deps = {}

        def add(d, kind):
            if d is None or d is ins:
                return
            if d.eng == eng and not isdma:
                if kind != "raw" or eng == "pe":
                    return
            deps[id(d)] = d

        for t in reads:
            add(t.w, "raw")
        for t in writes:
            add(t.w, "waw")
            for r in t.r.values():
                add(r, "war")
            for r in t.rd:
                add(r, "war")
        for t in reads:
            if isdma:
                t.rd.append(ins)
            else:
                t.r[eng] = ins
        for t in writes:
            t.w = ins
            t.r = {}
            t.rd = []
        ins.deps = list(deps.values())
        ins.idx = len(self.per[eng])
        self.per[eng].append(ins)
        self.all.append(ins)
        return ins

    def dma(self, out_ap, in_ap, reads=(), writes=(), q="sp", **kw):
        return self.op(q, lambda e: e.dma_start(out=out_ap, in_=in_ap, **kw), reads, writes)

    def emit(self):
        nc = self.nc
        engobj = {"pe": nc.tensor, "dve": nc.vector, "act": nc.scalar, "pool": nc.gpsimd,
                  "sp": nc.sync, "pq": nc.gpsimd}
        sems = {e: self.stack.enter_context(nc.semaphore(f"s_{e}")) for e in COMPUTE}
        dsem = {q: [self.stack.enter_context(nc.semaphore(f"d_{q}{i}")) for i in range(NDMASEM[q])]
                for q in DMAQ}
        for e in COMPUTE:
            if self.per[e]:
                self.per[e][-1].inc = True
        for e in COMPUTE:
            c = 0
            lst = self.per[e]
            for ins in lst:
                if ins.inc:
                    c += 1
                    ins.ticket = c
            nxt = None
            for ins in reversed(lst):
                if ins.inc:
                    nxt = ins.ticket
                else:
                    ins.ticket = nxt
        dcount = {q: [0] * NDMASEM[q] for q in DMAQ}
        dprev = {q: [None] * NDMASEM[q] for q in DMAQ}
        for q in DMAQ:
            for i, ins in enumerate(self.per[q]):
                k = i % NDMASEM[q]
                dcount[q][k] += 16
                ins.dma_sem = dsem[q][k]
                ins.dma_val = dcount[q][k]
                ins.ticket = (q, k)
        issue_eng = {"pe": "pe", "dve": "dve", "act": "act", "pool": "pool", "sp": "sp", "pq": "pool"}
        seen = {e: {} for e in ("pe", "dve", "act", "pool", "sp")}
        nwait = 0
        for ins in self.all:
            ie = issue_eng[ins.eng]
            eo = engobj[ins.eng]
            sn = seen[ie]
            waits = []
            for d in ins.deps:
                if d.eng in DMAQ:
                    waits.append((d.dma_sem, d.dma_val))
                else:
                    waits.append((sems[d.eng], d.ticket))
            if ins.eng in DMAQ:
                q, k = ins.ticket
                prev = ins.dma_val - 16
                if prev > 0:
                    waits.append((ins.dma_sem, prev))
            for s, v in waits:
                key = id(s)
                if sn.get(key, 0) >= v:
                    continue
                sn[key] = v
                eo.wait_ge(s, v)
                nwait += 1
            r = ins.fn(eo)
            if ins.eng in DMAQ:
                r.then_inc(ins.dma_sem, 16)
            elif ins.inc:
                r.then_inc(sems[ins.eng], 1)
        for q in DMAQ:
            for k in range(NDMASEM[q]):
                if dcount[q][k] > 0:
                    nc.sync.wait_ge(dsem[q][k], dcount[q][k])
        for e in COMPUTE:
            lst = self.per[e]
            if lst:
                nc.sync.wait_ge(sems[e], lst[-1].ticket)
        self.stats = {e: len(self.per[e]) for e in self.per}
        self.stats["waits"] = nwait
        return self.stats

    def close(self):
        self.stack.close()
from concourse.bass_utils import run_bass_kernel_spmd

D = 2048
KT = 16
NMEM = 256
EPS = 1e-6
F_AQ, F_AK, F_AV, F_AZ, F_BQ, F_BF, F_BI, F_BZ, F_CQ, F_G = 0, 8, 16, 24, 32, 40, 48, 56, 64, 72
PAD = 3
RS = 128 ** -0.5


def bcast_mid(ap2, n_inner):
    a = ap2.ap
    return bass.AP(ap2.tensor, ap2.offset, [list(a[0]), list(a[1]), [0, n_inner]])


def bcast_outer(ap2, n_outer):
    a = ap2.ap
    return bass.AP(ap2.tensor, ap2.offset, [list(a[0]), [0, n_outer], list(a[1])])


def v3(ap, inner):
    return ap.rearrange("p (a b) -> p a b", b=inner)


class Rot:
    def __init__(self, items):
        self.items = items
        self.i = 0

    def next(self):
        x = self.items[self.i % len(self.items)]
        self.i += 1
        return x


class K:
    def __init__(self, S, L):
        self.S, self.L = S, L
        self.NB = S // 512
        nc = bass.Bass("TRN2", target_bir_lowering=False)
        self.nc = nc
        P = self.P = Prog(nc)
        NB = self.NB
        ext = lambda n, s: P.dram(n, s, F32, kind="ExternalInput")
        self.xT_in = ext("xT", [D, S])
        self.memT = ext("memT", [D, NMEM])
        self.w_in = ext("w_in", [L, 120, 128, KT, 128])
        self.w_sm = ext("w_sm", [L, 128, KT, 16])
        self.w_kv = ext("w_kv", [L, 16, 128, KT, 128])
        self.w_br = ext("w_br", [L, 3, 16, 128, 8, 128])
        self.w_out = ext("w_out", [L, 16, 128, KT, 128])
        self.w1 = ext("w1", [L, 64, 128, KT, 128])
        self.w2 = ext("w2", [L, 16, 128, 64, 128])
        self.gains = ext("gains", [L, 128, 5, KT])
        self.convw = ext("convw", [L, 128, 24, 4])
        self.hvec = ext("hvec", [L, 128, 16])
        self.gnv = ext("gnv", [L, 128, 2])
        self.lbraw = ext("lbraw", [128, 8, L])
        self.outT = P.dram("outT", [D, S], F32, kind="ExternalOutput")
        self.XT = P.dram("XT", [D, S], F32)
        self.PROJ = P.dram("PROJ", [120, 128, S + PAD], F32)
        self.PSM = P.dram("PSM", [16, S], F32)
        self.OD = P.dram("OD", [24, 128, S], F32)
        self.XT_T = [T(None, f"XT{b}") for b in range(NB)]
        self.PROJ_T = [[T(None, f"PJ{f}_{b}") for b in range(NB)] for f in range(120)]
        self.PSM_T = [T(None, f"PSM{b}") for b in range(NB)]
        self.OD_T = [[T(None, f"OD{f}_{b}") for b in range(NB)] for f in range(24)]
        self.WT_ = T(None, "weights")
        self.OUT_T = T(None, "out")

        self.ones = P.sb([128, 128], F32, "ones")
        self.ident = P.sb([128, 128], F32, "ident")
        self.nlow_s = P.sb([128, 128], F32, "nlow_s")
        self.nup_s = P.sb([128, 128], F32, "nup_s")
        self.up_i = P.sb([128, 128], F32, "up_i")
        self.resetm = P.sb([128, 512], F32, "resetm")
        self.zero3 = P.sb([128, 4], F32, "zero3")
        self.epsb = P.sb([128, 1], F32, "epsb")
        self.oneb = P.sb([128, 1], F32, "oneb")
        ones, ident, nlow_s, nup_s, up_i, resetm, zero3 = self.ones, self.ident, self.nlow_s, self.nup_s, self.up_i, self.resetm, self.zero3
        P.op("pool", lambda e: e.memset(ones[:], 1.0), [], [ones])
        P.op("pool", lambda e: e.memset(zero3[:], 0.0), [], [zero3])
        P.op("pool", lambda e: e.memset(self.epsb[:], EPS), [], [self.epsb])
        P.op("pool", lambda e: e.memset(self.oneb[:], 1.0), [], [self.oneb])
        P.op("pool", lambda e: e.memset(resetm[:], 1.0), [], [resetm])
        P.op("pool", lambda e: e.memset(resetm[:, ::64], 0.0), [resetm], [resetm])
        P.op("pool", lambda e: e.affine_select(out=ident[:], in_=ones[:], pattern=[[-1, 128]], compare_op=ALU.is_equal, fill=0.0, base=0, channel_multiplier=1), [ones], [ident])
        P.op("pool", lambda e: e.memset(nlow_s[:], -1.0), [], [nlow_s])
        P.op("pool", lambda e: e.memset(nup_s[:], -1.0), [], [nup_s])
        P.op("pool", lambda e: e.affine_select(out=nlow_s[:], in_=nlow_s[:], pattern=[[-1, 128]], compare_op=ALU.is_ge, fill=0.0, base=-1, channel_multiplier=1), [nlow_s], [nlow_s])
        P.op("pool", lambda e: e.memset(nlow_s[64:128, 0:64], 0.0), [nlow_s], [nlow_s])
        P.op("pool", lambda e: e.affine_select(out=nup_s[:], in_=nup_s[:], pattern=[[1, 128]], compare_op=ALU.is_ge, fill=0.0, base=-1, channel_multiplier=-1), [nup_s], [nup_s])
        P.op("pool", lambda e: e.memset(nup_s[0:64, 64:128], 0.0), [nup_s], [nup_s])
        P.op("pool", lambda e: e.affine_select(out=up_i[:], in_=ones[:], pattern=[[1, 128]], compare_op=ALU.is_ge, fill=0.0, base=0, channel_multiplier=-1), [ones], [up_i])
        P.op("pool", lambda e: e.memset(up_i[0:64, 64:128], 0.0), [up_i], [up_i])

        self.psum = Rot([P.ps([128, 512], F32, f"psb{i}") for i in range(8)])
        self.wstage = Rot([P.sb([128, 16, 128], F32, f"wst{i}") for i in range(2)])
        self.wbf = Rot([P.sb([128, 16, 128], BF16, f"wbf{i}") for i in range(4)])
        self.slotbuf = P.sb([128, 32, 512], F32, "slots")
        self.slots = [T(self.slotbuf[:, i, :], f"slot{i}") for i in range(32)]
        self.hb = P.sb([128, KT, 512], BF16, "hb")
        self.big = P.sb([128, KT, 512], BF16, "big")
        self.stg = Rot([P.sb([128, 512], F32, f"stg{i}") for i in range(3)])
        self.tp = Rot([P.sb([128, 512], F32, f"tp{i}") for i in range(6)])
        self.m128 = Rot([P.sb([128, 128], F32, f"m{i}") for i in range(28)])
        self.small = Rot([P.sb([128, 8], F32, f"sm{i}") for i in range(32)])
        self.raw = Rot([P.sb([128, 516], F32, f"raw{i}") for i in range(3)])
        self.kmem = P.sb([128, 8, NMEM], BF16, "kmem")
        self.vmem = P.sb([128, 2, 1024], BF16, "vmem")
        self.gsb = P.sb([128, 5, KT], F32, "gsb")
        self.cwsb = P.sb([128, 24, 4], F32, "cwsb")
        self.hv = P.sb([128, 16], F32, "hv")
        self.negA = P.sb([128, 8], F32, "negA")
        self.gn = P.sb([128, 2], F32, "gn")
        self.lbr = P.sb([128, 8, L], F32, "lbr")
        self.lbe = P.sb([128, 8, L], F32, "lbe")
        self.lbs = P.sb([128, 8], F32, "lbs")
        self.lball = P.sb([128, L, 8], F32, "lball")
        self.omlb = P.sb([128, L, 8], F32, "omlb")
        self.wsmst = P.sb([128, KT, 16], F32, "wsmst")
        self.wsmb = P.sb([128, KT, 16], BF16, "wsmb")
        self.SA = [P.sb([128, 128], F32, f"SA{h}") for h in range(8)]
        self.SB = [P.sb([128, 128], F32, f"SB{h}") for h in range(8)]
        self.evq = 0

    def a(self, t):
        return t.t.ap()

    def mm(self, ps_ap, lhsT, rhs, reads, ps_t, start=True, stop=True, inc=None):
        return self.P.op("pe", lambda e: e.matmul(ps_ap, lhsT, rhs, start=start, stop=stop), reads, [ps_t], inc=(stop if inc is None else inc))

    def tr(self, ps_ap, in_ap, reads, ps_t):
        ident = self.ident
        return self.P.op("pe", lambda e: e.transpose(ps_ap, in_ap, ident[:]), reads + [ident], [ps_t], inc=True)

    def act(self, out, in_, func, reads, writes, **kw):
        return self.P.op("act", lambda e: e.activation(out=out, in_=in_, func=func, **kw), reads, writes)

    def ecopy(self, out, in_, reads, writes, eng=None):
        self.evq += 1
        if eng is None:
            eng = "act" if self.evq % 2 else "dve"
        if eng == "act":
            return self.P.op("act", lambda e: e.copy(out=out, in_=in_), reads, writes)
        return self.P.op("dve", lambda e: e.tensor_copy(out=out, in_=in_), reads, writes)

    def tt(self, out, in0, in1, op, reads, writes):
        return self.P.op("dve", lambda e: e.tensor_tensor(out=out, in0=in0, in1=in1, op=op), reads, writes)

    def ts(self, out, in0, s1, op0, reads, writes, s2=None, op1=None):
        if op1 is None:
            return self.P.op("dve", lambda e: e.tensor_scalar(out=out, in0=in0, scalar1=s1, scalar2=None, op0=op0), reads, writes)
        return self.P.op("dve", lambda e: e.tensor_scalar(out=out, in0=in0, scalar1=s1, scalar2=s2, op0=op0, op1=op1), reads, writes)

    def stt(self, out, in0, scalar, in1, op0, op1, reads, writes):
        return self.P.op("dve", lambda e: e.scalar_tensor_tensor(out=out, in0=in0, scalar=scalar, in1=in1, op0=op0, op1=op1), reads, writes)

    def recip(self, out, in_, reads, writes):
        return self.P.op("dve", lambda e: e.reciprocal(out=out, in_=in_), reads, writes)

    def rstd_from(self, src_fn, src_reads, nkt, scale, sqs, n=512):
        ps = self.psum.next()
        for kt in range(nkt):
            sq = sqs[kt % 2] if nkt > 1 else sqs[0]
            self.act(sq[:, 0:n], src_fn(kt), AF.Square, src_reads, [sq])
            self.mm(ps[:, 0:n], self.ones[:], sq[:, 0:n], [self.ones, sq], ps, start=(kt == 0), stop=(kt == nkt - 1), inc=True)
        r = sqs[0]
        self.act(r[:, 0:n], ps[:, 0:n], AF.Sqrt, [ps, self.epsb], [r], scale=scale, bias=self.epsb[:])
        self.recip(r[:, 0:n], r[:, 0:n], [r], [r])
        return r

    def load_w(self, w_ap):
        k = w_ap.shape[1]
        st = self.wstage.next()
        self.P.dma(st[:, 0:k, :], w_ap, reads=[self.WT_], writes=[st])
        wb = self.wbf.next()
        self.P.op("pool", lambda e: e.tensor_copy(out=wb[:, 0:k, :], in_=st[:, 0:k, :]), [st], [wb])
        return wb

    def diag_cols(self, src, scratch):
        ident = self.ident
        self.tt(v3(scratch[:], 128), v3(src[:], 128), bcast_outer(ident[:, :], 4), ALU.mult, [src, ident], [scratch])
        c = self.small.next()
        self.P.op("dve", lambda e: e.tensor_reduce(out=c[:, 0:4], in_=v3(scratch[:], 128), axis=AX.X, op=ALU.add), [scratch], [c])
        return c

    def prologue(self):
        P, L, NB = self.P, self.L, self.NB
        PROJa, XTa, xTa = self.a(self.PROJ), self.a(self.XT), self.a(self.xT_in)
        for f in range(24):
            P.dma(PROJa[f, :, 0:PAD], self.zero3[:, 0:PAD], reads=[self.zero3], writes=[self.PROJ_T[f][0]], q="pq")
        for b in range(NB):
            P.dma(XTa[:, b * 512:(b + 1) * 512], xTa[:, b * 512:(b + 1) * 512], reads=[self.WT_], writes=[self.XT_T[b]], q="pq")
        lbr, lbe, lbs, lball, omlb = self.lbr, self.lbe, self.lbs, self.lball, self.omlb
        P.dma(lbr[:], self.a(self.lbraw), reads=[self.WT_], writes=[lbr])
        self.act(lbe[:], lbr[:], AF.Exp, [lbr], [lbe])
        P.op("dve", lambda e: e.tensor_reduce(out=lbs[:], in_=lbe[:], axis=AX.X, op=ALU.add), [lbe], [lbs])
        self.recip(lbs[:], lbs[:], [lbs], [lbs])
        P.op("dve", lambda e: e.memset(lball[:, 0, :], 0.0), [], [lball])
        for l in range(1, L):
            tl = self.small.next()
            self.tt(tl[:], lbe[:, :, l], lbs[:], ALU.mult, [lbe, lbs], [tl])
            self.tt(lball[:, l, :], lball[:, l - 1, :], tl[:], ALU.add, [lball, tl], [lball])
        self.ts(omlb[:], lball[:], -1.0, ALU.mult, [lball], [omlb], s2=1.0, op1=ALU.add)

    def layer_params(self, l):
        P = self.P
        P.dma(self.gsb[:], self.a(self.gains)[l], reads=[self.WT_], writes=[self.gsb])
        P.dma(self.cwsb[:], self.a(self.convw)[l], reads=[self.WT_], writes=[self.cwsb])
        P.dma(self.hv[:], self.a(self.hvec)[l], reads=[self.WT_], writes=[self.hv])
        P.dma(self.gn[:], self.a(self.gnv)[l], reads=[self.WT_], writes=[self.gn])
        P.dma(self.wsmst[:], self.a(self.w_sm)[l], reads=[self.WT_], writes=[self.wsmst])
        wsmb, wsmst, negA, hv = self.wsmb, self.wsmst, self.negA, self.hv
        P.op("pool", lambda e: e.tensor_copy(out=wsmb[:], in_=wsmst[:]), [wsmst], [wsmb])
        self.act(negA[:], hv[:, 0:8], AF.Exp, [hv], [negA])
        self.ts(negA[:], negA[:], -1.0, ALU.mult, [negA], [negA])
        for h in range(8):
            P.op("pool", lambda e, h=h: e.memset(self.SA[h][:], 0.0), [], [self.SA[h]])
            P.op("pool", lambda e, h=h: e.memset(self.SB[h][:], 0.0), [], [self.SB[h]])

    def phase1(self, l, b):
        P = self.P
        sl = self.slots
        xts = sl[0:16]
        hb, gsb = self.hb, self.gsb
        XTv = self.a(self.XT).rearrange("(kt p) s -> p kt s", p=128)
        P.dma(self.slotbuf[:, 0:16, :], XTv[:, :, b * 512:(b + 1) * 512], reads=[self.XT_T[b]], writes=xts)
        import os
        kp1 = int(os.environ.get("KP1", "99"))
        if kp1 < 2:
            return
        r = self.rstd_from(lambda kt: xts[kt][:], xts, KT, 1.0 / D, [sl[16], sl[17]])
        if kp1 < 3:
            return
        for kt in range(KT):
            self.stt(hb[:, kt, :], xts[kt][:], gsb[:, 0, kt:kt + 1], r[:], ALU.mult, ALU.mult, [xts[kt], gsb, r], [hb])
        PROJa = self.a(self.PROJ)
        w_in = self.a(self.w_in)
        if kp1 < 4:
            return
        for f in range(120 if kp1 > 4 else 2):
            wb = self.load_w(w_in[l, f])
            ps = self.psum.next()
            for kt in range(KT):
                self.mm(ps[:], wb[:, kt, :], hb[:, kt, :], [wb, hb], ps, start=(kt == 0), stop=(kt == KT - 1))
            s = self.stg.next()
            if F_AZ <= f < F_BF or F_BZ <= f < F_CQ:
                self.act(s[:], ps[:], AF.Silu, [ps], [s])
            elif f >= F_G:
                self.act(s[:], ps[:], AF.Sigmoid, [ps], [s])
            else:
                self.ecopy(s[:], ps[:], [ps], [s], eng="dve")
            P.dma(PROJa[f, :, PAD + b * 512: PAD + (b + 1) * 512], s[:], reads=[s], writes=[self.PROJ_T[f][b]], q="pq")
        ps = self.psum.next()
        wsmb = self.wsmb
        for kt in range(KT):
            self.mm(ps[0:16, :], wsmb[:, kt, :], hb[:, kt, :], [wsmb, hb], ps, start=(kt == 0), stop=(kt == KT - 1))
        s = self.stg.next()
        self.ecopy(s[0:16, :], ps[0:16, :], [ps], [s], eng="dve")
        P.dma(self.a(self.PSM)[:, b * 512:(b + 1) * 512], s[0:16, :], reads=[s], writes=[self.PSM_T[b]], q="pq")

    def mem_prep(self, l):
        P = self.P
        sl = self.slots
        mx = sl[0:16]
        gsb, hm = self.gsb, self.big
        memv = self.a(self.memT).rearrange("(kt p) s -> p kt s", p=128)
        P.dma(self.slotbuf[:, 0:16, 0:NMEM], memv, reads=[self.WT_], writes=mx)
        r = self.rstd_from(lambda kt: mx[kt][:, 0:NMEM], mx, KT, 1.0 / D, [sl[16], sl[17]], n=NMEM)
        for kt in range(KT):
            self.stt(hm[:, kt, 0:NMEM], mx[kt][:, 0:NMEM], gsb[:, 4, kt:kt + 1], r[:, 0:NMEM], ALU.mult, ALU.mult, [mx[kt], gsb, r], [hm])
        w_kv = self.a(self.w_kv)
        kmem, vmem = self.kmem, self.vmem
        for f in range(16):
            wb = self.load_w(w_kv[l, f])
            if f < 8:
                ps = self.psum.next()
                for kt in range(KT):
                    self.mm(ps[:, 0:NMEM], wb[:, kt, :], hm[:, kt, 0:NMEM], [wb, hm], ps, start=(kt == 0), stop=(kt == KT - 1))
                self.ecopy(kmem[:, f, :], ps[:, 0:NMEM], [ps], [kmem])
            else:
                for mt in range(2):
                    ps = self.psum.next()
                    for kt in range(KT):
                        self.mm(ps[:, 0:128], hm[:, kt, mt * 128:(mt + 1) * 128], wb[:, kt, :], [wb, hm], ps, start=(kt == 0), stop=(kt == KT - 1))
                    self.ecopy(vmem[:, mt, (f - 8) * 128:(f - 7) * 128], ps[:, 0:128], [ps], [vmem])

    def gated_out(self, oT, X1, Z1, zrow, gcol, orow, b):
        P = self.P
        t0 = b * 512
        r = self.rstd_from(lambda kt: oT[:], [oT], 1, 1.0 / 128, [X1])
        P.dma(Z1[:], self.a(self.PROJ)[zrow, :, PAD + t0:PAD + t0 + 512], reads=[self.PROJ_T[zrow][b]], writes=[Z1])
        self.stt(oT[:], oT[:], self.gn[:, gcol:gcol + 1], r[:], ALU.mult, ALU.mult, [oT, self.gn, r], [oT])
        self.tt(oT[:], oT[:], Z1[:], ALU.mult, [oT, Z1], [oT])
        P.dma(self.a(self.OD)[orow, :, t0:t0 + 512], oT[:], reads=[oT], writes=[self.OD_T[orow][b]], q="pq")

    def mixer_A(self, l, b, h, smt):
        P = self.P
        t0 = b * 512
        sl = self.slots
        qc, kc, vc, X1, B1, L1, G1, EG, RK, QD, C1, NBU, E1, OT, Z1 = sl[0:15]
        import os
        self.ka = float(os.environ.get('KA', '99'))
        ones, ident, cw = self.ones, self.ident, self.cwsb
        PROJa = self.a(self.PROJ)
        for f0, c in ((F_AQ, qc), (F_AK, kc), (F_AV, vc)):
            f = f0 + h
            raw = self.raw.next()
            rd = [self.PROJ_T[f][b]] + ([self.PROJ_T[f][b - 1]] if b > 0 else [])
            P.dma(raw[:, 0:515], PROJa[f, :, t0:t0 + 515], reads=rd, writes=[raw])
            self.ts(c[:], raw[:, 0:512], cw[:, f, 0:1], ALU.mult, [raw, cw], [c])
            for k in range(1, 4):
                self.stt(c[:], raw[:, k:k + 512], cw[:, f, k:k + 1], c[:], ALU.mult, ALU.add, [raw, cw, c], [c])
            self.act(c[:], c[:], AF.Silu, [c], [c])
        if self.ka < 2:
            return
        for c, mul in ((qc, RS), (kc, 1.0)):
            r = self.rstd_from(lambda kt, c=c: c[:], [c], 1, 1.0, [X1])
            self.stt(c[:], c[:], mul, r[:], ALU.mult, ALU.mult, [c, r], [c])
        if self.ka < 3:
            return
        qn, kn = qc, kc
        hv, negA = self.hv, self.negA
        sm1 = self.tp.next()
        self.ts(sm1[0:16, :], smt[0:16, :], ident[0:16, h:h + 1], ALU.mult, [smt, ident], [sm1])
        ps = self.psum.next()
        self.mm(ps[:], ones[0:16, :], sm1[0:16, :], [ones, sm1], ps)
        self.act(B1[:], ps[:], AF.Sigmoid, [ps], [B1])
        sm2 = self.tp.next()
        self.ts(sm2[0:16, :], smt[0:16, :], ident[0:16, 8 + h:9 + h], ALU.mult, [smt, ident], [sm2])
        ps = self.psum.next()
        self.mm(ps[:], ones[0:16, :], sm2[0:16, :], [ones, sm2], ps)
        self.act(L1[:], ps[:], AF.Exp, [ps, hv], [L1], bias=hv[:, 8 + h:9 + h])
        self.act(L1[:], L1[:], AF.Ln, [L1, self.oneb], [L1], bias=self.oneb[:])
        self.ts(L1[:], L1[:], negA[:, h:h + 1], ALU.mult, [L1, negA], [L1])
        if self.ka < 4:
            return
        resetm = self.resetm
        P.op("dve", lambda e: e.tensor_tensor_scan(out=G1[:], data0=resetm[:], data1=L1[:], initial=0.0, op0=ALU.mult, op1=ALU.add), [resetm, L1], [G1])
        self.act(EG[:], G1[:], AF.Exp, [G1], [EG])
        egl = self.small.next()
        self.act(egl[:, 0:8], G1[:, 63::64], AF.Exp, [G1], [egl])
        self.tt(v3(RK[:], 64), bcast_mid(G1[:, 63::64], 64), v3(G1[:], 64), ALU.subtract, [G1], [RK])
        self.act(RK[:], RK[:], AF.Exp, [RK], [RK])
        if self.ka < 5:
            return
        self.tt(QD[:], qn[:], EG[:], ALU.mult, [qn, EG], [QD])
        self.tt(C1[:], B1[:], EG[:], ALU.mult, [B1, EG], [C1])
        gamcol = self.diag_cols(G1, X1)
        ngamcol = self.small.next()
        self.ts(ngamcol[:, 0:4], gamcol[:, 0:4], -1.0, ALU.mult, [gamcol], [ngamcol])
        betacol = self.diag_cols(B1, X1)
        c1col = self.diag_cols(C1, X1)
        rkcol = self.diag_cols(RK, X1)
        nup_s, nlow_s, up_i = self.nup_s, self.nlow_s, self.up_i
        self.tt(v3(NBU[:], 128), v3(B1[:], 128), bcast_outer(nup_s[:, :], 4), ALU.mult, [B1, nup_s], [NBU])
        if self.ka < 6:
            return
        for pr in range(4):
            s_ = slice(pr * 128, (pr + 1) * 128)
            self.act(E1[:, s_], G1[:, s_], AF.Exp, [G1, gamcol], [E1], scale=-1.0, bias=gamcol[:, pr:pr + 1])
            self.act(X1[:, s_], G1[:, s_], AF.Exp, [G1, ngamcol], [X1], scale=1.0, bias=ngamcol[:, pr:pr + 1])
        self.tt(E1[:], E1[:], X1[:], ALU.min, [E1, X1], [E1])
        if self.ka < 7:
            return
        if self.ka < 7.1:
            return
        S_ = self.SA[h]
        m = self.m128
        for pr in range(4):
            s_ = slice(pr * 128, (pr + 1) * 128)
            ps = self.psum.next()
            self.mm(ps[:, 0:128], kn[:, s_], kn[:, s_], [kn], ps)
            M1 = m.next()
            self.tt(M1[:], ps[:, 0:128], E1[:, s_], ALU.mult, [ps, E1], [M1])
            Pm, Qm, Tt = m.next(), m.next(), m.next()
            self.stt(Pm[:], M1[:], betacol[:, pr:pr + 1], nlow_s[:], ALU.mult, ALU.mult, [M1, betacol, nlow_s], [Pm])
            self.tt(Qm[:], M1[:], NBU[:, s_], ALU.mult, [M1, NBU], [Qm])
            self.tt(Tt[:], Qm[:], ident[:], ALU.add, [Qm, ident], [Tt])
            if self.ka < 7.5:
                continue
            for k in range(1, 6):
                ps = self.psum.next()
                self.mm(ps[:, 0:128], Qm[:], Pm[:], [Qm, Pm], ps)
                if k < 5:
                    psq = self.psum.next()
                    self.mm(psq[:, 0:128], Pm[:], Qm[:], [Qm, Pm], psq)
                Pn = m.next()
                self.ecopy(Pn[:], ps[:, 0:128], [ps], [Pn], eng="act")
                if k < 5:
                    Qn = m.next()
                    self.ecopy(Qn[:], psq[:, 0:128], [psq], [Qn], eng="dve")
                ps2 = self.psum.next()
                self.mm(ps2[:, 0:128], Pn[:], Tt[:], [Pn, Tt], ps2)
                Tn = m.next()
                self.tt(Tn[:], ps2[:, 0:128], Tt[:], ALU.add, [ps2, Tt], [Tn])
                Pm, Tt = Pn, Tn
                if k < 5:
                    Qm = Qn
            if self.ka < 8:
                continue
            ps = self.psum.next()
            self.tr(ps[:, 0:128], kn[:, s_], [kn], ps)
            kbg, kdt = m.next(), m.next()
            self.ts(kbg[:], ps[:, 0:128], c1col[:, pr:pr + 1], ALU.mult, [ps, c1col], [kbg])
            self.ts(kdt[:], ps[:, 0:128], rkcol[:, pr:pr + 1], ALU.mult, [ps, rkcol], [kdt])
            ps = self.psum.next()
            self.tr(ps[:, 0:128], vc[:, s_], [vc], ps)
            vb = m.next()
            self.ts(vb[:], ps[:, 0:128], betacol[:, pr:pr + 1], ALU.mult, [ps, betacol], [vb])
            ps = self.psum.next()
            self.mm(ps[:, 0:128], Tt[:], vb[:], [Tt, vb], ps)
            U = m.next()
            self.ecopy(U[:], ps[:, 0:128], [ps], [U], eng="act")
            ps = self.psum.next()
            self.mm(ps[:, 0:128], kbg[:], Tt[:], [kbg, Tt], ps)
            WTt = m.next()
            self.ecopy(WTt[:], ps[:, 0:128], [ps], [WTt], eng="dve")
            ps = self.psum.next()
            self.mm(ps[:, 0:128], kn[:, s_], qn[:, s_], [kn, qn], ps)
            AT = m.next()
            self.tt(AT[:], ps[:, 0:128], E1[:, s_], ALU.mult, [ps, E1], [AT])
            self.tt(AT[:], AT[:], up_i[:], ALU.mult, [AT, up_i], [AT])
            if self.ka < 9:
                continue
            un = m.next()
            for c in range(2):
                r0 = 64 * c
                ch = pr * 2 + c
                cs = slice(ch * 64, ch * 64 + 64)
                ps = self.psum.next()
                self.mm(ps[:, 0:128], WTt[:], S_[:], [WTt, S_], ps)
                self.tt(un[r0:r0 + 64, :], U[r0:r0 + 64, :], ps[r0:r0 + 64, 0:128], ALU.subtract, [U, ps], [un])
                pso = self.psum.next()
                self.mm(pso[:, 0:64], S_[:], QD[:, cs], [S_, QD], pso, start=True, stop=False)
                self.mm(pso[:, 0:64], un[r0:r0 + 64, :], AT[r0:r0 + 64, r0:r0 + 64], [un, AT], pso, start=False, stop=True)
                self.ecopy(OT[:, cs], pso[:, 0:64], [pso], [OT], eng="act")
                pss = self.psum.next()
                self.mm(pss[:, 0:128], kdt[r0:r0 + 64, :], un[r0:r0 + 64, :], [kdt, un], pss)
                self.stt(S_[:], S_[:], egl[:, ch:ch + 1], pss[:, 0:128], ALU.mult, ALU.add, [S_, egl, pss], [S_])
        if self.ka < 10:
            return
        self.gated_out(OT, X1, Z1, F_AZ + h, 0, h, b)

    def mixer_B(self, l, b, h):
        P = self.P
        t0 = b * 512
        sl = self.slots
        q, f_, vi, Z1, X1, B_, QT, KTl, QE, KE, OT, KK = sl[16:28]
        PROJa = self.a(self.PROJ)
        for row, dst in ((F_BQ + h, q), (F_BF + h, f_), (F_BI + h, vi)):
            P.dma(dst[:], PROJa[row, :, PAD + t0:PAD + t0 + 512], reads=[self.PROJ_T[row][b]], writes=[dst])
        lball, omlb, resetm = self.lball, self.omlb, self.resetm
        self.act(f_[:], f_[:], AF.Sigmoid, [f_], [f_])
        self.ts(f_[:], f_[:], omlb[:, l, h:h + 1], ALU.mult, [f_, omlb, lball], [f_], s2=lball[:, l, h:h + 1], op1=ALU.add)
        self.ts(KK[:], f_[:], -1.0, ALU.mult, [f_], [KK], s2=1.0, op1=ALU.add)
        self.act(f_[:], f_[:], AF.Ln, [f_], [f_])
        P.op("dve", lambda e: e.tensor_tensor_scan(out=B_[:], data0=resetm[:], data1=f_[:], initial=0.0, op0=ALU.mult, op1=ALU.add), [resetm, f_], [B_])
        ebl = self.small.next()
        self.act(ebl[:, 0:8], B_[:, 63::64], AF.Exp, [B_], [ebl])
        self.tt(v3(X1[:], 64), v3(B_[:], 64), bcast_mid(B_[:, 31::64], 64), ALU.subtract, [B_], [X1])
        self.act(QT[:], X1[:], AF.Exp, [X1], [QT])
        self.stt(QT[:], q[:], RS, QT[:], ALU.mult, ALU.mult, [q, QT], [QT])
        self.act(KTl[:], X1[:], AF.Exp, [X1], [KTl], scale=-1.0)
        self.tt(KTl[:], KTl[:], KK[:], ALU.mult, [KTl, KK], [KTl])
        self.act(QE[:], B_[:], AF.Exp, [B_], [QE])
        self.stt(QE[:], q[:], RS, QE[:], ALU.mult, ALU.mult, [q, QE], [QE])
        self.tt(v3(KE[:], 64), bcast_mid(B_[:, 63::64], 64), v3(B_[:], 64), ALU.subtract, [B_], [KE])
        self.act(KE[:], KE[:], AF.Exp, [KE], [KE])
        self.tt(KE[:], KE[:], KK[:], ALU.mult, [KE, KK], [KE])
        S_ = self.SB[h]
        m = self.m128
        up_i = self.up_i
        for pr in range(4):
            s_ = slice(pr * 128, (pr + 1) * 128)
            ps = self.psum.next()
            self.tr(ps[:, 0:128], vi[:, s_], [vi], ps)
            vt = m.next()
            self.ecopy(vt[:], ps[:, 0:128], [ps], [vt], eng="act")
            ps = self.psum.next()
            self.tr(ps[:, 0:128], KE[:, s_], [KE], ps)
            ket = m.next()
            self.ecopy(ket[:], ps[:, 0:128], [ps], [ket], eng="dve")
            ps = self.psum.next()
            self.mm(ps[:, 0:128], KTl[:, s_], QT[:, s_], [KTl, QT], ps)
            AT = m.next()
            self.ts(AT[:], ps[:, 0:128], 1e30, ALU.min, [ps], [AT], s2=-1e30, op1=ALU.max)
            self.tt(AT[:], AT[:], up_i[:], ALU.mult, [AT, up_i], [AT])
            for c in range(2):
                r0 = 64 * c
                ch = pr * 2 + c
                cs = slice(ch * 64, ch * 64 + 64)
                pso = self.psum.next()
                self.mm(pso[:, 0:64], S_[:], QE[:, cs], [S_, QE], pso, start=True, stop=False)
                self.mm(pso[:, 0:64], vt[r0:r0 + 64, :], AT[r0:r0 + 64, r0:r0 + 64], [vt, AT], pso, start=False, stop=True)
                self.ecopy(OT[:, cs], pso[:, 0:64], [pso], [OT], eng="act")
                pss = self.psum.next()
                self.mm(pss[:, 0:128], ket[r0:r0 + 64, :], vt[r0:r0 + 64, :], [ket, vt], pss)
                self.stt(S_[:], S_[:], ebl[:, ch:ch + 1], pss[:, 0:128], ALU.mult, ALU.add, [S_, ebl, pss], [S_])
        self.gated_out(OT, X1, Z1, F_BZ + h, 1, 8 + h, b)

    def mixer_C(self, l, b):
        P = self.P
        t0 = b * 512
        sl = self.slots
        PROJa = self.a(self.PROJ)
        kmem, vmem, ident = self.kmem, self.vmem, self.ident
        m = self.m128
        SC = 256 ** -0.5
        for hc in range(4):
            cqs = []
            for j in range(2):
                row = F_CQ + 2 * hc + j
                st = sl[28 + j]
                P.dma(st[:], PROJa[row, :, PAD + t0:PAD + t0 + 512], reads=[self.PROJ_T[row][b]], writes=[st])
                cb = self.cqb[j]
                self.ecopy(cb[:], st[:], [st], [cb], eng="dve")
                cqs.append(cb)
            o0, o1 = sl[30], sl[31]
            for tq in range(4):
                s_ = slice(tq * 128, (tq + 1) * 128)
                ps = self.psum.next()
                for j in range(2):
                    self.mm(ps[:, 0:NMEM], cqs[j][:, s_], kmem[:, 2 * hc + j, :], [cqs[j], kmem], ps, start=(j == 0), stop=(j == 1))
                mx = self.small.next()
                P.op("dve", lambda e, mx=mx, ps=ps: e.tensor_reduce(out=mx[:, 0:1], in_=ps[:, 0:NMEM], axis=AX.X, op=ALU.max), [ps], [mx])
                self.ts(mx[:, 1:2], mx[:, 0:1], -SC, ALU.mult, [mx], [mx])
                pe_ = self.tp.next()
                self.act(pe_[:, 0:NMEM], ps[:, 0:NMEM], AF.Exp, [ps, mx], [pe_, mx], scale=SC, bias=mx[:, 1:2], accum_out=mx[:, 2:3])
                self.recip(mx[:, 3:4], mx[:, 2:3], [mx], [mx])
                self.ts(pe_[:, 0:NMEM], pe_[:, 0:NMEM], mx[:, 3:4], ALU.mult, [pe_, mx], [pe_])
                pTs = []
                for mt in range(2):
                    ps2 = self.psum.next()
                    self.tr(ps2[:, 0:128], pe_[:, mt * 128:(mt + 1) * 128], [pe_], ps2)
                    pT = self.pTb[mt]
                    self.ecopy(pT[:], ps2[:, 0:128], [ps2], [pT])
                    pTs.append(pT)
                for j, o in ((0, o0), (1, o1)):
                    ps3 = self.psum.next()
                    for mt in range(2):
                        self.mm(ps3[:, 0:128], vmem[:, mt, (2 * hc + j) * 128:(2 * hc + j + 1) * 128], pTs[mt][:], [vmem, pTs[mt]], ps3, start=(mt == 0), stop=(mt == 1))
                    self.ecopy(o[:, s_], ps3[:, 0:128], [ps3], [o])
            for j, o in ((0, o0), (1, o1)):
                row = 16 + 2 * hc + j
                P.dma(self.a(self.OD)[row, :, t0:t0 + 512], o[:], reads=[o], writes=[self.OD_T[row][b]], q="pq")

    def phase3(self, l, b, last):
        P = self.P
        t0 = b * 512
        sl = self.slots
        xts, ys = sl[0:16], sl[16:32]
        hb, big, gsb = self.hb, self.big, self.gsb
        XTv = self.a(self.XT).rearrange("(kt p) s -> p kt s", p=128)
        ODa, PROJa = self.a(self.OD), self.a(self.PROJ)
        w_br, w_out, w1, w2 = self.a(self.w_br), self.a(self.w_out), self.a(self.w1), self.a(self.w2)
        for br in range(3):
            stgs = []
            for kt in range(8):
                row = br * 8 + kt
                st = ys[kt]
                P.dma(st[:], ODa[row, :, t0:t0 + 512], reads=[self.OD_T[row][b]], writes=[st])
                self.ecopy(big[:, kt, :], st[:], [st], [big], eng=("dve" if kt % 2 else "act"))
            for f in range(16):
                wb = self.load_w(w_br[l, br, f])
                ps = self.psum.next()
                for kt in range(8):
                    self.mm(ps[:], wb[:, kt, :], big[:, kt, :], [wb, big], ps, start=(kt == 0), stop=(kt == 7))
                g = self.tp.next()
                row = F_G + br * 16 + f
                P.dma(g[:], PROJa[row, :, PAD + t0:PAD + t0 + 512], reads=[self.PROJ_T[row][b]], writes=[g])
                acc = xts[f]
                if br == 0:
                    self.tt(acc[:], ps[:], g[:], ALU.mult, [ps, g], [acc])
                else:
                    self.tt(g[:], ps[:], g[:], ALU.mult, [ps, g], [g])
                    if br == 1:
                        self.tt(acc[:], acc[:], g[:], ALU.add, [acc, g], [acc])
                    else:
                        self.tt(hb[:, f, :], acc[:], g[:], ALU.add, [acc, g], [hb])
        for f in range(16):
            wb = self.load_w(w_out[l, f])
            ps = self.psum.next()
            for kt in range(KT):
                self.mm(ps[:], wb[:, kt, :], hb[:, kt, :], [wb, hb], ps, start=(kt == 0), stop=(kt == KT - 1))
            self.ecopy(ys[f][:], ps[:], [ps], [ys[f]])
        P.dma(self.slotbuf[:, 0:16, :], XTv[:, :, t0:t0 + 512], reads=[self.XT_T[b]], writes=xts)
        t1, t2 = self.tp.next(), self.tp.next()
        r = self.rstd_from(lambda kt: ys[kt][:], ys, KT, 1.0 / D, [t1, t2])
        for kt in range(KT):
            self.stt(ys[kt][:], ys[kt][:], gsb[:, 1, kt:kt + 1], r[:], ALU.mult, ALU.mult, [ys[kt], gsb, r], [ys[kt]])
            self.tt(xts[kt][:], xts[kt][:], ys[kt][:], ALU.add, [xts[kt], ys[kt]], [xts[kt]])
        t1, t2 = self.tp.next(), self.tp.next()
        r = self.rstd_from(lambda kt: xts[kt][:], xts, KT, 1.0 / D, [t1, t2])
        for kt in range(KT):
            self.stt(hb[:, kt, :], xts[kt][:], gsb[:, 2, kt:kt + 1], r[:], ALU.mult, ALU.mult, [xts[kt], gsb, r], [hb])
        for qtr in range(4):
            for fl in range(16):
                f = qtr * 16 + fl
                wb = self.load_w(w1[l, f])
                ps = self.psum.next()
                for kt in range(KT):
                    self.mm(ps[:], wb[:, kt, :], hb[:, kt, :], [wb, hb], ps, start=(kt == 0), stop=(kt == KT - 1))
                rl = self.tp.next()
                self.act(rl[:], ps[:], AF.Relu, [ps], [rl])
                self.tt(big[:, fl, :], rl[:], rl[:], ALU.mult, [rl], [big])
            for f in range(16):
                wb = self.load_w(w2[l, f, :, qtr * 16:(qtr + 1) * 16, :])
                ps = self.psum.next()
                for kt in range(16):
                    self.mm(ps[:], wb[:, kt, :], big[:, kt, :], [wb, big], ps, start=(kt == 0), stop=(kt == 15))
                if qtr == 0:
                    self.ecopy(ys[f][:], ps[:], [ps], [ys[f]])
                else:
                    self.tt(ys[f][:], ys[f][:], ps[:], ALU.add, [ys[f], ps], [ys[f]])
        t1, t2 = self.tp.next(), self.tp.next()
        r = self.rstd_from(lambda kt: ys[kt][:], ys, KT, 1.0 / D, [t1, t2])
        for kt in range(KT):
            self.stt(ys[kt][:], ys[kt][:], gsb[:, 3, kt:kt + 1], r[:], ALU.mult, ALU.mult, [ys[kt], gsb, r], [ys[kt]])
            self.tt(xts[kt][:], xts[kt][:], ys[kt][:], ALU.add, [xts[kt], ys[kt]], [xts[kt]])
        if last:
            ov = self.a(self.outT).rearrange("(kt p) s -> p kt s", p=128)
            P.dma(ov[:, :, t0:t0 + 512], self.slotbuf[:, 0:16, :], reads=xts, writes=[self.OUT_T], q="pq")
        else:
            P.dma(XTv[:, :, t0:t0 + 512], self.slotbuf[:, 0:16, :], reads=xts, writes=[self.XT_T[b]], q="pq")

    def build(self, stop_after=None):
        P, L, NB = self.P, self.L, self.NB
        self.cqb = [P.sb([128, 512], BF16, f"cqb{j}") for j in range(2)]
        self.pTb = [P.sb([128, 128], BF16, f"pTb{j}") for j in range(2)]
        self.smtb = Rot([P.sb([16, 512], F32, f"smt{j}") for j in range(2)])
        import os
        stop = os.environ.get("KSTOP", "")
        order = ["prologue", "p1", "mem", "A", "B", "C", "p3"]
        lim = order.index(stop) if stop in order else 99
        self.prologue()
        for l in range(L):
            self.layer_params(l)
            if lim >= 1:
                for b in range(NB):
                    self.phase1(l, b)
            if lim >= 2:
                self.mem_prep(l)
            for b in range(NB):
                smt = self.smtb.next()
                P.dma(smt[0:16, :], self.a(self.PSM)[:, b * 512:(b + 1) * 512], reads=[self.PSM_T[b]], writes=[smt])
                for h in range(8):
                    if lim >= 3:
                        self.mixer_A(l, b, h, smt)
                    if lim >= 4:
                        self.mixer_B(l, b, h)
                if lim >= 5:
                    self.mixer_C(l, b)
            if lim >= 6:
                for b in range(NB):
                    self.phase3(l, b, last=(l == L - 1))
        st = P.emit()
        P.close()
        return st

NCORES = 4
_CACHE = {}


def _tile_w(w, kt):
    K, F = w.shape
    return np.ascontiguousarray(w.reshape(kt, 128, F // 128, 128).transpose(2, 1, 0, 3))


def _fm_vec(g):
    return np.ascontiguousarray(g.reshape(-1, 128).T)


def prep_weights(inp, L):
    f32 = np.float32
    w_in = np.asarray(inp["w_in"], f32)
    cols_main = np.r_[0:4096, 4112:15376]
    out = {}
    out["w_in"] = np.stack([_tile_w(w_in[l][:, cols_main], 16) for l in range(L)])
    wsm = w_in[:, :, 4096:4112]
    out["w_sm"] = np.ascontiguousarray(wsm.reshape(L, 16, 128, 16).transpose(0, 2, 1, 3))
    out["w_kv"] = np.stack([_tile_w(np.asarray(inp["w_kv_mem"][l], f32), 16) for l in range(L)])
    out["w_br"] = np.stack([np.stack([_tile_w(np.asarray(inp[n][l], f32), 8) for n in ("w_br_a", "w_br_b", "w_br_c")]) for l in range(L)])
    out["w_out"] = np.stack([_tile_w(np.asarray(inp["w_out"][l], f32), 16) for l in range(L)])
    out["w1"] = np.stack([_tile_w(np.asarray(inp["w_mlp_in"][l], f32), 16) for l in range(L)])
    out["w2"] = np.stack([_tile_w(np.asarray(inp["w_mlp_out"][l], f32), 64) for l in range(L)])
    out["gains"] = np.stack([np.stack([_fm_vec(np.asarray(inp[n][l], f32)) for n in ("g_pre_mix", "g_post_mix", "g_pre_mlp", "g_post_mlp", "g_mem")], axis=1) for l in range(L)])
    cw = np.asarray(inp["conv_a"], f32)
    out["convw"] = np.ascontiguousarray(cw.reshape(L, 4, 24, 128).transpose(0, 3, 2, 1))
    hv = np.concatenate([np.asarray(inp["a_log"], f32), np.asarray(inp["dt_bias"], f32)], axis=1)
    out["hvec"] = np.ascontiguousarray(np.broadcast_to(hv[:, None, :], (L, 128, 16)))
    out["gnv"] = np.ascontiguousarray(np.stack([np.asarray(inp["gn_a"], f32), np.asarray(inp["gn_b"], f32)], axis=2))
    lb = np.asarray(inp["lb_raw"], f32)
    out["lbraw"] = np.ascontiguousarray(lb.reshape(L, 8, 128).transpose(2, 1, 0))
    return out


def kernel(**inp):
    x = np.asarray(inp["x"], np.float32)
    mem = np.asarray(inp["mem"], np.float32)
    B, S, _ = x.shape
    L = inp["w_in"].shape[0]
    key = (S, L)
    if key not in _CACHE:
        k = K(S, L)
        k.build()
        _CACHE[key] = k
    k = _CACHE[key]
    w = prep_weights(inp, L)
    in_maps = []
    for c in range(B):
        m = dict(w)
        m["xT"] = np.ascontiguousarray(x[c].T)
        m["memT"] = np.ascontiguousarray(mem[c].T)
        in_maps.append(m)
    res = run_bass_kernel_spmd(k.nc, in_maps, core_ids=list(range(B)))
    out = np.stack([np.ascontiguousarray(res.results[c]["outT"].T) for c in range(B)])
    return out.astype(np.float32)
```

```python
import numpy as np
import concourse.bass as bass
import concourse.mybir as mybir
from contextlib import ExitStack

F32 = mybir.dt.float32
BF16 = mybir.dt.bfloat16
I32 = mybir.dt.int32
AF = mybir.ActivationFunctionType
ALU = mybir.AluOpType
AX = mybir.AxisListType

COMPUTE = ("pe", "dve", "act", "pool")
DMAQ = ("sp", "pq")
NDMASEM = {"sp": 24, "pq": 12}


class T:
    __slots__ = ("t", "name", "w", "r", "rd")

    def __init__(self, t, name=""):
        self.t = t
        self.name = name
        self.w = None
        self.r = {}
        self.rd = []

    def __getitem__(self, idx):
        return self.t[idx]


class Ins:
    __slots__ = ("eng", "fn", "inc", "idx", "deps", "dma_sem", "dma_val", "ticket")

    def __init__(self, eng, fn, inc):
        self.eng = eng
        self.fn = fn
        self.inc = inc
        self.deps = []
        self.dma_sem = None
        self.dma_val = 0
        self.ticket = None


class Prog:
    def __init__(self, nc):
        self.nc = nc
        self.stack = ExitStack()
        self.all = []
        self.per = {e: [] for e in COMPUTE + DMAQ}
        self.nalloc = 0

    def sb(self, shape, dt=F32, name=None):
        self.nalloc += 1
        name = name or f"sb{self.nalloc}"
        t = self.stack.enter_context(self.nc.sbuf_tensor(name, list(shape), dt))
        return T(t, name)

    def ps(self, shape, dt=F32, name=None):
        self.nalloc += 1
        name = name or f"ps{self.nalloc}"
        t = self.stack.enter_context(self.nc.psum_tensor(name, list(shape), dt))
        return T(t, name)

    def dram(self, name, shape, dt=F32, kind="Internal"):
        t = self.nc.dram_tensor(name, list(shape), dt, kind=kind)
        return T(t, name)

    def view(self, t, name=""):
        return T(t.t if isinstance(t, T) else t, name)

    def op(self, eng, fn, reads=(), writes=(), inc=True):
        ins = Ins(eng, fn, inc)
        isdma = eng in DMAQ
        deps = {}

        def add(d, kind):
            if d is None or d is ins:
                return
            if d.eng == eng and not isdma:
                if kind != "raw" or eng == "pe":
                    return
            deps[id(d)] = d

        for t in reads:
            add(t.w, "raw")
        for t in writes:
            add(t.w, "waw")
            for r in t.r.values():
                add(r, "war")
            for r in t.rd:
                add(r, "war")
        for t in reads:
            if isdma:
                t.rd.append(ins)
            else:
                t.r[eng] = ins
        for t in writes:
            t.w = ins
            t.r = {}
            t.rd = []
        ins.deps = list(deps.values())
        ins.idx = len(self.per[eng])
        self.per[eng].append(ins)
        self.all.append(ins)
        return ins

    def dma(self, out_ap, in_ap, reads=(), writes=(), q="sp", **kw):
        return self.op(q, lambda e: e.dma_start(out=out_ap, in_=in_ap, **kw), reads, writes)

    def emit(self):
        nc = self.nc
        engobj = {"pe": nc.tensor, "dve": nc.vector, "act": nc.scalar, "pool": nc.gpsimd,
                  "sp": nc.sync, "pq": nc.gpsimd}
        sems = {e: self.stack.enter_context(nc.semaphore(f"s_{e}")) for e in COMPUTE}
        dsem = {q: [self.stack.enter_context(nc.semaphore(f"d_{q}{i}")) for i in range(NDMASEM[q])]
                for q in DMAQ}
        for e in COMPUTE:
            if self.per[e]:
                self.per[e][-1].inc = True
        for e in COMPUTE:
            c = 0
            lst = self.per[e]
            for ins in lst:
                if ins.inc:
                    c += 1
                    ins.ticket = c
            nxt = None
            for ins in reversed(lst):
                if ins.inc:
                    nxt = ins.ticket
                else:
                    ins.ticket = nxt
        dcount = {q: [0] * NDMASEM[q] for q in DMAQ}
        dprev = {q: [None] * NDMASEM[q] for q in DMAQ}
        for q in DMAQ:
            for i, ins in enumerate(self.per[q]):
                k = i % NDMASEM[q]
                dcount[q][k] += 16
                ins.dma_sem = dsem[q][k]
                ins.dma_val = dcount[q][k]
                ins.ticket = (q, k)
        issue_eng = {"pe": "pe", "dve": "dve", "act": "act", "pool": "pool", "sp": "sp", "pq": "pool"}
        seen = {e: {} for e in ("pe", "dve", "act", "pool", "sp")}
        nwait = 0
        for ins in self.all:
            ie = issue_eng[ins.eng]
            eo = engobj[ins.eng]
            sn = seen[ie]
            waits = []
            for d in ins.deps:
                if d.eng in DMAQ:
                    waits.append((d.dma_sem, d.dma_val))
                else:
                    waits.append((sems[d.eng], d.ticket))
            if ins.eng in DMAQ:
                q, k = ins.ticket
                prev = ins.dma_val - 16
                if prev > 0:
                    waits.append((ins.dma_sem, prev))
            for s, v in waits:
                key = id(s)
                if sn.get(key, 0) >= v:
                    continue
                sn[key] = v
                eo.wait_ge(s, v)
                nwait += 1
            r = ins.fn(eo)
            if ins.eng in DMAQ:
                r.then_inc(ins.dma_sem, 16)
            elif ins.inc:
                r.then_inc(sems[ins.eng], 1)
        for q in DMAQ:
            for k in range(NDMASEM[q]):
                if dcount[q][k] > 0:
                    nc.sync.wait_ge(dsem[q][k], dcount[q][k])
        for e in COMPUTE:
            lst = self.per[e]
            if lst:
                nc.sync.wait_ge(sems[e], lst[-1].ticket)
        self.stats = {e: len(self.per[e]) for e in self.per}
        self.stats["waits"] = nwait
        return self.stats

    def close(self):
        self.stack.close()
from concourse.bass_utils import run_bass_kernel_spmd

D = 2048
KT = 16
NMEM = 256
EPS = 1e-6
F_AQ, F_AK, F_AV, F_AZ, F_BQ, F_BF, F_BI, F_BZ, F_CQ, F_G = 0, 8, 16, 24, 32, 40, 48, 56, 64, 72
PAD = 3
RS = 128 ** -0.5


def bcast_mid(ap2, n_inner):
    a = ap2.ap
    return bass.AP(ap2.tensor, ap2.offset, [list(a[0]), list(a[1]), [0, n_inner]])


def bcast_outer(ap2, n_outer):
    a = ap2.ap
    return bass.AP(ap2.tensor, ap2.offset, [list(a[0]), [0, n_outer], list(a[1])])


def v3(ap, inner):
    return ap.rearrange("p (a b) -> p a b", b=inner)


class Rot:
    def __init__(self, items):
        self.items = items
        self.i = 0

    def next(self):
        x = self.items[self.i % len(self.items)]
        self.i += 1
        return x


class K:
    def __init__(self, S, L):
        self.S, self.L = S, L
        self.NB = S // 512
        nc = bass.Bass("TRN2", target_bir_lowering=False)
        self.nc = nc
        P = self.P = Prog(nc)
        NB = self.NB
        ext = lambda n, s: P.dram(n, s, F32, kind="ExternalInput")
        self.xT_in = ext("xT", [D, S])
        self.memT = ext("memT", [D, NMEM])
        self.w_in = ext("w_in", [L, 120, 128, KT, 128])
        self.w_sm = ext("w_sm", [L, 128, KT, 16])
        self.w_kv = ext("w_kv", [L, 16, 128, KT, 128])
        self.w_br = ext("w_br", [L, 3, 16, 128, 8, 128])
        self.w_out = ext("w_out", [L, 16, 128, KT, 128])
        self.w1 = ext("w1", [L, 64, 128, KT, 128])
        self.w2 = ext("w2", [L, 16, 128, 64, 128])
        self.gains = ext("gains", [L, 128, 5, KT])
        self.convw = ext("convw", [L, 128, 24, 4])
        self.hvec = ext("hvec", [L, 128, 16])
        self.gnv = ext("gnv", [L, 128, 2])
        self.lbraw = ext("lbraw", [128, 8, L])
        self.outT = P.dram("outT", [D, S], F32, kind="ExternalOutput")
        self.XT = P.dram("XT", [D, S], F32)
        self.PROJ = P.dram("PROJ", [120, 128, S + PAD], F32)
        self.PSM = P.dram("PSM", [16, S], F32)
        self.OD = P.dram("OD", [24, 128, S], F32)
        self.XT_T = [T(None, f"XT{b}") for b in range(NB)]
        self.PROJ_T = [[T(None, f"PJ{f}_{b}") for b in range(NB)] for f in range(120)]
        self.PSM_T = [T(None, f"PSM{b}") for b in range(NB)]
        self.OD_T = [[T(None, f"OD{f}_{b}") for b in range(NB)] for f in range(24)]
        self.WT_ = T(None, "weights")
        self.OUT_T = T(None, "out")

        self.ones = P.sb([128, 128], F32, "ones")
        self.ident = P.sb([128, 128], F32, "ident")
        self.nlow_s = P.sb([128, 128], F32, "nlow_s")
        self.nup_s = P.sb([128, 128], F32, "nup_s")
        self.up_i = P.sb([128, 128], F32, "up_i")
        self.resetm = P.sb([128, 512], F32, "resetm")
        self.zero3 = P.sb([128, 4], F32, "zero3")
        self.epsb = P.sb([128, 1], F32, "epsb")
        self.oneb = P.sb([128, 1], F32, "oneb")
        ones, ident, nlow_s, nup_s, up_i, resetm, zero3 = self.ones, self.ident, self.nlow_s, self.nup_s, self.up_i, self.resetm, self.zero3
        P.op("pool", lambda e: e.memset(ones[:], 1.0), [], [ones])
        P.op("pool", lambda e: e.memset(zero3[:], 0.0), [], [zero3])
        P.op("pool", lambda e: e.memset(self.epsb[:], EPS), [], [self.epsb])
        P.op("pool", lambda e: e.memset(self.oneb[:], 1.0), [], [self.oneb])
        P.op("pool", lambda e: e.memset(resetm[:], 1.0), [], [resetm])
        P.op("pool", lambda e: e.memset(resetm[:, ::64], 0.0), [resetm], [resetm])
        P.op("pool", lambda e: e.affine_select(out=ident[:], in_=ones[:], pattern=[[-1, 128]], compare_op=ALU.is_equal, fill=0.0, base=0, channel_multiplier=1), [ones], [ident])
        P.op("pool", lambda e: e.memset(nlow_s[:], -1.0), [], [nlow_s])
        P.op("pool", lambda e: e.memset(nup_s[:], -1.0), [], [nup_s])
        P.op("pool", lambda e: e.affine_select(out=nlow_s[:], in_=nlow_s[:], pattern=[[-1, 128]], compare_op=ALU.is_ge, fill=0.0, base=-1, channel_multiplier=1), [nlow_s], [nlow_s])
        P.op("pool", lambda e: e.memset(nlow_s[64:128, 0:64], 0.0), [nlow_s], [nlow_s])
        P.op("pool", lambda e: e.affine_select(out=nup_s[:], in_=nup_s[:], pattern=[[1, 128]], compare_op=ALU.is_ge, fill=0.0, base=-1, channel_multiplier=-1), [nup_s], [nup_s])
        P.op("pool", lambda e: e.memset(nup_s[0:64, 64:128], 0.0), [nup_s], [nup_s])
        P.op("pool", lambda e: e.affine_select(out=up_i[:], in_=ones[:], pattern=[[1, 128]], compare_op=ALU.is_ge, fill=0.0, base=0, channel_multiplier=-1), [ones], [up_i])
        P.op("pool", lambda e: e.memset(up_i[0:64, 64:128], 0.0), [up_i], [up_i])

        self.psum = Rot([P.ps([128, 512], F32, f"psb{i}") for i in range(8)])
        self.wstage = Rot([P.sb([128, 16, 128], F32, f"wst{i}") for i in range(2)])
        self.wbf = Rot([P.sb([128, 16, 128], BF16, f"wbf{i}") for i in range(4)])
        self.slotbuf = P.sb([128, 32, 512], F32, "slots")
        self.slots = [T(self.slotbuf[:, i, :], f"slot{i}") for i in range(32)]
        self.hb = P.sb([128, KT, 512], BF16, "hb")
        self.big = P.sb([128, KT, 512], BF16, "big")
        self.stg = Rot([P.sb([128, 512], F32, f"stg{i}") for i in range(3)])
        self.tp = Rot([P.sb([128, 512], F32, f"tp{i}") for i in range(6)])
        self.m128 = Rot([P.sb([128, 128], F32, f"m{i}") for i in range(28)])
        self.small = Rot([P.sb([128, 8], F32, f"sm{i}") for i in range(32)])
        self.raw = Rot([P.sb([128, 516], F32, f"raw{i}") for i in range(3)])
        self.kmem = P.sb([128, 8, NMEM], BF16, "kmem")
        self.vmem = P.sb([128, 2, 1024], BF16, "vmem")
        self.gsb = P.sb([128, 5, KT], F32, "gsb")
        self.cwsb = P.sb([128, 24, 4], F32, "cwsb")
        self.hv = P.sb([128, 16], F32, "hv")
        self.negA = P.sb([128, 8], F32, "negA")
        self.gn = P.sb([128, 2], F32, "gn")
        self.lbr = P.sb([128, 8, L], F32, "lbr")
        self.lbe = P.sb([128, 8, L], F32, "lbe")
        self.lbs = P.sb([128, 8], F32, "lbs")
        self.lball = P.sb([128, L, 8], F32, "lball")
        self.omlb = P.sb([128, L, 8], F32, "omlb")
        self.wsmst = P.sb([128, KT, 16], F32, "wsmst")
        self.wsmb = P.sb([128, KT, 16], BF16, "wsmb")
        self.SA = [P.sb([128, 128], F32, f"SA{h}") for h in range(8)]
        self.SB = [P.sb([128, 128], F32, f"SB{h}") for h in range(8)]
        self.evq = 0

    def a(self, t):
        return t.t.ap()

    def mm(self, ps_ap, lhsT, rhs, reads, ps_t, start=True, stop=True, inc=None):
        return self.P.op("pe", lambda e: e.matmul(ps_ap, lhsT, rhs, start=start, stop=stop), reads, [ps_t], inc=(stop if inc is None else inc))

    def tr(self, ps_ap, in_ap, reads, ps_t):
        ident = self.ident
        return self.P.op("pe", lambda e: e.transpose(ps_ap, in_ap, ident[:]), reads + [ident], [ps_t], inc=True)

    def act(self, out, in_, func, reads, writes, **kw):
        return self.P.op("act", lambda e: e.activation(out=out, in_=in_, func=func, **kw), reads, writes)

    def ecopy(self, out, in_, reads, writes, eng=None):
        self.evq += 1
        if eng is None:
            eng = "act" if self.evq % 2 else "dve"
        if eng == "act":
            return self.P.op("act", lambda e: e.copy(out=out, in_=in_), reads, writes)
        return self.P.op("dve", lambda e: e.tensor_copy(out=out, in_=in_), reads, writes)

    def tt(self, out, in0, in1, op, reads, writes):
        return self.P.op("dve", lambda e: e.tensor_tensor(out=out, in0=in0, in1=in1, op=op), reads, writes)

    def ts(self, out, in0, s1, op0, reads, writes, s2=None, op1=None):
        if op1 is None:
            return self.P.op("dve", lambda e: e.tensor_scalar(out=out, in0=in0, scalar1=s1, scalar2=None, op0=op0), reads, writes)
        return self.P.op("dve", lambda e: e.tensor_scalar(out=out, in0=in0, scalar1=s1, scalar2=s2, op0=op0, op1=op1), reads, writes)

    def stt(self, out, in0, scalar, in1, op0, op1, reads, writes):
        return self.P.op("dve", lambda e: e.scalar_tensor_tensor(out=out, in0=in0, scalar=scalar, in1=in1, op0=op0, op1=op1), reads, writes)

    def recip(self, out, in_, reads, writes):
        return self.P.op("dve", lambda e: e.reciprocal(out=out, in_=in_), reads, writes)

    def rstd_from(self, src_fn, src_reads, nkt, scale, sqs, n=512):
        ps = self.psum.next()
        for kt in range(nkt):
            sq = sqs[kt % 2] if nkt > 1 else sqs[0]
            self.act(sq[:, 0:n], src_fn(kt), AF.Square, src_reads, [sq])
            self.mm(ps[:, 0:n], self.ones[:], sq[:, 0:n], [self.ones, sq], ps, start=(kt == 0), stop=(kt == nkt - 1), inc=True)
        r = sqs[0]
        self.act(r[:, 0:n], ps[:, 0:n], AF.Sqrt, [ps, self.epsb], [r], scale=scale, bias=self.epsb[:])
        self.recip(r[:, 0:n], r[:, 0:n], [r], [r])
        return r

    def load_w(self, w_ap):
        k = w_ap.shape[1]
        st = self.wstage.next()
        self.P.dma(st[:, 0:k, :], w_ap, reads=[self.WT_], writes=[st])
        wb = self.wbf.next()
        self.ecopy(wb[:, 0:k, :], st[:, 0:k, :], [st], [wb])
        return wb

    def diag_cols(self, src, scratch):
        ident = self.ident
        self.tt(v3(scratch[:], 128), v3(src[:], 128), bcast_outer(ident[:, :], 4), ALU.mult, [src, ident], [scratch])
        c = self.small.next()
        self.P.op("dve", lambda e: e.tensor_reduce(out=c[:, 0:4], in_=v3(scratch[:], 128), axis=AX.X, op=ALU.add), [scratch], [c])
        return c

    def prologue(self):
        P, L, NB = self.P, self.L, self.NB
        PROJa, XTa, xTa = self.a(self.PROJ), self.a(self.XT), self.a(self.xT_in)
        for f in range(24):
            P.dma(PROJa[f, :, 0:PAD], self.zero3[:, 0:PAD], reads=[self.zero3], writes=[self.PROJ_T[f][0]], q="pq")
        for b in range(NB):
            P.dma(XTa[:, b * 512:(b + 1) * 512], xTa[:, b * 512:(b + 1) * 512], reads=[self.WT_], writes=[self.XT_T[b]], q="pq")
        lbr, lbe, lbs, lball, omlb = self.lbr, self.lbe, self.lbs, self.lball, self.omlb
        P.dma(lbr[:], self.a(self.lbraw), reads=[self.WT_], writes=[lbr])
        self.act(lbe[:], lbr[:], AF.Exp, [lbr], [lbe])
        P.op("dve", lambda e: e.tensor_reduce(out=lbs[:], in_=lbe[:], axis=AX.X, op=ALU.add), [lbe], [lbs])
        self.recip(lbs[:], lbs[:], [lbs], [lbs])
        P.op("dve", lambda e: e.memset(lball[:, 0, :], 0.0), [], [lball])
        for l in range(1, L):
            tl = self.small.next()
            self.tt(tl[:], lbe[:, :, l], lbs[:], ALU.mult, [lbe, lbs], [tl])
            self.tt(lball[:, l, :], lball[:, l - 1, :], tl[:], ALU.add, [lball, tl], [lball])
        self.ts(omlb[:], lball[:], -1.0, ALU.mult, [lball], [omlb], s2=1.0, op1=ALU.add)

    def layer_params(self, l):
        P = self.P
        P.dma(self.gsb[:], self.a(self.gains)[l], reads=[self.WT_], writes=[self.gsb])
        P.dma(self.cwsb[:], self.a(self.convw)[l], reads=[self.WT_], writes=[self.cwsb])
        P.dma(self.hv[:], self.a(self.hvec)[l], reads=[self.WT_], writes=[self.hv])
        P.dma(self.gn[:], self.a(self.gnv)[l], reads=[self.WT_], writes=[self.gn])
        P.dma(self.wsmst[:], self.a(self.w_sm)[l], reads=[self.WT_], writes=[self.wsmst])
        wsmb, wsmst, negA, hv = self.wsmb, self.wsmst, self.negA, self.hv
        P.op("pool", lambda e: e.tensor_copy(out=wsmb[:], in_=wsmst[:]), [wsmst], [wsmb])
        self.act(negA[:], hv[:, 0:8], AF.Exp, [hv], [negA])
        self.ts(negA[:], negA[:], -1.0, ALU.mult, [negA], [negA])
        for h in range(8):
            P.op("pool", lambda e, h=h: e.memset(self.SA[h][:], 0.0), [], [self.SA[h]])
            P.op("pool", lambda e, h=h: e.memset(self.SB[h][:], 0.0), [], [self.SB[h]])

    def phase1(self, l, b):
        P = self.P
        sl = self.slots
        xts = sl[0:16]
        hb, gsb = self.hb, self.gsb
        XTv = self.a(self.XT).rearrange("(kt p) s -> p kt s", p=128)
        P.dma(self.slotbuf[:, 0:16, :], XTv[:, :, b * 512:(b + 1) * 512], reads=[self.XT_T[b]], writes=xts)
        import os
        kp1 = int(os.environ.get("KP1", "99"))
        if kp1 < 2:
            return
        r = self.rstd_from(lambda kt: xts[kt][:], xts, KT, 1.0 / D, [sl[16], sl[17]])
        if kp1 < 3:
            return
        for kt in range(KT):
            self.stt(hb[:, kt, :], xts[kt][:], gsb[:, 0, kt:kt + 1], r[:], ALU.mult, ALU.mult, [xts[kt], gsb, r], [hb])
        PROJa = self.a(self.PROJ)
        w_in = self.a(self.w_in)
        if kp1 < 4:
            return
        for f in range(120 if kp1 > 4 else 2):
            wb = self.load_w(w_in[l, f])
            ps = self.psum.next()
            for kt in range(KT):
                self.mm(ps[:], wb[:, kt, :], hb[:, kt, :], [wb, hb], ps, start=(kt == 0), stop=(kt == KT - 1))
            s = self.stg.next()
            if F_AZ <= f < F_BF or F_BZ <= f < F_CQ:
                self.act(s[:], ps[:], AF.Silu, [ps], [s])
            elif f >= F_G:
                self.act(s[:], ps[:], AF.Sigmoid, [ps], [s])
            else:
                self.ecopy(s[:], ps[:], [ps], [s], eng="dve")
            P.dma(PROJa[f, :, PAD + b * 512: PAD + (b + 1) * 512], s[:], reads=[s], writes=[self.PROJ_T[f][b]], q="pq")
        ps = self.psum.next()
        wsmb = self.wsmb
        for kt in range(KT):
            self.mm(ps[0:16, :], wsmb[:, kt, :], hb[:, kt, :], [wsmb, hb], ps, start=(kt == 0), stop=(kt == KT - 1))
        s = self.stg.next()
        self.ecopy(s[0:16, :], ps[0:16, :], [ps], [s], eng="dve")
        P.dma(self.a(self.PSM)[:, b * 512:(b + 1) * 512], s[0:16, :], reads=[s], writes=[self.PSM_T[b]], q="pq")

    def mem_prep(self, l):
        P = self.P
        sl = self.slots
        mx = sl[0:16]
        gsb, hm = self.gsb, self.big
        memv = self.a(self.memT).rearrange("(kt p) s -> p kt s", p=128)
        P.dma(self.slotbuf[:, 0:16, 0:NMEM], memv, reads=[self.WT_], writes=mx)
        r = self.rstd_from(lambda kt: mx[kt][:, 0:NMEM], mx, KT, 1.0 / D, [sl[16], sl[17]], n=NMEM)
        for kt in range(KT):
            self.stt(hm[:, kt, 0:NMEM], mx[kt][:, 0:NMEM], gsb[:, 4, kt:kt + 1], r[:, 0:NMEM], ALU.mult, ALU.mult, [mx[kt], gsb, r], [hm])
        w_kv = self.a(self.w_kv)
        kmem, vmem = self.kmem, self.vmem
        for f in range(16):
            wb = self.load_w(w_kv[l, f])
            if f < 8:
                ps = self.psum.next()
                for kt in range(KT):
                    self.mm(ps[:, 0:NMEM], wb[:, kt, :], hm[:, kt, 0:NMEM], [wb, hm], ps, start=(kt == 0), stop=(kt == KT - 1))
                self.ecopy(kmem[:, f, :], ps[:, 0:NMEM], [ps], [kmem])
            else:
                for mt in range(2):
                    ps = self.psum.next()
                    for kt in range(KT):
                        self.mm(ps[:, 0:128], hm[:, kt, mt * 128:(mt + 1) * 128], wb[:, kt, :], [wb, hm], ps, start=(kt == 0), stop=(kt == KT - 1))
                    self.ecopy(vmem[:, mt, (f - 8) * 128:(f - 7) * 128], ps[:, 0:128], [ps], [vmem])

    def gated_out(self, oT, X1, Z1, zrow, gcol, orow, b):
        P = self.P
        t0 = b * 512
        r = self.rstd_from(lambda kt: oT[:], [oT], 1, 1.0 / 128, [X1])
        P.dma(Z1[:], self.a(self.PROJ)[zrow, :, PAD + t0:PAD + t0 + 512], reads=[self.PROJ_T[zrow][b]], writes=[Z1])
        self.stt(oT[:], oT[:], self.gn[:, gcol:gcol + 1], r[:], ALU.mult, ALU.mult, [oT, self.gn, r], [oT])
        self.tt(oT[:], oT[:], Z1[:], ALU.mult, [oT, Z1], [oT])
        P.dma(self.a(self.OD)[orow, :, t0:t0 + 512], oT[:], reads=[oT], writes=[self.OD_T[orow][b]], q="pq")

    def mixer_A(self, l, b, h, smt):
        P = self.P
        t0 = b * 512
        sl = self.slots
        qc, kc, vc, X1, B1, L1, G1, EG, RK, QD, C1, NBU, E1, OT, Z1 = sl[0:15]
        import os
        self.ka = float(os.environ.get('KA', '99'))
        ones, ident, cw = self.ones, self.ident, self.cwsb
        PROJa = self.a(self.PROJ)
        for f0, c in ((F_AQ, qc), (F_AK, kc), (F_AV, vc)):
            f = f0 + h
            raw = self.raw.next()
            rd = [self.PROJ_T[f][b]] + ([self.PROJ_T[f][b - 1]] if b > 0 else [])
            P.dma(raw[:, 0:515], PROJa[f, :, t0:t0 + 515], reads=rd, writes=[raw])
            self.ts(c[:], raw[:, 0:512], cw[:, f, 0:1], ALU.mult, [raw, cw], [c])
            for k in range(1, 4):
                self.stt(c[:], raw[:, k:k + 512], cw[:, f, k:k + 1], c[:], ALU.mult, ALU.add, [raw, cw, c], [c])
            self.act(c[:], c[:], AF.Silu, [c], [c])
        if self.ka < 2:
            return
        for c, mul in ((qc, RS), (kc, 1.0)):
            r = self.rstd_from(lambda kt, c=c: c[:], [c], 1, 1.0, [X1])
            self.stt(c[:], c[:], mul, r[:], ALU.mult, ALU.mult, [c, r], [c])
        if self.ka < 3:
            return
        qn, kn = qc, kc
        hv, negA = self.hv, self.negA
        sm1 = self.tp.next()
        self.ts(sm1[0:16, :], smt[0:16, :], ident[0:16, h:h + 1], ALU.mult, [smt, ident], [sm1])
        ps = self.psum.next()
        self.mm(ps[:], ones[0:16, :], sm1[0:16, :], [ones, sm1], ps)
        self.act(B1[:], ps[:], AF.Sigmoid, [ps], [B1])
        sm2 = self.tp.next()
        self.ts(sm2[0:16, :], smt[0:16, :], ident[0:16, 8 + h:9 + h], ALU.mult, [smt, ident], [sm2])
        ps = self.psum.next()
        self.mm(ps[:], ones[0:16, :], sm2[0:16, :], [ones, sm2], ps)
        self.act(L1[:], ps[:], AF.Exp, [ps, hv], [L1], bias=hv[:, 8 + h:9 + h])
        self.act(L1[:], L1[:], AF.Ln, [L1, self.oneb], [L1], bias=self.oneb[:])
        self.ts(L1[:], L1[:], negA[:, h:h + 1], ALU.mult, [L1, negA], [L1])
        if self.ka < 4:
            return
        resetm = self.resetm
        P.op("dve", lambda e: e.tensor_tensor_scan(out=G1[:], data0=resetm[:], data1=L1[:], initial=0.0, op0=ALU.mult, op1=ALU.add), [resetm, L1], [G1])
        self.act(EG[:], G1[:], AF.Exp, [G1], [EG])
        egl = self.small.next()
        self.act(egl[:, 0:8], G1[:, 63::64], AF.Exp, [G1], [egl])
        self.tt(v3(RK[:], 64), bcast_mid(G1[:, 63::64], 64), v3(G1[:], 64), ALU.subtract, [G1], [RK])
        self.act(RK[:], RK[:], AF.Exp, [RK], [RK])
        if self.ka < 5:
            return
        self.tt(QD[:], qn[:], EG[:], ALU.mult, [qn, EG], [QD])
        self.tt(C1[:], B1[:], EG[:], ALU.mult, [B1, EG], [C1])
        gamcol = self.diag_cols(G1, X1)
        ngamcol = self.small.next()
        self.ts(ngamcol[:, 0:4], gamcol[:, 0:4], -1.0, ALU.mult, [gamcol], [ngamcol])
        betacol = self.diag_cols(B1, X1)
        c1col = self.diag_cols(C1, X1)
        rkcol = self.diag_cols(RK, X1)
        nup_s, nlow_s, up_i = self.nup_s, self.nlow_s, self.up_i
        self.tt(v3(NBU[:], 128), v3(B1[:], 128), bcast_outer(nup_s[:, :], 4), ALU.mult, [B1, nup_s], [NBU])
        if self.ka < 6:
            return
        for pr in range(4):
            s_ = slice(pr * 128, (pr + 1) * 128)
            self.act(E1[:, s_], G1[:, s_], AF.Exp, [G1, gamcol], [E1], scale=-1.0, bias=gamcol[:, pr:pr + 1])
            self.act(X1[:, s_], G1[:, s_], AF.Exp, [G1, ngamcol], [X1], scale=1.0, bias=ngamcol[:, pr:pr + 1])
        self.tt(E1[:], E1[:], X1[:], ALU.min, [E1, X1], [E1])
        if self.ka < 7:
            return
        if self.ka < 7.1:
            return
        S_ = self.SA[h]
        m = self.m128
        for pr in range(4):
            s_ = slice(pr * 128, (pr + 1) * 128)
            ps = self.psum.next()
            self.mm(ps[:, 0:128], kn[:, s_], kn[:, s_], [kn], ps)
            M1 = m.next()
            self.tt(M1[:], ps[:, 0:128], E1[:, s_], ALU.mult, [ps, E1], [M1])
            Pm, Qm, Tt = m.next(), m.next(), m.next()
            self.stt(Pm[:], M1[:], betacol[:, pr:pr + 1], nlow_s[:], ALU.mult, ALU.mult, [M1, betacol, nlow_s], [Pm])
            self.tt(Qm[:], M1[:], NBU[:, s_], ALU.mult, [M1, NBU], [Qm])
            self.tt(Tt[:], Qm[:], ident[:], ALU.add, [Qm, ident], [Tt])
            if self.ka < 7.5:
                continue
            for k in range(1, 6):
                ps = self.psum.next()
                self.mm(ps[:, 0:128], Qm[:], Pm[:], [Qm, Pm], ps)
                if k < 5:
                    psq = self.psum.next()
                    self.mm(psq[:, 0:128], Pm[:], Qm[:], [Qm, Pm], psq)
                Pn = m.next()
                self.ecopy(Pn[:], ps[:, 0:128], [ps], [Pn], eng="act")
                if k < 5:
                    Qn = m.next()
                    self.ecopy(Qn[:], psq[:, 0:128], [psq], [Qn], eng="dve")
                ps2 = self.psum.next()
                self.mm(ps2[:, 0:128], Pn[:], Tt[:], [Pn, Tt], ps2)
                Tn = m.next()
                self.tt(Tn[:], ps2[:, 0:128], Tt[:], ALU.add, [ps2, Tt], [Tn])
                Pm, Tt = Pn, Tn
                if k < 5:
                    Qm = Qn
            if self.ka < 8:
                continue
            ps = self.psum.next()
            self.tr(ps[:, 0:128], kn[:, s_], [kn], ps)
            kbg, kdt = m.next(), m.next()
            self.ts(kbg[:], ps[:, 0:128], c1col[:, pr:pr + 1], ALU.mult, [ps, c1col], [kbg])
            self.ts(kdt[:], ps[:, 0:128], rkcol[:, pr:pr + 1], ALU.mult, [ps, rkcol], [kdt])
            ps = self.psum.next()
            self.tr(ps[:, 0:128], vc[:, s_], [vc], ps)
            vb = m.next()
            self.ts(vb[:], ps[:, 0:128], betacol[:, pr:pr + 1], ALU.mult, [ps, betacol], [vb])
            ps = self.psum.next()
            self.mm(ps[:, 0:128], Tt[:], vb[:], [Tt, vb], ps)
            U = m.next()
            self.ecopy(U[:], ps[:, 0:128], [ps], [U], eng="act")
            ps = self.psum.next()
            self.mm(ps[:, 0:128], kbg[:], Tt[:], [kbg, Tt], ps)
            WTt = m.next()
            self.ecopy(WTt[:], ps[:, 0:128], [ps], [WTt], eng="dve")
            ps = self.psum.next()
            self.mm(ps[:, 0:128], kn[:, s_], qn[:, s_], [kn, qn], ps)
            AT = m.next()
            self.tt(AT[:], ps[:, 0:128], E1[:, s_], ALU.mult, [ps, E1], [AT])
            self.tt(AT[:], AT[:], up_i[:], ALU.mult, [AT, up_i], [AT])
            if self.ka < 9:
                continue
            un = m.next()
            for c in range(2):
                r0 = 64 * c
                ch = pr * 2 + c
                cs = slice(ch * 64, ch * 64 + 64)
                ps = self.psum.next()
                self.mm(ps[:, 0:128], WTt[:], S_[:], [WTt, S_], ps)
                self.tt(un[r0:r0 + 64, :], U[r0:r0 + 64, :], ps[r0:r0 + 64, 0:128], ALU.subtract, [U, ps], [un])
                pso = self.psum.next()
                self.mm(pso[:, 0:64], S_[:], QD[:, cs], [S_, QD], pso, start=True, stop=False)
                self.mm(pso[:, 0:64], un[r0:r0 + 64, :], AT[r0:r0 + 64, r0:r0 + 64], [un, AT], pso, start=False, stop=True)
                self.ecopy(OT[:, cs], pso[:, 0:64], [pso], [OT], eng="act")
                pss = self.psum.next()
                self.mm(pss[:, 0:128], kdt[r0:r0 + 64, :], un[r0:r0 + 64, :], [kdt, un], pss)
                self.stt(S_[:], S_[:], egl[:, ch:ch + 1], pss[:, 0:128], ALU.mult, ALU.add, [S_, egl, pss], [S_])
        if self.ka < 10:
            return
        self.gated_out(OT, X1, Z1, F_AZ + h, 0, h, b)

    def mixer_B(self, l, b, h):
        P = self.P
        t0 = b * 512
        sl = self.slots
        q, f_, vi, Z1, X1, B_, QT, KTl, QE, KE, OT, KK = sl[16:28]
        PROJa = self.a(self.PROJ)
        for row, dst in ((F_BQ + h, q), (F_BF + h, f_), (F_BI + h, vi)):
            P.dma(dst[:], PROJa[row, :, PAD + t0:PAD + t0 + 512], reads=[self.PROJ_T[row][b]], writes=[dst])
        lball, omlb, resetm = self.lball, self.omlb, self.resetm
        self.act(f_[:], f_[:], AF.Sigmoid, [f_], [f_])
        self.ts(f_[:], f_[:], omlb[:, l, h:h + 1], ALU.mult, [f_, omlb, lball], [f_], s2=lball[:, l, h:h + 1], op1=ALU.add)
        self.ts(KK[:], f_[:], -1.0, ALU.mult, [f_], [KK], s2=1.0, op1=ALU.add)
        self.act(f_[:], f_[:], AF.Ln, [f_], [f_])
        P.op("dve", lambda e: e.tensor_tensor_scan(out=B_[:], data0=resetm[:], data1=f_[:], initial=0.0, op0=ALU.mult, op1=ALU.add), [resetm, f_], [B_])
        ebl = self.small.next()
        self.act(ebl[:, 0:8], B_[:, 63::64], AF.Exp, [B_], [ebl])
        self.tt(v3(X1[:], 64), v3(B_[:], 64), bcast_mid(B_[:, 31::64], 64), ALU.subtract, [B_], [X1])
        self.act(QT[:], X1[:], AF.Exp, [X1], [QT])
        self.stt(QT[:], q[:], RS, QT[:], ALU.mult, ALU.mult, [q, QT], [QT])
        self.act(KTl[:], X1[:], AF.Exp, [X1], [KTl], scale=-1.0)
        self.tt(KTl[:], KTl[:], KK[:], ALU.mult, [KTl, KK], [KTl])
        self.act(QE[:], B_[:], AF.Exp, [B_], [QE])
        self.stt(QE[:], q[:], RS, QE[:], ALU.mult, ALU.mult, [q, QE], [QE])
        self.tt(v3(KE[:], 64), bcast_mid(B_[:, 63::64], 64), v3(B_[:], 64), ALU.subtract, [B_], [KE])
        self.act(KE[:], KE[:], AF.Exp, [KE], [KE])
        self.tt(KE[:], KE[:], KK[:], ALU.mult, [KE, KK], [KE])
        S_ = self.SB[h]
        m = self.m128
        up_i = self.up_i
        for pr in range(4):
            s_ = slice(pr * 128, (pr + 1) * 128)
            ps = self.psum.next()
            self.tr(ps[:, 0:128], vi[:, s_], [vi], ps)
            vt = m.next()
            self.ecopy(vt[:], ps[:, 0:128], [ps], [vt], eng="act")
            ps = self.psum.next()
            self.tr(ps[:, 0:128], KE[:, s_], [KE], ps)
            ket = m.next()
            self.ecopy(ket[:], ps[:, 0:128], [ps], [ket], eng="dve")
            ps = self.psum.next()
            self.mm(ps[:, 0:128], KTl[:, s_], QT[:, s_], [KTl, QT], ps)
            AT = m.next()
            self.ts(AT[:], ps[:, 0:128], 1e30, ALU.min, [ps], [AT], s2=-1e30, op1=ALU.max)
            self.tt(AT[:], AT[:], up_i[:], ALU.mult, [AT, up_i], [AT])
            for c in range(2):
                r0 = 64 * c
                ch = pr * 2 + c
                cs = slice(ch * 64, ch * 64 + 64)
                pso = self.psum.next()
                self.mm(pso[:, 0:64], S_[:], QE[:, cs], [S_, QE], pso, start=True, stop=False)
                self.mm(pso[:, 0:64], vt[r0:r0 + 64, :], AT[r0:r0 + 64, r0:r0 + 64], [vt, AT], pso, start=False, stop=True)
                self.ecopy(OT[:, cs], pso[:, 0:64], [pso], [OT], eng="act")
                pss = self.psum.next()
                self.mm(pss[:, 0:128], ket[r0:r0 + 64, :], vt[r0:r0 + 64, :], [ket, vt], pss)
                self.stt(S_[:], S_[:], ebl[:, ch:ch + 1], pss[:, 0:128], ALU.mult, ALU.add, [S_, ebl, pss], [S_])
        self.gated_out(OT, X1, Z1, F_BZ + h, 1, 8 + h, b)

    def mixer_C(self, l, b):
        P = self.P
        t0 = b * 512
        sl = self.slots
        PROJa = self.a(self.PROJ)
        kmem, vmem, ident = self.kmem, self.vmem, self.ident
        m = self.m128
        SC = 256 ** -0.5
        for hc in range(4):
            cqs = []
            for j in range(2):
                row = F_CQ + 2 * hc + j
                st = sl[28 + j]
                P.dma(st[:], PROJa[row, :, PAD + t0:PAD + t0 + 512], reads=[self.PROJ_T[row][b]], writes=[st])
                cb = self.cqb[j]
                self.ecopy(cb[:], st[:], [st], [cb], eng="dve")
                cqs.append(cb)
            o0, o1 = sl[30], sl[31]
            for tq in range(4):
                s_ = slice(tq * 128, (tq + 1) * 128)
                ps = self.psum.next()
                for j in range(2):
                    self.mm(ps[:, 0:NMEM], cqs[j][:, s_], kmem[:, 2 * hc + j, :], [cqs[j], kmem], ps, start=(j == 0), stop=(j == 1))
                mx = self.small.next()
                P.op("dve", lambda e, mx=mx, ps=ps: e.tensor_reduce(out=mx[:, 0:1], in_=ps[:, 0:NMEM], axis=AX.X, op=ALU.max), [ps], [mx])
                self.ts(mx[:, 1:2], mx[:, 0:1], -SC, ALU.mult, [mx], [mx])
                pe_ = self.tp.next()
                self.act(pe_[:, 0:NMEM], ps[:, 0:NMEM], AF.Exp, [ps, mx], [pe_, mx], scale=SC, bias=mx[:, 1:2], accum_out=mx[:, 2:3])
                self.recip(mx[:, 3:4], mx[:, 2:3], [mx], [mx])
                self.ts(pe_[:, 0:NMEM], pe_[:, 0:NMEM], mx[:, 3:4], ALU.mult, [pe_, mx], [pe_])
                pTs = []
                for mt in range(2):
                    ps2 = self.psum.next()
                    self.tr(ps2[:, 0:128], pe_[:, mt * 128:(mt + 1) * 128], [pe_], ps2)
                    pT = self.pTb[mt]
                    self.ecopy(pT[:], ps2[:, 0:128], [ps2], [pT])
                    pTs.append(pT)
                for j, o in ((0, o0), (1, o1)):
                    ps3 = self.psum.next()
                    for mt in range(2):
                        self.mm(ps3[:, 0:128], vmem[:, mt, (2 * hc + j) * 128:(2 * hc + j + 1) * 128], pTs[mt][:], [vmem, pTs[mt]], ps3, start=(mt == 0), stop=(mt == 1))
                    self.ecopy(o[:, s_], ps3[:, 0:128], [ps3], [o])
            for j, o in ((0, o0), (1, o1)):
                row = 16 + 2 * hc + j
                P.dma(self.a(self.OD)[row, :, t0:t0 + 512], o[:], reads=[o], writes=[self.OD_T[row][b]], q="pq")

    def phase3(self, l, b, last):
        P = self.P
        t0 = b * 512
        sl = self.slots
        xts, ys = sl[0:16], sl[16:32]
        hb, big, gsb = self.hb, self.big, self.gsb
        XTv = self.a(self.XT).rearrange("(kt p) s -> p kt s", p=128)
        ODa, PROJa = self.a(self.OD), self.a(self.PROJ)
        w_br, w_out, w1, w2 = self.a(self.w_br), self.a(self.w_out), self.a(self.w1), self.a(self.w2)
        for br in range(3):
            stgs = []
            for kt in range(8):
                row = br * 8 + kt
                st = ys[kt]
                P.dma(st[:], ODa[row, :, t0:t0 + 512], reads=[self.OD_T[row][b]], writes=[st])
                self.ecopy(big[:, kt, :], st[:], [st], [big], eng=("dve" if kt % 2 else "act"))
            for f in range(16):
                wb = self.load_w(w_br[l, br, f])
                ps = self.psum.next()
                for kt in range(8):
                    self.mm(ps[:], wb[:, kt, :], big[:, kt, :], [wb, big], ps, start=(kt == 0), stop=(kt == 7))
                g = self.tp.next()
                row = F_G + br * 16 + f
                P.dma(g[:], PROJa[row, :, PAD + t0:PAD + t0 + 512], reads=[self.PROJ_T[row][b]], writes=[g])
                acc = xts[f]
                if br == 0:
                    self.tt(acc[:], ps[:], g[:], ALU.mult, [ps, g], [acc])
                else:
                    self.tt(g[:], ps[:], g[:], ALU.mult, [ps, g], [g])
                    if br == 1:
                        self.tt(acc[:], acc[:], g[:], ALU.add, [acc, g], [acc])
                    else:
                        self.tt(hb[:, f, :], acc[:], g[:], ALU.add, [acc, g], [hb])
        for f in range(16):
            wb = self.load_w(w_out[l, f])
            ps = self.psum.next()
            for kt in range(KT):
                self.mm(ps[:], wb[:, kt, :], hb[:, kt, :], [wb, hb], ps, start=(kt == 0), stop=(kt == KT - 1))
            self.ecopy(ys[f][:], ps[:], [ps], [ys[f]])
        P.dma(self.slotbuf[:, 0:16, :], XTv[:, :, t0:t0 + 512], reads=[self.XT_T[b]], writes=xts)
        t1, t2 = self.tp.next(), self.tp.next()
        r = self.rstd_from(lambda kt: ys[kt][:], ys, KT, 1.0 / D, [t1, t2])
        for kt in range(KT):
            self.stt(ys[kt][:], ys[kt][:], gsb[:, 1, kt:kt + 1], r[:], ALU.mult, ALU.mult, [ys[kt], gsb, r], [ys[kt]])
            self.tt(xts[kt][:], xts[kt][:], ys[kt][:], ALU.add, [xts[kt], ys[kt]], [xts[kt]])
        t1, t2 = self.tp.next(), self.tp.next()
        r = self.rstd_from(lambda kt: xts[kt][:], xts, KT, 1.0 / D, [t1, t2])
        for kt in range(KT):
            self.stt(hb[:, kt, :], xts[kt][:], gsb[:, 2, kt:kt + 1], r[:], ALU.mult, ALU.mult, [xts[kt], gsb, r], [hb])
        for qtr in range(4):
            for fl in range(16):
                f = qtr * 16 + fl
                wb = self.load_w(w1[l, f])
                ps = self.psum.next()
                for kt in range(KT):
                    self.mm(ps[:], wb[:, kt, :], hb[:, kt, :], [wb, hb], ps, start=(kt == 0), stop=(kt == KT - 1))
                rl = self.tp.next()
                self.act(rl[:], ps[:], AF.Relu, [ps], [rl])
                self.tt(big[:, fl, :], rl[:], rl[:], ALU.mult, [rl], [big])
            for f in range(16):
                wb = self.load_w(w2[l, f, :, qtr * 16:(qtr + 1) * 16, :])
                ps = self.psum.next()
                for kt in range(16):
                    self.mm(ps[:], wb[:, kt, :], big[:, kt, :], [wb, big], ps, start=(kt == 0), stop=(kt == 15))
                if qtr == 0:
                    self.ecopy(ys[f][:], ps[:], [ps], [ys[f]])
                else:
                    self.tt(ys[f][:], ys[f][:], ps[:], ALU.add, [ys[f], ps], [ys[f]])
        t1, t2 = self.tp.next(), self.tp.next()
        r = self.rstd_from(lambda kt: ys[kt][:], ys, KT, 1.0 / D, [t1, t2])
        for kt in range(KT):
            self.stt(ys[kt][:], ys[kt][:], gsb[:, 3, kt:kt + 1], r[:], ALU.mult, ALU.mult, [ys[kt], gsb, r], [ys[kt]])
            self.tt(xts[kt][:], xts[kt][:], ys[kt][:], ALU.add, [xts[kt], ys[kt]], [xts[kt]])
        if last:
            ov = self.a(self.outT).rearrange("(kt p) s -> p kt s", p=128)
            P.dma(ov[:, :, t0:t0 + 512], self.slotbuf[:, 0:16, :], reads=xts, writes=[self.OUT_T], q="pq")
        else:
            P.dma(XTv[:, :, t0:t0 + 512], self.slotbuf[:, 0:16, :], reads=xts, writes=[self.XT_T[b]], q="pq")

    def build(self, stop_after=None):
        P, L, NB = self.P, self.L, self.NB
        self.cqb = [P.sb([128, 512], BF16, f"cqb{j}") for j in range(2)]
        self.pTb = [P.sb([128, 128], BF16, f"pTb{j}") for j in range(2)]
        self.smtb = Rot([P.sb([16, 512], F32, f"smt{j}") for j in range(2)])
        import os
        stop = os.environ.get("KSTOP", "")
        order = ["prologue", "p1", "mem", "A", "B", "C", "p3"]
        lim = order.index(stop) if stop in order else 99
        self.prologue()
        for l in range(L):
            self.layer_params(l)
            if lim >= 1:
                for b in range(NB):
                    self.phase1(l, b)
            if lim >= 2:
                self.mem_prep(l)
            for b in range(NB):
                smt = self.smtb.next()
                P.dma(smt[0:16, :], self.a(self.PSM)[:, b * 512:(b + 1) * 512], reads=[self.PSM_T[b]], writes=[smt])
                for h in range(8):
                    if lim >= 3:
                        self.mixer_A(l, b, h, smt)
                    if lim >= 4:
                        self.mixer_B(l, b, h)
                if lim >= 5:
                    self.mixer_C(l, b)
            if lim >= 6:
                for b in range(NB):
                    self.phase3(l, b, last=(l == L - 1))
        st = P.emit()
        P.close()
        return st

NCORES = 4
_CACHE = {}


def _tile_w(w, kt):
    K, F = w.shape
    return np.ascontiguousarray(w.reshape(kt, 128, F // 128, 128).transpose(2, 1, 0, 3))


def _fm_vec(g):
    return np.ascontiguousarray(g.reshape(-1, 128).T)


def prep_weights(inp, L):
    f32 = np.float32
    w_in = np.asarray(inp["w_in"], f32)
    cols_main = np.r_[0:4096, 4112:15376]
    out = {}
    out["w_in"] = np.stack([_tile_w(w_in[l][:, cols_main], 16) for l in range(L)])
    wsm = w_in[:, :, 4096:4112]
    out["w_sm"] = np.ascontiguousarray(wsm.reshape(L, 16, 128, 16).transpose(0, 2, 1, 3))
    out["w_kv"] = np.stack([_tile_w(np.asarray(inp["w_kv_mem"][l], f32), 16) for l in range(L)])
    out["w_br"] = np.stack([np.stack([_tile_w(np.asarray(inp[n][l], f32), 8) for n in ("w_br_a", "w_br_b", "w_br_c")]) for l in range(L)])
    out["w_out"] = np.stack([_tile_w(np.asarray(inp["w_out"][l], f32), 16) for l in range(L)])
    out["w1"] = np.stack([_tile_w(np.asarray(inp["w_mlp_in"][l], f32), 16) for l in range(L)])
    out["w2"] = np.stack([_tile_w(np.asarray(inp["w_mlp_out"][l], f32), 64) for l in range(L)])
    out["gains"] = np.stack([np.stack([_fm_vec(np.asarray(inp[n][l], f32)) for n in ("g_pre_mix", "g_post_mix", "g_pre_mlp", "g_post_mlp", "g_mem")], axis=1) for l in range(L)])
    cw = np.asarray(inp["conv_a"], f32)
    out["convw"] = np.ascontiguousarray(cw.reshape(L, 4, 24, 128).transpose(0, 3, 2, 1))
    hv = np.concatenate([np.asarray(inp["a_log"], f32), np.asarray(inp["dt_bias"], f32)], axis=1)
    out["hvec"] = np.ascontiguousarray(np.broadcast_to(hv[:, None, :], (L, 128, 16)))
    out["gnv"] = np.ascontiguousarray(np.stack([np.asarray(inp["gn_a"], f32), np.asarray(inp["gn_b"], f32)], axis=2))
    lb = np.asarray(inp["lb_raw"], f32)
    out["lbraw"] = np.ascontiguousarray(lb.reshape(L, 8, 128).transpose(2, 1, 0))
    return out


def kernel(**inp):
    x = np.asarray(inp["x"], np.float32)
    mem = np.asarray(inp["mem"], np.float32)
    B, S, _ = x.shape
    L = inp["w_in"].shape[0]
    key = (S, L)
    if key not in _CACHE:
        k = K(S, L)
        k.build()
        _CACHE[key] = k
    k = _CACHE[key]
    w = prep_weights(inp, L)
    in_maps = []
    for c in range(B):
        m = dict(w)
        m["xT"] = np.ascontiguousarray(x[c].T)
        m["memT"] = np.ascontiguousarray(mem[c].T)
        in_maps.append(m)
    res = run_bass_kernel_spmd(k.nc, in_maps, core_ids=list(range(B)))
    out = np.stack([np.ascontiguousarray(res.results[c]["outT"].T) for c in range(B)])
    return out.astype(np.float32)
```

```python
import numpy as np
import concourse.bass as bass
import concourse.mybir as mybir
from contextlib import ExitStack

F32 = mybir.dt.float32
BF16 = mybir.dt.bfloat16
I32 = mybir.dt.int32
AF = mybir.ActivationFunctionType
ALU = mybir.AluOpType
AX = mybir.AxisListType

COMPUTE = ("pe", "dve", "act", "pool")
DMAQ = ("sp", "pq")
NDMASEM = {"sp": 24, "pq": 12}


class T:
    __slots__ = ("t", "name", "w", "r", "rd")

    def __init__(self, t, name=""):
        self.t = t
        self.name = name
        self.w = None
        self.r = {}
        self.rd = []

    def __getitem__(self, idx):
        return self.t[idx]


class Ins:
    __slots__ = ("eng", "fn", "inc", "idx", "deps", "dma_sem", "dma_val", "ticket")

    def __init__(self, eng, fn, inc):
        self.eng = eng
        self.fn = fn
        self.inc = inc
        self.deps = []
        self.dma_sem = None
        self.dma_val = 0
        self.ticket = None


class Prog:
    def __init__(self, nc):
        self.nc = nc
        self.stack = ExitStack()
        self.all = []
        self.per = {e: [] for e in COMPUTE + DMAQ}
        self.nalloc = 0

    def sb(self, shape, dt=F32, name=None):
        self.nalloc += 1
        name = name or f"sb{self.nalloc}"
        t = self.stack.enter_context(self.nc.sbuf_tensor(name, list(shape), dt))
        return T(t, name)

    def ps(self, shape, dt=F32, name=None):
        self.nalloc += 1
        name = name or f"ps{self.nalloc}"
        t = self.stack.enter_context(self.nc.psum_tensor(name, list(shape), dt))
        return T(t, name)

    def dram(self, name, shape, dt=F32, kind="Internal"):
        t = self.nc.dram_tensor(name, list(shape), dt, kind=kind)
        return T(t, name)

    def view(self, t, name=""):
        return T(t.t if isinstance(t, T) else t, name)

    def op(self, eng, fn, reads=(), writes=(), inc=True):
        ins = Ins(eng, fn, inc)
        isdma = eng in DMAQ
        deps = {}

        def add(d, kind):
            if d is None or d is ins:
                return
            if d.eng == eng and not isdma:
                if kind != "raw" or eng == "pe":
                    return
            deps[id(d)] = d

        for t in reads:
            add(t.w, "raw")
        for t in writes:
            add(t.w, "waw")
            for r in t.r.values():
                add(r, "war")
            for r in t.rd:
                add(r, "war")
        for t in reads:
            if isdma:
                t.rd.append(ins)
            else:
                t.r[eng] = ins
        for t in writes:
            t.w = ins
            t.r = {}
            t.rd = []
        ins.deps = list(deps.values())
        ins.idx = len(self.per[eng])
        self.per[eng].append(ins)
        self.all.append(ins)
        return ins

    def dma(self, out_ap, in_ap, reads=(), writes=(), q="sp", **kw):
        return self.op(q, lambda e: e.dma_start(out=out_ap, in_=in_ap, **kw), reads, writes)

    def emit(self):
        nc = self.nc
        engobj = {"pe": nc.tensor, "dve": nc.vector, "act": nc.scalar, "pool": nc.gpsimd,
                  "sp": nc.sync, "pq": nc.gpsimd}
        sems = {e: self.stack.enter_context(nc.semaphore(f"s_{e}")) for e in COMPUTE}
        dsem = {q: [self.stack.enter_context(nc.semaphore(f"d_{q}{i}")) for i in range(NDMASEM[q])]
                for q in DMAQ}
        for e in COMPUTE:
            if self.per[e]:
                self.per[e][-1].inc = True
        for e in COMPUTE:
            c = 0
            lst = self.per[e]
            for ins in lst:
                if ins.inc:
                    c += 1
                    ins.ticket = c
            nxt = None
            for ins in reversed(lst):
                if ins.inc:
                    nxt = ins.ticket
                else:
                    ins.ticket = nxt
        dcount = {q: [0] * NDMASEM[q] for q in DMAQ}
        dprev = {q: [None] * NDMASEM[q] for q in DMAQ}
        for q in DMAQ:
            for i, ins in enumerate(self.per[q]):
                k = i % NDMASEM[q]
                dcount[q][k] += 16
                ins.dma_sem = dsem[q][k]
                ins.dma_val = dcount[q][k]
                ins.ticket = (q, k)
        issue_eng = {"pe": "pe", "dve": "dve", "act": "act", "pool": "pool", "sp": "sp", "pq": "pool"}
        seen = {e: {} for e in ("pe", "dve", "act", "pool", "sp")}
        nwait = 0
        for ins in self.all:
            ie = issue_eng[ins.eng]
            eo = engobj[ins.eng]
            sn = seen[ie]
            waits = []
            for d in ins.deps:
                if d.eng in DMAQ:
                    waits.append((d.dma_sem, d.dma_val))
                else:
                    waits.append((sems[d.eng], d.ticket))
            if ins.eng in DMAQ:
                q, k = ins.ticket
                prev = ins.dma_val - 16
                if prev > 0:
                    waits.append((ins.dma_sem, prev))
            for s, v in waits:
                key = id(s)
                if sn.get(key, 0) >= v:
                    continue
                sn[key] = v
                eo.wait_ge(s, v)
                nwait += 1
            r = ins.fn(eo)
            if ins.eng in DMAQ:
                r.then_inc(ins.dma_sem, 16)
            elif ins.inc:
                r.then_inc(sems[ins.eng], 1)
        for q in DMAQ:
            for k in range(NDMASEM[q]):
                if dcount[q][k] > 0:
                    nc.sync.wait_ge(dsem[q][k], dcount[q][k])
        for e in COMPUTE:
            lst = self.per[e]
            if lst:
                nc.sync.wait_ge(sems[e], lst[-1].ticket)
        self.stats = {e: len(self.per[e]) for e in self.per}
        self.stats["waits"] = nwait
        return self.stats

    def close(self):
        self.stack.close()
from concourse.bass_utils import run_bass_kernel_spmd

D = 2048
KT = 16
NMEM = 256
EPS = 1e-6
F_AQ, F_AK, F_AV, F_AZ, F_BQ, F_BF, F_BI, F_BZ, F_CQ, F_G = 0, 8, 16, 24, 32, 40, 48, 56, 64, 72
PAD = 3
RS = 128 ** -0.5


def bcast_mid(ap2, n_inner):
    a = ap2.ap
    return bass.AP(ap2.tensor, ap2.offset, [list(a[0]), list(a[1]), [0, n_inner]])


def bcast_outer(ap2, n_outer):
    a = ap2.ap
    return bass.AP(ap2.tensor, ap2.offset, [list(a[0]), [0, n_outer], list(a[1])])


def v3(ap, inner):
    return ap.rearrange("p (a b) -> p a b", b=inner)


def rr(gens):
    gens = list(gens)
    while gens:
        for g in list(gens):
            try:
                next(g)
                yield
            except StopIteration:
                gens.remove(g)


class Rot:
    def __init__(self, items):
        self.items = items
        self.i = 0

    def next(self):
        x = self.items[self.i % len(self.items)]
        self.i += 1
        return x


class K:
    def __init__(self, S, L):
        self.S, self.L = S, L
        self.NB = S // 512
        nc = bass.Bass("TRN2", target_bir_lowering=False)
        self.nc = nc
        P = self.P = Prog(nc)
        NB = self.NB
        ext = lambda n, s: P.dram(n, s, F32, kind="ExternalInput")
        self.xT_in = ext("xT", [D, S])
        self.memT = ext("memT", [D, NMEM])
        self.w_in = ext("w_in", [L, 120, 128, KT, 128])
        self.w_sm = ext("w_sm", [L, 128, KT, 16])
        self.w_kv = ext("w_kv", [L, 16, 128, KT, 128])
        self.w_br = ext("w_br", [L, 3, 16, 128, 8, 128])
        self.w_out = ext("w_out", [L, 16, 128, KT, 128])
        self.w1 = ext("w1", [L, 64, 128, KT, 128])
        self.w2 = ext("w2", [L, 16, 128, 64, 128])
        self.gains = ext("gains", [L, 128, 5, KT])
        self.convw = ext("convw", [L, 128, 24, 4])
        self.hvec = ext("hvec", [L, 128, 16])
        self.gnv = ext("gnv", [L, 128, 2])
        self.lbraw = ext("lbraw", [128, 8, L])
        self.outT = P.dram("outT", [D, S], F32, kind="ExternalOutput")
        self.XT = P.dram("XT", [D, S], F32)
        self.PROJ = P.dram("PROJ", [120, 128, S + PAD], F32)
        self.PSM = P.dram("PSM", [16, S], F32)
        self.OD = P.dram("OD", [24, 128, S], F32)
        self.wb_in = P.dram("wb_in", [120, 128, KT, 128], BF16)
        self.wb_br = P.dram("wb_br", [3, 16, 128, 8, 128], BF16)
        self.wb_out = P.dram("wb_out", [16, 128, KT, 128], BF16)
        self.wb_w1 = P.dram("wb_w1", [64, 128, KT, 128], BF16)
        self.wb_w2 = P.dram("wb_w2", [16, 4, 128, KT, 128], BF16)
        self.WB_T = {}
        self.XT_T = [T(None, f"XT{b}") for b in range(NB)]
        self.PROJ_T = [[T(None, f"PJ{f}_{b}") for b in range(NB)] for f in range(120)]
        self.PSM_T = [T(None, f"PSM{b}") for b in range(NB)]
        self.OD_T = [[T(None, f"OD{f}_{b}") for b in range(NB)] for f in range(24)]
        self.WT_ = T(None, "weights")
        self.OUT_T = T(None, "out")

        self.ones = P.sb([128, 128], F32, "ones")
        self.ident = P.sb([128, 128], F32, "ident")
        self.nlow_s = P.sb([128, 128], F32, "nlow_s")
        self.nup_s = P.sb([128, 128], F32, "nup_s")
        self.up_i = P.sb([128, 128], F32, "up_i")
        self.resetm = P.sb([128, 512], F32, "resetm")
        self.zero3 = P.sb([128, 4], F32, "zero3")
        self.epsb = P.sb([128, 1], F32, "epsb")
        self.oneb = P.sb([128, 1], F32, "oneb")
        ones, ident, nlow_s, nup_s, up_i, resetm, zero3 = self.ones, self.ident, self.nlow_s, self.nup_s, self.up_i, self.resetm, self.zero3
        P.op("pool", lambda e: e.memset(ones[:], 1.0), [], [ones])
        P.op("pool", lambda e: e.memset(zero3[:], 0.0), [], [zero3])
        P.op("pool", lambda e: e.memset(self.epsb[:], EPS), [], [self.epsb])
        P.op("pool", lambda e: e.memset(self.oneb[:], 1.0), [], [self.oneb])
        P.op("pool", lambda e: e.memset(resetm[:], 1.0), [], [resetm])
        P.op("pool", lambda e: e.memset(resetm[:, ::64], 0.0), [resetm], [resetm])
        P.op("pool", lambda e: e.affine_select(out=ident[:], in_=ones[:], pattern=[[-1, 128]], compare_op=ALU.is_equal, fill=0.0, base=0, channel_multiplier=1), [ones], [ident])
        P.op("pool", lambda e: e.memset(nlow_s[:], -1.0), [], [nlow_s])
        P.op("pool", lambda e: e.memset(nup_s[:], -1.0), [], [nup_s])
        P.op("pool", lambda e: e.affine_select(out=nlow_s[:], in_=nlow_s[:], pattern=[[-1, 128]], compare_op=ALU.is_ge, fill=0.0, base=-1, channel_multiplier=1), [nlow_s], [nlow_s])
        P.op("pool", lambda e: e.memset(nlow_s[64:128, 0:64], 0.0), [nlow_s], [nlow_s])
        P.op("pool", lambda e: e.affine_select(out=nup_s[:], in_=nup_s[:], pattern=[[1, 128]], compare_op=ALU.is_ge, fill=0.0, base=-1, channel_multiplier=-1), [nup_s], [nup_s])
        P.op("pool", lambda e: e.memset(nup_s[0:64, 64:128], 0.0), [nup_s], [nup_s])
        P.op("pool", lambda e: e.affine_select(out=up_i[:], in_=ones[:], pattern=[[1, 128]], compare_op=ALU.is_ge, fill=0.0, base=0, channel_multiplier=-1), [ones], [up_i])
        P.op("pool", lambda e: e.memset(up_i[0:64, 64:128], 0.0), [up_i], [up_i])

        self.psum = Rot([P.ps([128, 512], F32, f"psb{i}") for i in range(8)])
        self.wstage = Rot([P.sb([128, 16, 128], F32, f"wst{i}") for i in range(2)])
        self.wbf = Rot([P.sb([128, 16, 128], BF16, f"wbf{i}") for i in range(4)])
        self.slotbuf = P.sb([128, 32, 512], F32, "slots")
        self.slots = [T(self.slotbuf[:, i, :], f"slot{i}") for i in range(32)]
        self.hb = P.sb([128, KT, 512], BF16, "hb")
        self.big = P.sb([128, KT, 512], BF16, "big")
        self.stg = Rot([P.sb([128, 512], F32, f"stg{i}") for i in range(4)])
        self.tp = Rot([P.sb([128, 512], F32, f"tp{i}") for i in range(5)])
        self.m128 = Rot([P.sb([128, 128], F32, f"m{i}") for i in range(38)])
        self.m128b = Rot([P.sb([128, 128], F32, f"mb{i}") for i in range(10)])
        self.small = Rot([P.sb([128, 8], F32, f"sm{i}") for i in range(32)])
        self.raw = Rot([P.sb([128, 516], F32, f"raw{i}") for i in range(3)])
        self.kmem = P.sb([128, 8, NMEM], BF16, "kmem")
        self.vmem = P.sb([128, 2, 1024], BF16, "vmem")
        self.gsb = P.sb([128, 5, KT], F32, "gsb")
        self.cwsb = P.sb([128, 24, 4], F32, "cwsb")
        self.hv = P.sb([128, 16], F32, "hv")
        self.negA = P.sb([128, 8], F32, "negA")
        self.gn = P.sb([128, 2], F32, "gn")
        self.lbr = P.sb([128, 8, L], F32, "lbr")
        self.lbe = P.sb([128, 8, L], F32, "lbe")
        self.lbs = P.sb([128, 8], F32, "lbs")
        self.lball = P.sb([128, L, 8], F32, "lball")
        self.omlb = P.sb([128, L, 8], F32, "omlb")
        self.wsmst = P.sb([128, KT, 16], F32, "wsmst")
        self.wsmb = P.sb([128, KT, 16], BF16, "wsmb")
        self.SA = [P.sb([128, 128], F32, f"SA{h}") for h in range(8)]
        self.SB = [P.sb([128, 128], F32, f"SB{h}") for h in range(8)]
        self.evq = 0

    def a(self, t):
        return t.t.ap()

    def mm(self, ps_ap, lhsT, rhs, reads, ps_t, start=True, stop=True, inc=None):
        return self.P.op("pe", lambda e: e.matmul(ps_ap, lhsT, rhs, start=start, stop=stop), reads, [ps_t], inc=(stop if inc is None else inc))

    def tr(self, ps_ap, in_ap, reads, ps_t):
        ident = self.ident
        return self.P.op("pe", lambda e: e.transpose(ps_ap, in_ap, ident[:]), reads + [ident], [ps_t], inc=True)

    def act(self, out, in_, func, reads, writes, **kw):
        return self.P.op("act", lambda e: e.activation(out=out, in_=in_, func=func, **kw), reads, writes)

    def ecopy(self, out, in_, reads, writes, eng=None):
        self.evq += 1
        if eng is None:
            eng = "act" if self.evq % 2 else "dve"
        if eng == "act":
            return self.P.op("act", lambda e: e.copy(out=out, in_=in_), reads, writes)
        return self.P.op("dve", lambda e: e.tensor_copy(out=out, in_=in_), reads, writes)

    def tt(self, out, in0, in1, op, reads, writes):
        return self.P.op("dve", lambda e: e.tensor_tensor(out=out, in0=in0, in1=in1, op=op), reads, writes)

    def ts(self, out, in0, s1, op0, reads, writes, s2=None, op1=None):
        if op1 is None:
            return self.P.op("dve", lambda e: e.tensor_scalar(out=out, in0=in0, scalar1=s1, scalar2=None, op0=op0), reads, writes)
        return self.P.op("dve", lambda e: e.tensor_scalar(out=out, in0=in0, scalar1=s1, scalar2=s2, op0=op0, op1=op1), reads, writes)

    def stt(self, out, in0, scalar, in1, op0, op1, reads, writes):
        return self.P.op("dve", lambda e: e.scalar_tensor_tensor(out=out, in0=in0, scalar=scalar, in1=in1, op0=op0, op1=op1), reads, writes)

    def recip(self, out, in_, reads, writes):
        return self.P.op("dve", lambda e: e.reciprocal(out=out, in_=in_), reads, writes)

    def rstd_from(self, src_fn, src_reads, nkt, scale, sqs, n=512):
        ps = self.psum.next()
        for kt in range(nkt):
            sq = sqs[kt % 2] if nkt > 1 else sqs[0]
            self.act(sq[:, 0:n], src_fn(kt), AF.Square, src_reads, [sq])
            self.mm(ps[:, 0:n], self.ones[:], sq[:, 0:n], [self.ones, sq], ps, start=(kt == 0), stop=(kt == nkt - 1), inc=True)
        r = sqs[0]
        self.act(r[:, 0:n], ps[:, 0:n], AF.Sqrt, [ps, self.epsb], [r], scale=scale, bias=self.epsb[:])
        self.recip(r[:, 0:n], r[:, 0:n], [r], [r])
        return r

    def load_w(self, w_ap, key=None, b=0, wb_ap=None):
        k = w_ap.shape[1]
        wb = self.wbf.next()
        if key is not None:
            wt = self.WB_T.setdefault(key, T(None, str(key)))
            if b > 0:
                self.P.dma(wb[:, 0:k, :], wb_ap, reads=[wt], writes=[wb])
                return wb
        st = self.wstage.next()
        self.P.dma(st[:, 0:k, :], w_ap, reads=[self.WT_], writes=[st])
        self.ecopy(wb[:, 0:k, :], st[:, 0:k, :], [st], [wb])
        if key is not None:
            self.P.dma(wb_ap, wb[:, 0:k, :], reads=[wb], writes=[wt], q="pq")
        return wb

    def diag_cols(self, src, scratch):
        ident = self.ident
        self.tt(v3(scratch[:], 128), v3(src[:], 128), bcast_outer(ident[:, :], 4), ALU.mult, [src, ident], [scratch])
        c = self.small.next()
        self.P.op("dve", lambda e: e.tensor_reduce(out=c[:, 0:4], in_=v3(scratch[:], 128), axis=AX.X, op=ALU.add), [scratch], [c])
        return c

    def prologue(self):
        P, L, NB = self.P, self.L, self.NB
        PROJa, XTa, xTa = self.a(self.PROJ), self.a(self.XT), self.a(self.xT_in)
        for f in range(24):
            P.dma(PROJa[f, :, 0:PAD], self.zero3[:, 0:PAD], reads=[self.zero3], writes=[self.PROJ_T[f][0]], q="pq")
        for b in range(NB):
            P.dma(XTa[:, b * 512:(b + 1) * 512], xTa[:, b * 512:(b + 1) * 512], reads=[self.WT_], writes=[self.XT_T[b]], q="pq")
        lbr, lbe, lbs, lball, omlb = self.lbr, self.lbe, self.lbs, self.lball, self.omlb
        P.dma(lbr[:], self.a(self.lbraw), reads=[self.WT_], writes=[lbr])
        self.act(lbe[:], lbr[:], AF.Exp, [lbr], [lbe])
        P.op("dve", lambda e: e.tensor_reduce(out=lbs[:], in_=lbe[:], axis=AX.X, op=ALU.add), [lbe], [lbs])
        self.recip(lbs[:], lbs[:], [lbs], [lbs])
        P.op("dve", lambda e: e.memset(lball[:, 0, :], 0.0), [], [lball])
        for l in range(1, L):
            tl = self.small.next()
            self.tt(tl[:], lbe[:, :, l], lbs[:], ALU.mult, [lbe, lbs], [tl])
            self.tt(lball[:, l, :], lball[:, l - 1, :], tl[:], ALU.add, [lball, tl], [lball])
        self.ts(omlb[:], lball[:], -1.0, ALU.mult, [lball], [omlb], s2=1.0, op1=ALU.add)

    def layer_params(self, l):
        P = self.P
        P.dma(self.gsb[:], self.a(self.gains)[l], reads=[self.WT_], writes=[self.gsb])
        P.dma(self.cwsb[:], self.a(self.convw)[l], reads=[self.WT_], writes=[self.cwsb])
        P.dma(self.hv[:], self.a(self.hvec)[l], reads=[self.WT_], writes=[self.hv])
        P.dma(self.gn[:], self.a(self.gnv)[l], reads=[self.WT_], writes=[self.gn])
        P.dma(self.wsmst[:], self.a(self.w_sm)[l], reads=[self.WT_], writes=[self.wsmst])
        wsmb, wsmst, negA, hv = self.wsmb, self.wsmst, self.negA, self.hv
        P.op("pool", lambda e: e.tensor_copy(out=wsmb[:], in_=wsmst[:]), [wsmst], [wsmb])
        self.act(negA[:], hv[:, 0:8], AF.Exp, [hv], [negA])
        self.ts(negA[:], negA[:], -1.0, ALU.mult, [negA], [negA])
        for h in range(8):
            P.op("pool", lambda e, h=h: e.memset(self.SA[h][:], 0.0), [], [self.SA[h]])
            P.op("pool", lambda e, h=h: e.memset(self.SB[h][:], 0.0), [], [self.SB[h]])

    def phase1(self, l, b0, nb):
        P = self.P
        sl = self.slots
        xts = sl[0:16]
        gsb = self.gsb
        acts = [self.hb, self.big][:nb]
        XTv = self.a(self.XT).rearrange("(kt p) s -> p kt s", p=128)
        for i in range(nb):
            b = b0 + i
            hb = acts[i]
            P.dma(self.slotbuf[:, 0:16, :], XTv[:, :, b * 512:(b + 1) * 512], reads=[self.XT_T[b]], writes=xts)
            r = self.rstd_from(lambda kt: xts[kt][:], xts, KT, 1.0 / D, [sl[16], sl[17]])
            for kt in range(KT):
                self.stt(hb[:, kt, :], xts[kt][:], gsb[:, 0, kt:kt + 1], r[:], ALU.mult, ALU.mult, [xts[kt], gsb, r], [hb])
        PROJa = self.a(self.PROJ)
        w_in = self.a(self.w_in)
        wb_in = self.a(self.wb_in)
        for f in range(120):
            wb = self.load_w(w_in[l, f], key=("in", f), b=b0, wb_ap=wb_in[f])
            for i in range(nb):
                b = b0 + i
                hb = acts[i]
                ps = self.psum.next()
                for kt in range(KT):
                    self.mm(ps[:], wb[:, kt, :], hb[:, kt, :], [wb, hb], ps, start=(kt == 0), stop=(kt == KT - 1))
                s = self.stg.next()
                if F_AZ <= f < F_BF or F_BZ <= f < F_CQ:
                    self.act(s[:], ps[:], AF.Silu, [ps], [s])
                elif f >= F_G:
                    self.act(s[:], ps[:], AF.Sigmoid, [ps], [s])
                else:
                    self.ecopy(s[:], ps[:], [ps], [s], eng="dve")
                P.dma(PROJa[f, :, PAD + b * 512: PAD + (b + 1) * 512], s[:], reads=[s], writes=[self.PROJ_T[f][b]], q="pq")
        wsmb = self.wsmb
        for i in range(nb):
            b = b0 + i
            hb = acts[i]
            ps = self.psum.next()
            for kt in range(KT):
                self.mm(ps[0:16, :], wsmb[:, kt, :], hb[:, kt, :], [wsmb, hb], ps, start=(kt == 0), stop=(kt == KT - 1))
            s = self.stg.next()
            self.ecopy(s[0:16, :], ps[0:16, :], [ps], [s], eng="dve")
            P.dma(self.a(self.PSM)[:, b * 512:(b + 1) * 512], s[0:16, :], reads=[s], writes=[self.PSM_T[b]], q="pq")

    def mem_prep(self, l):
        P = self.P
        sl = self.slots
        mx = sl[0:16]
        gsb, hm = self.gsb, self.big
        memv = self.a(self.memT).rearrange("(kt p) s -> p kt s", p=128)
        P.dma(self.slotbuf[:, 0:16, 0:NMEM], memv, reads=[self.WT_], writes=mx)
        r = self.rstd_from(lambda kt: mx[kt][:, 0:NMEM], mx, KT, 1.0 / D, [sl[16], sl[17]], n=NMEM)
        for kt in range(KT):
            self.stt(hm[:, kt, 0:NMEM], mx[kt][:, 0:NMEM], gsb[:, 4, kt:kt + 1], r[:, 0:NMEM], ALU.mult, ALU.mult, [mx[kt], gsb, r], [hm])
        w_kv = self.a(self.w_kv)
        kmem, vmem = self.kmem, self.vmem
        for f in range(16):
            wb = self.load_w(w_kv[l, f])
            if f < 8:
                ps = self.psum.next()
                for kt in range(KT):
                    self.mm(ps[:, 0:NMEM], wb[:, kt, :], hm[:, kt, 0:NMEM], [wb, hm], ps, start=(kt == 0), stop=(kt == KT - 1))
                self.ecopy(kmem[:, f, :], ps[:, 0:NMEM], [ps], [kmem])
            else:
                for mt in range(2):
                    ps = self.psum.next()
                    for kt in range(KT):
                        self.mm(ps[:, 0:128], hm[:, kt, mt * 128:(mt + 1) * 128], wb[:, kt, :], [wb, hm], ps, start=(kt == 0), stop=(kt == KT - 1))
                    self.ecopy(vmem[:, mt, (f - 8) * 128:(f - 7) * 128], ps[:, 0:128], [ps], [vmem])

    def gated_out(self, oT, X1, Z1, zrow, gcol, orow, b):
        P = self.P
        t0 = b * 512
        r = self.rstd_from(lambda kt: oT[:], [oT], 1, 1.0 / 128, [X1])
        P.dma(Z1[:], self.a(self.PROJ)[zrow, :, PAD + t0:PAD + t0 + 512], reads=[self.PROJ_T[zrow][b]], writes=[Z1])
        self.stt(oT[:], oT[:], self.gn[:, gcol:gcol + 1], r[:], ALU.mult, ALU.mult, [oT, self.gn, r], [oT])
        self.tt(oT[:], oT[:], Z1[:], ALU.mult, [oT, Z1], [oT])
        P.dma(self.a(self.OD)[orow, :, t0:t0 + 512], oT[:], reads=[oT], writes=[self.OD_T[orow][b]], q="pq")

    def mixer_A(self, l, b, h, smt):
        P = self.P
        t0 = b * 512
        sl = self.slots
        qc, kc, vc, X1, B1, L1, G1, EG, RK, QD, C1, NBU, E1, OT, Z1 = sl[0:15]
        import os
        self.ka = float(os.environ.get('KA', '99'))
        ones, ident, cw = self.ones, self.ident, self.cwsb
        PROJa = self.a(self.PROJ)
        for f0, c in ((F_AQ, qc), (F_AK, kc), (F_AV, vc)):
            f = f0 + h
            raw = self.raw.next()
            rd = [self.PROJ_T[f][b]] + ([self.PROJ_T[f][b - 1]] if b > 0 else [])
            P.dma(raw[:, 0:515], PROJa[f, :, t0:t0 + 515], reads=rd, writes=[raw])
            self.ts(c[:], raw[:, 0:512], cw[:, f, 0:1], ALU.mult, [raw, cw], [c])
            for k in range(1, 4):
                self.stt(c[:], raw[:, k:k + 512], cw[:, f, k:k + 1], c[:], ALU.mult, ALU.add, [raw, cw, c], [c])
            self.act(c[:], c[:], AF.Silu, [c], [c])
        if self.ka < 2:
            return
        for c, mul in ((qc, RS), (kc, 1.0)):
            r = self.rstd_from(lambda kt, c=c: c[:], [c], 1, 1.0, [X1])
            self.stt(c[:], c[:], mul, r[:], ALU.mult, ALU.mult, [c, r], [c])
        if self.ka < 3:
            return
        qn, kn = qc, kc
        hv, negA = self.hv, self.negA
        sm1 = self.tp.next()
        self.ts(sm1[0:16, :], smt[0:16, :], ident[0:16, h:h + 1], ALU.mult, [smt, ident], [sm1])
        ps = self.psum.next()
        self.mm(ps[:], ones[0:16, :], sm1[0:16, :], [ones, sm1], ps)
        self.act(B1[:], ps[:], AF.Sigmoid, [ps], [B1])
        sm2 = self.tp.next()
        self.ts(sm2[0:16, :], smt[0:16, :], ident[0:16, 8 + h:9 + h], ALU.mult, [smt, ident], [sm2])
        ps = self.psum.next()
        self.mm(ps[:], ones[0:16, :], sm2[0:16, :], [ones, sm2], ps)
        self.act(L1[:], ps[:], AF.Exp, [ps, hv], [L1], bias=hv[:, 8 + h:9 + h])
        self.act(L1[:], L1[:], AF.Ln, [L1, self.oneb], [L1], bias=self.oneb[:])
        self.ts(L1[:], L1[:], negA[:, h:h + 1], ALU.mult, [L1, negA], [L1])
        if self.ka < 4:
            return
        resetm = self.resetm
        P.op("dve", lambda e: e.tensor_tensor_scan(out=G1[:], data0=resetm[:], data1=L1[:], initial=0.0, op0=ALU.mult, op1=ALU.add), [resetm, L1], [G1])
        self.act(EG[:], G1[:], AF.Exp, [G1], [EG])
        egl = self.small.next()
        self.act(egl[:, 0:8], G1[:, 63::64], AF.Exp, [G1], [egl])
        self.tt(v3(RK[:], 64), bcast_mid(G1[:, 63::64], 64), v3(G1[:], 64), ALU.subtract, [G1], [RK])
        self.act(RK[:], RK[:], AF.Exp, [RK], [RK])
        if self.ka < 5:
            return
        self.tt(QD[:], qn[:], EG[:], ALU.mult, [qn, EG], [QD])
        self.tt(C1[:], B1[:], EG[:], ALU.mult, [B1, EG], [C1])
        gamcol = self.diag_cols(G1, X1)
        ngamcol = self.small.next()
        self.ts(ngamcol[:, 0:4], gamcol[:, 0:4], -1.0, ALU.mult, [gamcol], [ngamcol])
        betacol = self.diag_cols(B1, X1)
        c1col = self.diag_cols(C1, X1)
        rkcol = self.diag_cols(RK, X1)
        nup_s, nlow_s, up_i = self.nup_s, self.nlow_s, self.up_i
        self.tt(v3(NBU[:], 128), v3(B1[:], 128), bcast_outer(nup_s[:, :], 4), ALU.mult, [B1, nup_s], [NBU])
        if self.ka < 6:
            return
        for pr in range(4):
            s_ = slice(pr * 128, (pr + 1) * 128)
            self.act(E1[:, s_], G1[:, s_], AF.Exp, [G1, gamcol], [E1], scale=-1.0, bias=gamcol[:, pr:pr + 1])
            self.act(X1[:, s_], G1[:, s_], AF.Exp, [G1, ngamcol], [X1], scale=1.0, bias=ngamcol[:, pr:pr + 1])
        self.tt(E1[:], E1[:], X1[:], ALU.min, [E1, X1], [E1])
        if self.ka < 7:
            return
        if self.ka < 7.1:
            return
        yield
        S_ = self.SA[h]
        m = self.m128
        res = {}

        def prep(pr):
            s_ = slice(pr * 128, (pr + 1) * 128)
            ps = self.psum.next()
            self.mm(ps[:, 0:128], kn[:, s_], kn[:, s_], [kn], ps)
            M1 = m.next()
            self.tt(M1[:], ps[:, 0:128], E1[:, s_], ALU.mult, [ps, E1], [M1])
            Pm, Qm, Tt = m.next(), m.next(), m.next()
            self.stt(Pm[:], M1[:], betacol[:, pr:pr + 1], nlow_s[:], ALU.mult, ALU.mult, [M1, betacol, nlow_s], [Pm])
            self.tt(Qm[:], M1[:], NBU[:, s_], ALU.mult, [M1, NBU], [Qm])
            self.tt(Tt[:], Qm[:], ident[:], ALU.add, [Qm, ident], [Tt])
            yield
            for k in range(1, 6):
                ps = self.psum.next()
                self.mm(ps[:, 0:128], Qm[:], Pm[:], [Qm, Pm], ps)
                if k < 5:
                    psq = self.psum.next()
                    self.mm(psq[:, 0:128], Pm[:], Qm[:], [Qm, Pm], psq)
                Pn = m.next()
                self.ecopy(Pn[:], ps[:, 0:128], [ps], [Pn], eng="act")
                if k < 5:
                    Qn = m.next()
                    self.ecopy(Qn[:], psq[:, 0:128], [psq], [Qn], eng="dve")
                yield
                ps2 = self.psum.next()
                self.mm(ps2[:, 0:128], Pn[:], Tt[:], [Pn, Tt], ps2)
                Tn = m.next()
                self.tt(Tn[:], ps2[:, 0:128], Tt[:], ALU.add, [ps2, Tt], [Tn])
                Pm, Tt = Pn, Tn
                if k < 5:
                    Qm = Qn
                yield
            ps = self.psum.next()
            self.tr(ps[:, 0:128], kn[:, s_], [kn], ps)
            kbg, kdt = m.next(), m.next()
            self.ts(kbg[:], ps[:, 0:128], c1col[:, pr:pr + 1], ALU.mult, [ps, c1col], [kbg])
            self.ts(kdt[:], ps[:, 0:128], rkcol[:, pr:pr + 1], ALU.mult, [ps, rkcol], [kdt])
            ps = self.psum.next()
            self.tr(ps[:, 0:128], vc[:, s_], [vc], ps)
            vb = m.next()
            self.ts(vb[:], ps[:, 0:128], betacol[:, pr:pr + 1], ALU.mult, [ps, betacol], [vb])
            yield
            ps = self.psum.next()
            self.mm(ps[:, 0:128], Tt[:], vb[:], [Tt, vb], ps)
            U = m.next()
            self.ecopy(U[:], ps[:, 0:128], [ps], [U], eng="act")
            ps = self.psum.next()
            self.mm(ps[:, 0:128], kbg[:], Tt[:], [kbg, Tt], ps)
            WTt = m.next()
            self.ecopy(WTt[:], ps[:, 0:128], [ps], [WTt], eng="dve")
            yield
            ps = self.psum.next()
            self.mm(ps[:, 0:128], kn[:, s_], qn[:, s_], [kn, qn], ps)
            AT = m.next()
            self.tt(AT[:], ps[:, 0:128], E1[:, s_], ALU.mult, [ps, E1], [AT])
            self.tt(AT[:], AT[:], up_i[:], ALU.mult, [AT, up_i], [AT])
            res[pr] = (kdt, U, WTt, AT)
            yield

        def rec(pr):
            kdt, U, WTt, AT = res[pr]
            un = m.next()
            for c in range(2):
                r0 = 64 * c
                ch = pr * 2 + c
                cs = slice(ch * 64, ch * 64 + 64)
                ps = self.psum.next()
                self.mm(ps[:, 0:128], WTt[:], S_[:], [WTt, S_], ps)
                self.tt(un[r0:r0 + 64, :], U[r0:r0 + 64, :], ps[r0:r0 + 64, 0:128], ALU.subtract, [U, ps], [un])
                yield
                pso = self.psum.next()
                self.mm(pso[:, 0:64], S_[:], QD[:, cs], [S_, QD], pso, start=True, stop=False)
                self.mm(pso[:, 0:64], un[r0:r0 + 64, :], AT[r0:r0 + 64, r0:r0 + 64], [un, AT], pso, start=False, stop=True)
                self.ecopy(OT[:, cs], pso[:, 0:64], [pso], [OT], eng="act")
                pss = self.psum.next()
                self.mm(pss[:, 0:128], kdt[r0:r0 + 64, :], un[r0:r0 + 64, :], [kdt, un], pss)
                self.stt(S_[:], S_[:], egl[:, ch:ch + 1], pss[:, 0:128], ALU.mult, ALU.add, [S_, egl, pss], [S_])
                yield

        yield from prep(0)
        for pr in range(4):
            gens = [rec(pr)] + ([prep(pr + 1)] if pr < 3 else [])
            yield from rr(gens)
        self.gated_out(OT, X1, Z1, F_AZ + h, 0, h, b)

    def mixer_B(self, l, b, h):
        P = self.P
        t0 = b * 512
        sl = self.slots
        q, f_, vi, Z1, X1, B_, QT, KTl, QE, KE, OT, KK = sl[16:28]
        PROJa = self.a(self.PROJ)
        for row, dst in ((F_BQ + h, q), (F_BF + h, f_), (F_BI + h, vi)):
            P.dma(dst[:], PROJa[row, :, PAD + t0:PAD + t0 + 512], reads=[self.PROJ_T[row][b]], writes=[dst])
        lball, omlb, resetm = self.lball, self.omlb, self.resetm
        self.act(f_[:], f_[:], AF.Sigmoid, [f_], [f_])
        self.ts(f_[:], f_[:], omlb[:, l, h:h + 1], ALU.mult, [f_, omlb, lball], [f_], s2=lball[:, l, h:h + 1], op1=ALU.add)
        self.ts(KK[:], f_[:], -1.0, ALU.mult, [f_], [KK], s2=1.0, op1=ALU.add)
        self.act(f_[:], f_[:], AF.Ln, [f_], [f_])
        P.op("dve", lambda e: e.tensor_tensor_scan(out=B_[:], data0=resetm[:], data1=f_[:], initial=0.0, op0=ALU.mult, op1=ALU.add), [resetm, f_], [B_])
        ebl = self.small.next()
        self.act(ebl[:, 0:8], B_[:, 63::64], AF.Exp, [B_], [ebl])
        self.tt(v3(X1[:], 64), v3(B_[:], 64), bcast_mid(B_[:, 31::64], 64), ALU.subtract, [B_], [X1])
        self.act(QT[:], X1[:], AF.Exp, [X1], [QT])
        self.stt(QT[:], q[:], RS, QT[:], ALU.mult, ALU.mult, [q, QT], [QT])
        self.act(KTl[:], X1[:], AF.Exp, [X1], [KTl], scale=-1.0)
        self.tt(KTl[:], KTl[:], KK[:], ALU.mult, [KTl, KK], [KTl])
        self.act(QE[:], B_[:], AF.Exp, [B_], [QE])
        self.stt(QE[:], q[:], RS, QE[:], ALU.mult, ALU.mult, [q, QE], [QE])
        self.tt(v3(KE[:], 64), bcast_mid(B_[:, 63::64], 64), v3(B_[:], 64), ALU.subtract, [B_], [KE])
        self.act(KE[:], KE[:], AF.Exp, [KE], [KE])
        self.tt(KE[:], KE[:], KK[:], ALU.mult, [KE, KK], [KE])
        yield
        S_ = self.SB[h]
        m = self.m128b
        up_i = self.up_i
        for pr in range(4):
            s_ = slice(pr * 128, (pr + 1) * 128)
            ps = self.psum.next()
            self.tr(ps[:, 0:128], vi[:, s_], [vi], ps)
            vt = m.next()
            self.ecopy(vt[:], ps[:, 0:128], [ps], [vt], eng="act")
            ps = self.psum.next()
            self.tr(ps[:, 0:128], KE[:, s_], [KE], ps)
            ket = m.next()
            self.ecopy(ket[:], ps[:, 0:128], [ps], [ket], eng="dve")
            ps = self.psum.next()
            self.mm(ps[:, 0:128], KTl[:, s_], QT[:, s_], [KTl, QT], ps)
            AT = m.next()
            self.ts(AT[:], ps[:, 0:128], 1e30, ALU.min, [ps], [AT], s2=-1e30, op1=ALU.max)
            self.tt(AT[:], AT[:], up_i[:], ALU.mult, [AT, up_i], [AT])
            for c in range(2):
                r0 = 64 * c
                ch = pr * 2 + c
                cs = slice(ch * 64, ch * 64 + 64)
                pso = self.psum.next()
                self.mm(pso[:, 0:64], S_[:], QE[:, cs], [S_, QE], pso, start=True, stop=False)
                self.mm(pso[:, 0:64], vt[r0:r0 + 64, :], AT[r0:r0 + 64, r0:r0 + 64], [vt, AT], pso, start=False, stop=True)
                self.ecopy(OT[:, cs], pso[:, 0:64], [pso], [OT], eng="act")
                pss = self.psum.next()
                self.mm(pss[:, 0:128], ket[r0:r0 + 64, :], vt[r0:r0 + 64, :], [ket, vt], pss)
                self.stt(S_[:], S_[:], ebl[:, ch:ch + 1], pss[:, 0:128], ALU.mult, ALU.add, [S_, ebl, pss], [S_])
                yield
        self.gated_out(OT, X1, Z1, F_BZ + h, 1, 8 + h, b)

    def mixer_C(self, l, b):
        P = self.P
        t0 = b * 512
        sl = self.slots
        PROJa = self.a(self.PROJ)
        kmem, vmem, ident = self.kmem, self.vmem, self.ident
        m = self.m128
        SC = 256 ** -0.5
        for hc in range(4):
            cqs = []
            for j in range(2):
                row = F_CQ + 2 * hc + j
                st = sl[28 + j]
                P.dma(st[:], PROJa[row, :, PAD + t0:PAD + t0 + 512], reads=[self.PROJ_T[row][b]], writes=[st])
                cb = self.cqb[j]
                self.ecopy(cb[:], st[:], [st], [cb], eng="dve")
                cqs.append(cb)
            o0, o1 = sl[30], sl[31]
            for tq in range(4):
                s_ = slice(tq * 128, (tq + 1) * 128)
                ps = self.psum.next()
                for j in range(2):
                    self.mm(ps[:, 0:NMEM], cqs[j][:, s_], kmem[:, 2 * hc + j, :], [cqs[j], kmem], ps, start=(j == 0), stop=(j == 1))
                mx = self.small.next()
                P.op("dve", lambda e, mx=mx, ps=ps: e.tensor_reduce(out=mx[:, 0:1], in_=ps[:, 0:NMEM], axis=AX.X, op=ALU.max), [ps], [mx])
                self.ts(mx[:, 1:2], mx[:, 0:1], -SC, ALU.mult, [mx], [mx])
                pe_ = self.tp.next()
                self.act(pe_[:, 0:NMEM], ps[:, 0:NMEM], AF.Exp, [ps, mx], [pe_, mx], scale=SC, bias=mx[:, 1:2], accum_out=mx[:, 2:3])
                self.recip(mx[:, 3:4], mx[:, 2:3], [mx], [mx])
                self.ts(pe_[:, 0:NMEM], pe_[:, 0:NMEM], mx[:, 3:4], ALU.mult, [pe_, mx], [pe_])
                pTs = []
                for mt in range(2):
                    ps2 = self.psum.next()
                    self.tr(ps2[:, 0:128], pe_[:, mt * 128:(mt + 1) * 128], [pe_], ps2)
                    pT = self.pTb[mt]
                    self.ecopy(pT[:], ps2[:, 0:128], [ps2], [pT])
                    pTs.append(pT)
                for j, o in ((0, o0), (1, o1)):
                    ps3 = self.psum.next()
                    for mt in range(2):
                        self.mm(ps3[:, 0:128], vmem[:, mt, (2 * hc + j) * 128:(2 * hc + j + 1) * 128], pTs[mt][:], [vmem, pTs[mt]], ps3, start=(mt == 0), stop=(mt == 1))
                    self.ecopy(o[:, s_], ps3[:, 0:128], [ps3], [o])
            for j, o in ((0, o0), (1, o1)):
                row = 16 + 2 * hc + j
                P.dma(self.a(self.OD)[row, :, t0:t0 + 512], o[:], reads=[o], writes=[self.OD_T[row][b]], q="pq")

    def phase3(self, l, b, last):
        P = self.P
        t0 = b * 512
        sl = self.slots
        xts, ys = sl[0:16], sl[16:32]
        hb, big, gsb = self.hb, self.big, self.gsb
        XTv = self.a(self.XT).rearrange("(kt p) s -> p kt s", p=128)
        ODa, PROJa = self.a(self.OD), self.a(self.PROJ)
        w_br, w_out, w1, w2 = self.a(self.w_br), self.a(self.w_out), self.a(self.w1), self.a(self.w2)
        for br in range(3):
            stgs = []
            for kt in range(8):
                row = br * 8 + kt
                st = ys[kt]
                P.dma(st[:], ODa[row, :, t0:t0 + 512], reads=[self.OD_T[row][b]], writes=[st])
                self.ecopy(big[:, kt, :], st[:], [st], [big], eng=("dve" if kt % 2 else "act"))
            for f in range(16):
                wb = self.load_w(w_br[l, br, f], key=("br", br, f), b=b, wb_ap=self.a(self.wb_br)[br, f])
                ps = self.psum.next()
                for kt in range(8):
                    self.mm(ps[:], wb[:, kt, :], big[:, kt, :], [wb, big], ps, start=(kt == 0), stop=(kt == 7))
                g = self.tp.next()
                row = F_G + br * 16 + f
                P.dma(g[:], PROJa[row, :, PAD + t0:PAD + t0 + 512], reads=[self.PROJ_T[row][b]], writes=[g])
                acc = xts[f]
                if br == 0:
                    self.tt(acc[:], ps[:], g[:], ALU.mult, [ps, g], [acc])
                else:
                    self.tt(g[:], ps[:], g[:], ALU.mult, [ps, g], [g])
                    if br == 1:
                        self.tt(acc[:], acc[:], g[:], ALU.add, [acc, g], [acc])
                    else:
                        self.tt(hb[:, f, :], acc[:], g[:], ALU.add, [acc, g], [hb])
        for f in range(16):
            wb = self.load_w(w_out[l, f], key=("out", f), b=b, wb_ap=self.a(self.wb_out)[f])
            ps = self.psum.next()
            for kt in range(KT):
                self.mm(ps[:], wb[:, kt, :], hb[:, kt, :], [wb, hb], ps, start=(kt == 0), stop=(kt == KT - 1))
            self.ecopy(ys[f][:], ps[:], [ps], [ys[f]])
        P.dma(self.slotbuf[:, 0:16, :], XTv[:, :, t0:t0 + 512], reads=[self.XT_T[b]], writes=xts)
        t1, t2 = self.tp.next(), self.tp.next()
        r = self.rstd_from(lambda kt: ys[kt][:], ys, KT, 1.0 / D, [t1, t2])
        for kt in range(KT):
            self.stt(ys[kt][:], ys[kt][:], gsb[:, 1, kt:kt + 1], r[:], ALU.mult, ALU.mult, [ys[kt], gsb, r], [ys[kt]])
            self.tt(xts[kt][:], xts[kt][:], ys[kt][:], ALU.add, [xts[kt], ys[kt]], [xts[kt]])
        t1, t2 = self.tp.next(), self.tp.next()
        r = self.rstd_from(lambda kt: xts[kt][:], xts, KT, 1.0 / D, [t1, t2])
        for kt in range(KT):
            self.stt(hb[:, kt, :], xts[kt][:], gsb[:, 2, kt:kt + 1], r[:], ALU.mult, ALU.mult, [xts[kt], gsb, r], [hb])
        for qtr in range(4):
            for fl in range(16):
                f = qtr * 16 + fl
                wb = self.load_w(w1[l, f], key=("w1", f), b=b, wb_ap=self.a(self.wb_w1)[f])
                ps = self.psum.next()
                for kt in range(KT):
                    self.mm(ps[:], wb[:, kt, :], hb[:, kt, :], [wb, hb], ps, start=(kt == 0), stop=(kt == KT - 1))
                rl = self.tp.next()
                self.act(rl[:], ps[:], AF.Relu, [ps], [rl])
                self.tt(big[:, fl, :], rl[:], rl[:], ALU.mult, [rl], [big])
            for f in range(16):
                wb = self.load_w(w2[l, f, :, qtr * 16:(qtr + 1) * 16, :], key=("w2", f, qtr), b=b, wb_ap=self.a(self.wb_w2)[f, qtr])
                ps = self.psum.next()
                for kt in range(16):
                    self.mm(ps[:], wb[:, kt, :], big[:, kt, :], [wb, big], ps, start=(kt == 0), stop=(kt == 15))
                if qtr == 0:
                    self.ecopy(ys[f][:], ps[:], [ps], [ys[f]])
                else:
                    self.tt(ys[f][:], ys[f][:], ps[:], ALU.add, [ys[f], ps], [ys[f]])
        t1, t2 = self.tp.next(), self.tp.next()
        r = self.rstd_from(lambda kt: ys[kt][:], ys, KT, 1.0 / D, [t1, t2])
        for kt in range(KT):
            self.stt(ys[kt][:], ys[kt][:], gsb[:, 3, kt:kt + 1], r[:], ALU.mult, ALU.mult, [ys[kt], gsb, r], [ys[kt]])
            self.tt(xts[kt][:], xts[kt][:], ys[kt][:], ALU.add, [xts[kt], ys[kt]], [xts[kt]])
        if last:
            ov = self.a(self.outT).rearrange("(kt p) s -> p kt s", p=128)
            P.dma(ov[:, :, t0:t0 + 512], self.slotbuf[:, 0:16, :], reads=xts, writes=[self.OUT_T], q="pq")
        else:
            P.dma(XTv[:, :, t0:t0 + 512], self.slotbuf[:, 0:16, :], reads=xts, writes=[self.XT_T[b]], q="pq")

    def build(self, stop_after=None):
        P, L, NB = self.P, self.L, self.NB
        self.cqb = [P.sb([128, 512], BF16, f"cqb{j}") for j in range(2)]
        self.pTb = [P.sb([128, 128], BF16, f"pTb{j}") for j in range(2)]
        self.smtb = Rot([P.sb([16, 512], F32, f"smt{j}") for j in range(2)])
        import os
        stop = os.environ.get("KSTOP", "")
        order = ["prologue", "p1", "mem", "A", "B", "C", "p3"]
        lim = order.index(stop) if stop in order else 99
        self.prologue()
        for l in range(L):
            self.layer_params(l)
            if lim >= 1:
                for b0 in range(0, NB, 2):
                    self.phase1(l, b0, min(2, NB - b0))
            if lim >= 2:
                self.mem_prep(l)
            for b in range(NB):
                smt = self.smtb.next()
                P.dma(smt[0:16, :], self.a(self.PSM)[:, b * 512:(b + 1) * 512], reads=[self.PSM_T[b]], writes=[smt])
                for h in range(8):
                    gl = []
                    if lim >= 3:
                        gl.append(self.mixer_A(l, b, h, smt))
                    if lim >= 4:
                        gl.append(self.mixer_B(l, b, h))
                    for _ in rr(gl):
                        pass
                if lim >= 5:
                    self.mixer_C(l, b)
            if lim >= 6:
                for b in range(NB):
                    self.phase3(l, b, last=(l == L - 1))
        st = P.emit()
        P.close()
        return st

NCORES = 4
_CACHE = {}


def _tile_w(w, kt):
    K, F = w.shape
    return np.ascontiguousarray(w.reshape(kt, 128, F // 128, 128).transpose(2, 1, 0, 3))


def _fm_vec(g):
    return np.ascontiguousarray(g.reshape(-1, 128).T)


def prep_weights(inp, L):
    f32 = np.float32
    w_in = np.asarray(inp["w_in"], f32)
    cols_main = np.r_[0:4096, 4112:15376]
    out = {}
    out["w_in"] = np.stack([_tile_w(w_in[l][:, cols_main], 16) for l in range(L)])
    wsm = w_in[:, :, 4096:4112]
    out["w_sm"] = np.ascontiguousarray(wsm.reshape(L, 16, 128, 16).transpose(0, 2, 1, 3))
    out["w_kv"] = np.stack([_tile_w(np.asarray(inp["w_kv_mem"][l], f32), 16) for l in range(L)])
    out["w_br"] = np.stack([np.stack([_tile_w(np.asarray(inp[n][l], f32), 8) for n in ("w_br_a", "w_br_b", "w_br_c")]) for l in range(L)])
    out["w_out"] = np.stack([_tile_w(np.asarray(inp["w_out"][l], f32), 16) for l in range(L)])
    out["w1"] = np.stack([_tile_w(np.asarray(inp["w_mlp_in"][l], f32), 16) for l in range(L)])
    out["w2"] = np.stack([_tile_w(np.asarray(inp["w_mlp_out"][l], f32), 64) for l in range(L)])
    out["gains"] = np.stack([np.stack([_fm_vec(np.asarray(inp[n][l], f32)) for n in ("g_pre_mix", "g_post_mix", "g_pre_mlp", "g_post_mlp", "g_mem")], axis=1) for l in range(L)])
    cw = np.asarray(inp["conv_a"], f32)
    out["convw"] = np.ascontiguousarray(cw.reshape(L, 4, 24, 128).transpose(0, 3, 2, 1))
    hv = np.concatenate([np.asarray(inp["a_log"], f32), np.asarray(inp["dt_bias"], f32)], axis=1)
    out["hvec"] = np.ascontiguousarray(np.broadcast_to(hv[:, None, :], (L, 128, 16)))
    out["gnv"] = np.ascontiguousarray(np.stack([np.asarray(inp["gn_a"], f32), np.asarray(inp["gn_b"], f32)], axis=2))
    lb = np.asarray(inp["lb_raw"], f32)
    out["lbraw"] = np.ascontiguousarray(lb.reshape(L, 8, 128).transpose(2, 1, 0))
    return out


def kernel(**inp):
    x = np.asarray(inp["x"], np.float32)
    mem = np.asarray(inp["mem"], np.float32)
    B, S, _ = x.shape
    L = inp["w_in"].shape[0]
    key = (S, L)
    if key not in _CACHE:
        k = K(S, L)
        k.build()
        _CACHE[key] = k
    k = _CACHE[key]
    w = prep_weights(inp, L)
    in_maps = []
    for c in range(B):
        m = dict(w)
        m["xT"] = np.ascontiguousarray(x[c].T)
        m["memT"] = np.ascontiguousarray(mem[c].T)
        in_maps.append(m)
    res = run_bass_kernel_spmd(k.nc, in_maps, core_ids=list(range(B)))
    out = np.stack([np.ascontiguousarray(res.results[c]["outT"].T) for c in range(B)])
    return out.astype(np.float32)
```

```python
import numpy as np
import concourse.bass as bass
import concourse.mybir as mybir
from contextlib import ExitStack

F32 = mybir.dt.float32
BF16 = mybir.dt.bfloat16
I32 = mybir.dt.int32
AF = mybir.ActivationFunctionType
ALU = mybir.AluOpType
AX = mybir.AxisListType

COMPUTE = ("pe", "dve", "act", "pool")
DMAQ = ("sp", "pq")
NDMASEM = {"sp": 24, "pq": 12}


class T:
    __slots__ = ("t", "name", "w", "r", "rd")

    def __init__(self, t, name=""):
        self.t = t
        self.name = name
        self.w = None
        self.r = {}
        self.rd = []

    def __getitem__(self, idx):
        return self.t[idx]


class Ins:
    __slots__ = ("eng", "fn", "inc", "idx", "deps", "dma_sem", "dma_val", "ticket")

    def __init__(self, eng, fn, inc):
        self.eng = eng
        self.fn = fn
        self.inc = inc
        self.deps = []
        self.dma_sem = None
        self.dma_val = 0
        self.ticket = None


class Prog:
    def __init__(self, nc):
        self.nc = nc
        self.stack = ExitStack()
        self.all = []
        self.per = {e: [] for e in COMPUTE + DMAQ}
        self.nalloc = 0

    def sb(self, shape, dt=F32, name=None):
        self.nalloc += 1
        name = name or f"sb{self.nalloc}"
        t = self.stack.enter_context(self.nc.sbuf_tensor(name, list(shape), dt))
        return T(t, name)

    def ps(self, shape, dt=F32, name=None):
        self.nalloc += 1
        name = name or f"ps{self.nalloc}"
        t = self.stack.enter_context(self.nc.psum_tensor(name, list(shape), dt))
        return T(t, name)

    def dram(self, name, shape, dt=F32, kind="Internal"):
        t = self.nc.dram_tensor(name, list(shape), dt, kind=kind)
        return T(t, name)

    def view(self, t, name=""):
        return T(t.t if isinstance(t, T) else t, name)

    def op(self, eng, fn, reads=(), writes=(), inc=True):
        ins = Ins(eng, fn, inc)
        isdma = eng in DMAQ
        deps = {}

        def add(d, kind):
            if d is None or d is ins:
                return
            if d.eng == eng and not isdma:
                if kind != "raw" or eng == "pe":
                    return
            deps[id(d)] = d

        for t in reads:
            add(t.w, "raw")
        for t in writes:
            add(t.w, "waw")
            for r in t.r.values():
                add(r, "war")
            for r in t.rd:
                add(r, "war")
        for t in reads:
            if isdma:
                t.rd.append(ins)
            else:
                t.r[eng] = ins
        for t in writes:
            t.w = ins
            t.r = {}
            t.rd = []
        ins.deps = list(deps.values())
        ins.idx = len(self.per[eng])
        self.per[eng].append(ins)
        self.all.append(ins)
        return ins

    def dma(self, out_ap, in_ap, reads=(), writes=(), q="sp", **kw):
        return self.op(q, lambda e: e.dma_start(out=out_ap, in_=in_ap, **kw), reads, writes)

    def emit(self):
        nc = self.nc
        engobj = {"pe": nc.tensor, "dve": nc.vector, "act": nc.scalar, "pool": nc.gpsimd,
                  "sp": nc.sync, "pq": nc.gpsimd}
        sems = {e: self.stack.enter_context(nc.semaphore(f"s_{e}")) for e in COMPUTE}
        dsem = {q: [self.stack.enter_context(nc.semaphore(f"d_{q}{i}")) for i in range(NDMASEM[q])]
                for q in DMAQ}
        for e in COMPUTE:
            if self.per[e]:
                self.per[e][-1].inc = True
        for e in COMPUTE:
            c = 0
            lst = self.per[e]
            for ins in lst:
                if ins.inc:
                    c += 1
                    ins.ticket = c
            nxt = None
            for ins in reversed(lst):
                if ins.inc:
                    nxt = ins.ticket
                else:
                    ins.ticket = nxt
        dcount = {q: [0] * NDMASEM[q] for q in DMAQ}
        dprev = {q: [None] * NDMASEM[q] for q in DMAQ}
        for q in DMAQ:
            for i, ins in enumerate(self.per[q]):
                k = i % NDMASEM[q]
                dcount[q][k] += 16
                ins.dma_sem = dsem[q][k]
                ins.dma_val = dcount[q][k]
                ins.ticket = (q, k)
        issue_eng = {"pe": "pe", "dve": "dve", "act": "act", "pool": "pool", "sp": "sp", "pq": "pool"}
        seen = {e: {} for e in ("pe", "dve", "act", "pool", "sp")}
        nwait = 0
        for ins in self.all:
            ie = issue_eng[ins.eng]
            eo = engobj[ins.eng]
            sn = seen[ie]
            waits = []
            for d in ins.deps:
                if d.eng in DMAQ:
                    waits.append((d.dma_sem, d.dma_val))
                else:
                    waits.append((sems[d.eng], d.ticket))
            if ins.eng in DMAQ:
                q, k = ins.ticket
                prev = ins.dma_val - 16
                if prev > 0:
                    waits.append((ins.dma_sem, prev))
            for s, v in waits:
                key = id(s)
                if sn.get(key, 0) >= v:
                    continue
                sn[key] = v
                eo.wait_ge(s, v)
                nwait += 1
            r = ins.fn(eo)
            if ins.eng in DMAQ:
                r.then_inc(ins.dma_sem, 16)
            elif ins.inc:
                r.then_inc(sems[ins.eng], 1)
        for q in DMAQ:
            for k in range(NDMASEM[q]):
                if dcount[q][k] > 0:
                    nc.sync.wait_ge(dsem[q][k], dcount[q][k])
        for e in COMPUTE:
            lst = self.per[e]
            if lst:
                nc.sync.wait_ge(sems[e], lst[-1].ticket)
        self.stats = {e: len(self.per[e]) for e in self.per}
        self.stats["waits"] = nwait
        return self.stats

    def close(self):
        self.stack.close()
from concourse.bass_utils import run_bass_kernel_spmd

D = 2048
KT = 16
NMEM = 256
EPS = 1e-6
F_AQ, F_AK, F_AV, F_AZ, F_BQ, F_BF, F_BI, F_BZ, F_CQ, F_G = 0, 8, 16, 24, 32, 40, 48, 56, 64, 72
PAD = 3
RS = 128 ** -0.5


def bcast_mid(ap2, n_inner):
    a = ap2.ap
    return bass.AP(ap2.tensor, ap2.offset, [list(a[0]), list(a[1]), [0, n_inner]])


def bcast_outer(ap2, n_outer):
    a = ap2.ap
    return bass.AP(ap2.tensor, ap2.offset, [list(a[0]), [0, n_outer], list(a[1])])


def v3(ap, inner):
    return ap.rearrange("p (a b) -> p a b", b=inner)


def rr(gens):
    gens = list(gens)
    while gens:
        for g in list(gens):
            try:
                next(g)
                yield
            except StopIteration:
                gens.remove(g)


def rrw(main, side, ratio):
    main_done = False
    side_done = side is None
    while not (main_done and side_done):
        if not main_done:
            for _ in range(ratio):
                try:
                    next(main)
                except StopIteration:
                    main_done = True
                    break
        if not side_done:
            try:
                next(side)
            except StopIteration:
                side_done = True
        yield


class Rot:
    def __init__(self, items):
        self.items = items
        self.i = 0

    def next(self):
        x = self.items[self.i % len(self.items)]
        self.i += 1
        return x


class K:
    def __init__(self, S, L):
        self.S, self.L = S, L
        self.NB = S // 512
        nc = bass.Bass("TRN2", target_bir_lowering=False)
        self.nc = nc
        P = self.P = Prog(nc)
        NB = self.NB
        ext = lambda n, s: P.dram(n, s, F32, kind="ExternalInput")
        self.xT_in = ext("xT", [D, S])
        self.memT = ext("memT", [D, NMEM])
        self.w_in = ext("w_in", [L, 120, 128, KT, 128])
        self.w_sm = ext("w_sm", [L, 128, KT, 16])
        self.w_kv = ext("w_kv", [L, 16, 128, KT, 128])
        self.w_br = ext("w_br", [L, 3, 16, 128, 8, 128])
        self.w_out = ext("w_out", [L, 16, 128, KT, 128])
        self.w1 = ext("w1", [L, 64, 128, KT, 128])
        self.w2 = ext("w2", [L, 16, 128, 64, 128])
        self.gains = ext("gains", [L, 128, 5, KT])
        self.convw = ext("convw", [L, 128, 24, 4])
        self.hvec = ext("hvec", [L, 128, 16])
        self.gnv = ext("gnv", [L, 128, 2])
        self.lbraw = ext("lbraw", [128, 8, L])
        self.outT = P.dram("outT", [D, S], F32, kind="ExternalOutput")
        self.XT = P.dram("XT", [D, S], F32)
        self.PROJ = P.dram("PROJ", [120, 128, S + PAD], F32)
        self.PSM = P.dram("PSM", [16, S], F32)
        self.OD = P.dram("OD", [24, 128, S], F32)
        self.wb_in = P.dram("wb_in", [120, 128, KT, 128], BF16)
        self.wb_br = P.dram("wb_br", [3, 16, 128, 8, 128], BF16)
        self.wb_out = P.dram("wb_out", [16, 128, KT, 128], BF16)
        self.wb_w1 = P.dram("wb_w1", [64, 128, KT, 128], BF16)
        self.wb_w2 = P.dram("wb_w2", [16, 4, 128, KT, 128], BF16)
        self.WB_T = {}
        self.XT_T = [T(None, f"XT{b}") for b in range(NB)]
        self.PROJ_T = [[T(None, f"PJ{f}_{b}") for b in range(NB)] for f in range(120)]
        self.PSM_T = [T(None, f"PSM{b}") for b in range(NB)]
        self.OD_T = [[T(None, f"OD{f}_{b}") for b in range(NB)] for f in range(24)]
        self.WT_ = T(None, "weights")
        self.OUT_T = T(None, "out")

        self.ones = P.sb([128, 128], F32, "ones")
        self.ident = P.sb([128, 128], F32, "ident")
        self.nlow_s = P.sb([128, 128], F32, "nlow_s")
        self.nup_s = P.sb([128, 128], F32, "nup_s")
        self.up_i = P.sb([128, 128], F32, "up_i")
        self.resetm = P.sb([128, 512], F32, "resetm")
        self.zero3 = P.sb([128, 4], F32, "zero3")
        self.epsb = P.sb([128, 1], F32, "epsb")
        self.oneb = P.sb([128, 1], F32, "oneb")
        ones, ident, nlow_s, nup_s, up_i, resetm, zero3 = self.ones, self.ident, self.nlow_s, self.nup_s, self.up_i, self.resetm, self.zero3
        P.op("pool", lambda e: e.memset(ones[:], 1.0), [], [ones])
        P.op("pool", lambda e: e.memset(zero3[:], 0.0), [], [zero3])
        P.op("pool", lambda e: e.memset(self.epsb[:], EPS), [], [self.epsb])
        P.op("pool", lambda e: e.memset(self.oneb[:], 1.0), [], [self.oneb])
        P.op("pool", lambda e: e.memset(resetm[:], 1.0), [], [resetm])
        P.op("pool", lambda e: e.memset(resetm[:, ::64], 0.0), [resetm], [resetm])
        P.op("pool", lambda e: e.affine_select(out=ident[:], in_=ones[:], pattern=[[-1, 128]], compare_op=ALU.is_equal, fill=0.0, base=0, channel_multiplier=1), [ones], [ident])
        P.op("pool", lambda e: e.memset(nlow_s[:], -1.0), [], [nlow_s])
        P.op("pool", lambda e: e.memset(nup_s[:], -1.0), [], [nup_s])
        P.op("pool", lambda e: e.affine_select(out=nlow_s[:], in_=nlow_s[:], pattern=[[-1, 128]], compare_op=ALU.is_ge, fill=0.0, base=-1, channel_multiplier=1), [nlow_s], [nlow_s])
        P.op("pool", lambda e: e.memset(nlow_s[64:128, 0:64], 0.0), [nlow_s], [nlow_s])
        P.op("pool", lambda e: e.affine_select(out=nup_s[:], in_=nup_s[:], pattern=[[1, 128]], compare_op=ALU.is_ge, fill=0.0, base=-1, channel_multiplier=-1), [nup_s], [nup_s])
        P.op("pool", lambda e: e.memset(nup_s[0:64, 64:128], 0.0), [nup_s], [nup_s])
        P.op("pool", lambda e: e.affine_select(out=up_i[:], in_=ones[:], pattern=[[1, 128]], compare_op=ALU.is_ge, fill=0.0, base=0, channel_multiplier=-1), [ones], [up_i])
        P.op("pool", lambda e: e.memset(up_i[0:64, 64:128], 0.0), [up_i], [up_i])

        self.psum = Rot([P.ps([128, 512], F32, f"psb{i}") for i in range(8)])
        self.wstage = Rot([P.sb([128, 8, 128], F32, f"wst{i}") for i in range(2)])
        self.xc = Rot([P.sb([128, 512], F32, f"xc{i}") for i in range(4)])
        self.r1 = P.sb([128, 512], F32, "r1")
        self.wbf = Rot([P.sb([128, 16, 128], BF16, f"wbf{i}") for i in range(4)])
        self.slotbuf = P.sb([128, 32, 512], F32, "slots")
        self.slots = [T(self.slotbuf[:, i, :], f"slot{i}") for i in range(32)]
        self.hb = P.sb([128, KT, 512], BF16, "hb")
        self.big = P.sb([128, KT, 512], BF16, "big")
        self.stg = Rot([P.sb([128, 512], F32, f"stg{i}") for i in range(4)])
        self.tp = Rot([P.sb([128, 512], F32, f"tp{i}") for i in range(5)])
        self.m128 = Rot([P.sb([128, 128], F32, f"m{i}") for i in range(35)])
        self.m128b = Rot([P.sb([128, 128], F32, f"mb{i}") for i in range(10)])
        self.small = Rot([P.sb([128, 8], F32, f"sm{i}") for i in range(32)])
        self.raw = Rot([P.sb([128, 516], F32, f"raw{i}") for i in range(3)])
        self.kmem = P.sb([128, 8, NMEM], BF16, "kmem")
        self.vmem = P.sb([128, 2, 1024], BF16, "vmem")
        self.gsb = P.sb([128, 5, KT], F32, "gsb")
        self.cwsb = P.sb([128, 24, 4], F32, "cwsb")
        self.hv = P.sb([128, 16], F32, "hv")
        self.negA = P.sb([128, 8], F32, "negA")
        self.gn = P.sb([128, 2], F32, "gn")
        self.lbr = P.sb([128, 8, L], F32, "lbr")
        self.lbe = P.sb([128, 8, L], F32, "lbe")
        self.lbs = P.sb([128, 8], F32, "lbs")
        self.lball = P.sb([128, L, 8], F32, "lball")
        self.omlb = P.sb([128, L, 8], F32, "omlb")
        self.wsmst = P.sb([128, KT, 16], F32, "wsmst")
        self.wsmb = P.sb([128, KT, 16], BF16, "wsmb")
        self.SA = [P.sb([128, 128], F32, f"SA{h}") for h in range(8)]
        self.SB = [P.sb([128, 128], F32, f"SB{h}") for h in range(8)]
        self.evq = 0

    def a(self, t):
        return t.t.ap()

    def mm(self, ps_ap, lhsT, rhs, reads, ps_t, start=True, stop=True, inc=None):
        return self.P.op("pe", lambda e: e.matmul(ps_ap, lhsT, rhs, start=start, stop=stop), reads, [ps_t], inc=(stop if inc is None else inc))

    def tr(self, ps_ap, in_ap, reads, ps_t):
        ident = self.ident
        return self.P.op("pe", lambda e: e.transpose(ps_ap, in_ap, ident[:]), reads + [ident], [ps_t], inc=True)

    def act(self, out, in_, func, reads, writes, **kw):
        return self.P.op("act", lambda e: e.activation(out=out, in_=in_, func=func, **kw), reads, writes)

    def ecopy(self, out, in_, reads, writes, eng=None):
        self.evq += 1
        if eng is None:
            eng = "act" if self.evq % 2 else "dve"
        if eng == "act":
            return self.P.op("act", lambda e: e.copy(out=out, in_=in_), reads, writes)
        return self.P.op("dve", lambda e: e.tensor_copy(out=out, in_=in_), reads, writes)

    def tt(self, out, in0, in1, op, reads, writes):
        return self.P.op("dve", lambda e: e.tensor_tensor(out=out, in0=in0, in1=in1, op=op), reads, writes)

    def ts(self, out, in0, s1, op0, reads, writes, s2=None, op1=None):
        if op1 is None:
            return self.P.op("dve", lambda e: e.tensor_scalar(out=out, in0=in0, scalar1=s1, scalar2=None, op0=op0), reads, writes)
        return self.P.op("dve", lambda e: e.tensor_scalar(out=out, in0=in0, scalar1=s1, scalar2=s2, op0=op0, op1=op1), reads, writes)

    def stt(self, out, in0, scalar, in1, op0, op1, reads, writes):
        return self.P.op("dve", lambda e: e.scalar_tensor_tensor(out=out, in0=in0, scalar=scalar, in1=in1, op0=op0, op1=op1), reads, writes)

    def recip(self, out, in_, reads, writes):
        return self.P.op("dve", lambda e: e.reciprocal(out=out, in_=in_), reads, writes)

    def rstd_from(self, src_fn, src_reads, nkt, scale, sqs, n=512):
        ps = self.psum.next()
        for kt in range(nkt):
            sq = sqs[kt % 2] if nkt > 1 else sqs[0]
            self.act(sq[:, 0:n], src_fn(kt), AF.Square, src_reads, [sq])
            self.mm(ps[:, 0:n], self.ones[:], sq[:, 0:n], [self.ones, sq], ps, start=(kt == 0), stop=(kt == nkt - 1), inc=True)
        r = sqs[0]
        self.act(r[:, 0:n], ps[:, 0:n], AF.Sqrt, [ps, self.epsb], [r], scale=scale, bias=self.epsb[:])
        self.recip(r[:, 0:n], r[:, 0:n], [r], [r])
        return r

    def load_w(self, w_ap, key=None, b=0, wb_ap=None):
        k = w_ap.shape[1]
        wb = self.wbf.next()
        if key is not None:
            wt = self.WB_T.setdefault(key, T(None, str(key)))
            if b > 0:
                self.P.dma(wb[:, 0:k, :], wb_ap, reads=[wt], writes=[wb])
                return wb
        for k0 in range(0, k, 8):
            st = self.wstage.next()
            self.P.dma(st[:, 0:8, :], w_ap[:, k0:k0 + 8, :], reads=[self.WT_], writes=[st])
            self.ecopy(wb[:, k0:k0 + 8, :], st[:, 0:8, :], [st], [wb])
        if key is not None:
            self.P.dma(wb_ap, wb[:, 0:k, :], reads=[wb], writes=[wt], q="pq")
        return wb

    def diag_cols(self, src, scratch):
        ident = self.ident
        self.tt(v3(scratch[:], 128), v3(src[:], 128), bcast_outer(ident[:, :], 4), ALU.mult, [src, ident], [scratch])
        c = self.small.next()
        self.P.op("dve", lambda e: e.tensor_reduce(out=c[:, 0:4], in_=v3(scratch[:], 128), axis=AX.X, op=ALU.add), [scratch], [c])
        return c

    def prologue(self):
        P, L, NB = self.P, self.L, self.NB
        PROJa, XTa, xTa = self.a(self.PROJ), self.a(self.XT), self.a(self.xT_in)
        for f in range(24):
            P.dma(PROJa[f, :, 0:PAD], self.zero3[:, 0:PAD], reads=[self.zero3], writes=[self.PROJ_T[f][0]], q="pq")
        for b in range(NB):
            P.dma(XTa[:, b * 512:(b + 1) * 512], xTa[:, b * 512:(b + 1) * 512], reads=[self.WT_], writes=[self.XT_T[b]], q="pq")
        lbr, lbe, lbs, lball, omlb = self.lbr, self.lbe, self.lbs, self.lball, self.omlb
        P.dma(lbr[:], self.a(self.lbraw), reads=[self.WT_], writes=[lbr])
        self.act(lbe[:], lbr[:], AF.Exp, [lbr], [lbe])
        P.op("dve", lambda e: e.tensor_reduce(out=lbs[:], in_=lbe[:], axis=AX.X, op=ALU.add), [lbe], [lbs])
        self.recip(lbs[:], lbs[:], [lbs], [lbs])
        P.op("dve", lambda e: e.memset(lball[:, 0, :], 0.0), [], [lball])
        for l in range(1, L):
            tl = self.small.next()
            self.tt(tl[:], lbe[:, :, l], lbs[:], ALU.mult, [lbe, lbs], [tl])
            self.tt(lball[:, l, :], lball[:, l - 1, :], tl[:], ALU.add, [lball, tl], [lball])
        self.ts(omlb[:], lball[:], -1.0, ALU.mult, [lball], [omlb], s2=1.0, op1=ALU.add)

    def layer_params(self, l):
        P = self.P
        P.dma(self.gsb[:], self.a(self.gains)[l], reads=[self.WT_], writes=[self.gsb])
        P.dma(self.cwsb[:], self.a(self.convw)[l], reads=[self.WT_], writes=[self.cwsb])
        P.dma(self.hv[:], self.a(self.hvec)[l], reads=[self.WT_], writes=[self.hv])
        P.dma(self.gn[:], self.a(self.gnv)[l], reads=[self.WT_], writes=[self.gn])
        P.dma(self.wsmst[:], self.a(self.w_sm)[l], reads=[self.WT_], writes=[self.wsmst])
        wsmb, wsmst, negA, hv = self.wsmb, self.wsmst, self.negA, self.hv
        P.op("pool", lambda e: e.tensor_copy(out=wsmb[:], in_=wsmst[:]), [wsmst], [wsmb])
        self.act(negA[:], hv[:, 0:8], AF.Exp, [hv], [negA])
        self.ts(negA[:], negA[:], -1.0, ALU.mult, [negA], [negA])
        for h in range(8):
            P.op("pool", lambda e, h=h: e.memset(self.SA[h][:], 0.0), [], [self.SA[h]])
            P.op("pool", lambda e, h=h: e.memset(self.SB[h][:], 0.0), [], [self.SB[h]])

    def phase1(self, l, b0, nb):
        P = self.P
        gsb = self.gsb
        acts = [self.hb, self.big][:nb]
        XTa = self.a(self.XT)
        r1 = self.r1
        for i in range(nb):
            b = b0 + i
            hb = acts[i]
            cs = slice(b * 512, (b + 1) * 512)
            ps = self.psum.next()
            for kt in range(KT):
                xc = self.xc.next()
                P.dma(xc[:], XTa[kt * 128:(kt + 1) * 128, cs], reads=[self.XT_T[b]], writes=[xc])
                self.act(xc[:], xc[:], AF.Square, [xc], [xc])
                self.mm(ps[:], self.ones[:], xc[:], [self.ones, xc], ps, start=(kt == 0), stop=(kt == KT - 1), inc=True)
            self.act(r1[:], ps[:], AF.Sqrt, [ps, self.epsb], [r1], scale=1.0 / D, bias=self.epsb[:])
            self.recip(r1[:], r1[:], [r1], [r1])
            yield
            for kt in range(KT):
                xc = self.xc.next()
                P.dma(xc[:], XTa[kt * 128:(kt + 1) * 128, cs], reads=[self.XT_T[b]], writes=[xc])
                self.stt(hb[:, kt, :], xc[:], gsb[:, 0, kt:kt + 1], r1[:], ALU.mult, ALU.mult, [xc, gsb, r1], [hb])
                if kt % 4 == 3:
                    yield
        PROJa = self.a(self.PROJ)
        w_in = self.a(self.w_in)
        wb_in = self.a(self.wb_in)
        for f in range(120):
            wb = self.load_w(w_in[l, f], key=("in", f), b=b0, wb_ap=wb_in[f])
            for i in range(nb):
                b = b0 + i
                hb = acts[i]
                ps = self.psum.next()
                for kt in range(KT):
                    self.mm(ps[:], wb[:, kt, :], hb[:, kt, :], [wb, hb], ps, start=(kt == 0), stop=(kt == KT - 1))
                s = self.stg.next()
                if F_AZ <= f < F_BF or F_BZ <= f < F_CQ:
                    self.act(s[:], ps[:], AF.Silu, [ps], [s])
                elif f >= F_G:
                    self.act(s[:], ps[:], AF.Sigmoid, [ps], [s])
                else:
                    self.ecopy(s[:], ps[:], [ps], [s], eng="dve")
                P.dma(PROJa[f, :, PAD + b * 512: PAD + (b + 1) * 512], s[:], reads=[s], writes=[self.PROJ_T[f][b]], q="pq")
            yield
        wsmb = self.wsmb
        for i in range(nb):
            b = b0 + i
            hb = acts[i]
            ps = self.psum.next()
            for kt in range(KT):
                self.mm(ps[0:16, :], wsmb[:, kt, :], hb[:, kt, :], [wsmb, hb], ps, start=(kt == 0), stop=(kt == KT - 1))
            s = self.stg.next()
            self.ecopy(s[0:16, :], ps[0:16, :], [ps], [s], eng="dve")
            P.dma(self.a(self.PSM)[:, b * 512:(b + 1) * 512], s[0:16, :], reads=[s], writes=[self.PSM_T[b]], q="pq")

    def mem_prep(self, l):
        P = self.P
        sl = self.slots
        mx = sl[0:16]
        gsb, hm = self.gsb, self.big
        memv = self.a(self.memT).rearrange("(kt p) s -> p kt s", p=128)
        P.dma(self.slotbuf[:, 0:16, 0:NMEM], memv, reads=[self.WT_], writes=mx)
        r = self.rstd_from(lambda kt: mx[kt][:, 0:NMEM], mx, KT, 1.0 / D, [sl[16], sl[17]], n=NMEM)
        for kt in range(KT):
            self.stt(hm[:, kt, 0:NMEM], mx[kt][:, 0:NMEM], gsb[:, 4, kt:kt + 1], r[:, 0:NMEM], ALU.mult, ALU.mult, [mx[kt], gsb, r], [hm])
        w_kv = self.a(self.w_kv)
        kmem, vmem = self.kmem, self.vmem
        for f in range(16):
            wb = self.load_w(w_kv[l, f])
            if f < 8:
                ps = self.psum.next()
                for kt in range(KT):
                    self.mm(ps[:, 0:NMEM], wb[:, kt, :], hm[:, kt, 0:NMEM], [wb, hm], ps, start=(kt == 0), stop=(kt == KT - 1))
                self.ecopy(kmem[:, f, :], ps[:, 0:NMEM], [ps], [kmem])
            else:
                for mt in range(2):
                    ps = self.psum.next()
                    for kt in range(KT):
                        self.mm(ps[:, 0:128], hm[:, kt, mt * 128:(mt + 1) * 128], wb[:, kt, :], [wb, hm], ps, start=(kt == 0), stop=(kt == KT - 1))
                    self.ecopy(vmem[:, mt, (f - 8) * 128:(f - 7) * 128], ps[:, 0:128], [ps], [vmem])

    def gated_out(self, oT, X1, Z1, zrow, gcol, orow, b):
        P = self.P
        t0 = b * 512
        r = self.rstd_from(lambda kt: oT[:], [oT], 1, 1.0 / 128, [X1])
        P.dma(Z1[:], self.a(self.PROJ)[zrow, :, PAD + t0:PAD + t0 + 512], reads=[self.PROJ_T[zrow][b]], writes=[Z1])
        self.stt(oT[:], oT[:], self.gn[:, gcol:gcol + 1], r[:], ALU.mult, ALU.mult, [oT, self.gn, r], [oT])
        self.tt(oT[:], oT[:], Z1[:], ALU.mult, [oT, Z1], [oT])
        P.dma(self.a(self.OD)[orow, :, t0:t0 + 512], oT[:], reads=[oT], writes=[self.OD_T[orow][b]], q="pq")

    def mixer_A(self, l, b, h, smt):
        P = self.P
        t0 = b * 512
        sl = self.slots
        qc, kc, vc, X1, B1, L1, G1, EG, RK, QD, C1, NBU, E1, OT, Z1 = sl[0:15]
        import os
        self.ka = float(os.environ.get('KA', '99'))
        ones, ident, cw = self.ones, self.ident, self.cwsb
        PROJa = self.a(self.PROJ)
        for f0, c in ((F_AQ, qc), (F_AK, kc), (F_AV, vc)):
            f = f0 + h
            raw = self.raw.next()
            rd = [self.PROJ_T[f][b]] + ([self.PROJ_T[f][b - 1]] if b > 0 else [])
            P.dma(raw[:, 0:515], PROJa[f, :, t0:t0 + 515], reads=rd, writes=[raw])
            self.ts(c[:], raw[:, 0:512], cw[:, f, 0:1], ALU.mult, [raw, cw], [c])
            for k in range(1, 4):
                self.stt(c[:], raw[:, k:k + 512], cw[:, f, k:k + 1], c[:], ALU.mult, ALU.add, [raw, cw, c], [c])
            self.act(c[:], c[:], AF.Silu, [c], [c])
        if self.ka < 2:
            return
        for c, mul in ((qc, RS), (kc, 1.0)):
            r = self.rstd_from(lambda kt, c=c: c[:], [c], 1, 1.0, [X1])
            self.stt(c[:], c[:], mul, r[:], ALU.mult, ALU.mult, [c, r], [c])
        if self.ka < 3:
            return
        qn, kn = qc, kc
        hv, negA = self.hv, self.negA
        sm1 = self.tp.next()
        self.ts(sm1[0:16, :], smt[0:16, :], ident[0:16, h:h + 1], ALU.mult, [smt, ident], [sm1])
        ps = self.psum.next()
        self.mm(ps[:], ones[0:16, :], sm1[0:16, :], [ones, sm1], ps)
        self.act(B1[:], ps[:], AF.Sigmoid, [ps], [B1])
        sm2 = self.tp.next()
        self.ts(sm2[0:16, :], smt[0:16, :], ident[0:16, 8 + h:9 + h], ALU.mult, [smt, ident], [sm2])
        ps = self.psum.next()
        self.mm(ps[:], ones[0:16, :], sm2[0:16, :], [ones, sm2], ps)
        self.act(L1[:], ps[:], AF.Exp, [ps, hv], [L1], bias=hv[:, 8 + h:9 + h])
        self.act(L1[:], L1[:], AF.Ln, [L1, self.oneb], [L1], bias=self.oneb[:])
        self.ts(L1[:], L1[:], negA[:, h:h + 1], ALU.mult, [L1, negA], [L1])
        if self.ka < 4:
            return
        resetm = self.resetm
        P.op("dve", lambda e: e.tensor_tensor_scan(out=G1[:], data0=resetm[:], data1=L1[:], initial=0.0, op0=ALU.mult, op1=ALU.add), [resetm, L1], [G1])
        self.act(EG[:], G1[:], AF.Exp, [G1], [EG])
        egl = self.small.next()
        self.act(egl[:, 0:8], G1[:, 63::64], AF.Exp, [G1], [egl])
        self.tt(v3(RK[:], 64), bcast_mid(G1[:, 63::64], 64), v3(G1[:], 64), ALU.subtract, [G1], [RK])
        self.act(RK[:], RK[:], AF.Exp, [RK], [RK])
        if self.ka < 5:
            return
        self.tt(QD[:], qn[:], EG[:], ALU.mult, [qn, EG], [QD])
        self.tt(C1[:], B1[:], EG[:], ALU.mult, [B1, EG], [C1])
        gamcol = self.diag_cols(G1, X1)
        ngamcol = self.small.next()
        self.ts(ngamcol[:, 0:4], gamcol[:, 0:4], -1.0, ALU.mult, [gamcol], [ngamcol])
        betacol = self.diag_cols(B1, X1)
        c1col = self.diag_cols(C1, X1)
        rkcol = self.diag_cols(RK, X1)
        nup_s, nlow_s, up_i = self.nup_s, self.nlow_s, self.up_i
        self.tt(v3(NBU[:], 128), v3(B1[:], 128), bcast_outer(nup_s[:, :], 4), ALU.mult, [B1, nup_s], [NBU])
        if self.ka < 6:
            return
        for pr in range(4):
            s_ = slice(pr * 128, (pr + 1) * 128)
            self.act(E1[:, s_], G1[:, s_], AF.Exp, [G1, gamcol], [E1], scale=-1.0, bias=gamcol[:, pr:pr + 1])
            self.act(X1[:, s_], G1[:, s_], AF.Exp, [G1, ngamcol], [X1], scale=1.0, bias=ngamcol[:, pr:pr + 1])
        self.tt(E1[:], E1[:], X1[:], ALU.min, [E1, X1], [E1])
        if self.ka < 7:
            return
        if self.ka < 7.1:
            return
        yield
        S_ = self.SA[h]
        m = self.m128
        res = {}

        def prep(pr):
            s_ = slice(pr * 128, (pr + 1) * 128)
            ps = self.psum.next()
            self.mm(ps[:, 0:128], kn[:, s_], kn[:, s_], [kn], ps)
            M1 = m.next()
            self.tt(M1[:], ps[:, 0:128], E1[:, s_], ALU.mult, [ps, E1], [M1])
            Pm, Qm, Tt = m.next(), m.next(), m.next()
            self.stt(Pm[:], M1[:], betacol[:, pr:pr + 1], nlow_s[:], ALU.mult, ALU.mult, [M1, betacol, nlow_s], [Pm])
            self.tt(Qm[:], M1[:], NBU[:, s_], ALU.mult, [M1, NBU], [Qm])
            self.tt(Tt[:], Qm[:], ident[:], ALU.add, [Qm, ident], [Tt])
            yield
            for k in range(1, 6):
                ps = self.psum.next()
                self.mm(ps[:, 0:128], Qm[:], Pm[:], [Qm, Pm], ps)
                if k < 5:
                    psq = self.psum.next()
                    self.mm(psq[:, 0:128], Pm[:], Qm[:], [Qm, Pm], psq)
                Pn = m.next()
                self.ecopy(Pn[:], ps[:, 0:128], [ps], [Pn], eng="act")
                if k < 5:
                    Qn = m.next()
                    self.ecopy(Qn[:], psq[:, 0:128], [psq], [Qn], eng="dve")
                yield
                ps2 = self.psum.next()
                self.mm(ps2[:, 0:128], Pn[:], Tt[:], [Pn, Tt], ps2)
                Tn = m.next()
                self.tt(Tn[:], ps2[:, 0:128], Tt[:], ALU.add, [ps2, Tt], [Tn])
                Pm, Tt = Pn, Tn
                if k < 5:
                    Qm = Qn
                yield
            ps = self.psum.next()
            self.tr(ps[:, 0:128], kn[:, s_], [kn], ps)
            kbg, kdt = m.next(), m.next()
            self.ts(kbg[:], ps[:, 0:128], c1col[:, pr:pr + 1], ALU.mult, [ps, c1col], [kbg])
            self.ts(kdt[:], ps[:, 0:128], rkcol[:, pr:pr + 1], ALU.mult, [ps, rkcol], [kdt])
            ps = self.psum.next()
            self.tr(ps[:, 0:128], vc[:, s_], [vc], ps)
            vb = m.next()
            self.ts(vb[:], ps[:, 0:128], betacol[:, pr:pr + 1], ALU.mult, [ps, betacol], [vb])
            yield
            ps = self.psum.next()
            self.mm(ps[:, 0:128], Tt[:], vb[:], [Tt, vb], ps)
            U = m.next()
            self.ecopy(U[:], ps[:, 0:128], [ps], [U], eng="act")
            ps = self.psum.next()
            self.mm(ps[:, 0:128], kbg[:], Tt[:], [kbg, Tt], ps)
            WTt = m.next()
            self.ecopy(WTt[:], ps[:, 0:128], [ps], [WTt], eng="dve")
            yield
            ps = self.psum.next()
            self.mm(ps[:, 0:128], kn[:, s_], qn[:, s_], [kn, qn], ps)
            AT = m.next()
            self.tt(AT[:], ps[:, 0:128], E1[:, s_], ALU.mult, [ps, E1], [AT])
            self.tt(AT[:], AT[:], up_i[:], ALU.mult, [AT, up_i], [AT])
            res[pr] = (kdt, U, WTt, AT)
            yield

        def rec(pr):
            kdt, U, WTt, AT = res[pr]
            un = m.next()
            for c in range(2):
                r0 = 64 * c
                ch = pr * 2 + c
                cs = slice(ch * 64, ch * 64 + 64)
                ps = self.psum.next()
                self.mm(ps[:, 0:128], WTt[:], S_[:], [WTt, S_], ps)
                self.tt(un[r0:r0 + 64, :], U[r0:r0 + 64, :], ps[r0:r0 + 64, 0:128], ALU.subtract, [U, ps], [un])
                yield
                pso = self.psum.next()
                self.mm(pso[:, 0:64], S_[:], QD[:, cs], [S_, QD], pso, start=True, stop=False)
                self.mm(pso[:, 0:64], un[r0:r0 + 64, :], AT[r0:r0 + 64, r0:r0 + 64], [un, AT], pso, start=False, stop=True)
                self.ecopy(OT[:, cs], pso[:, 0:64], [pso], [OT], eng="act")
                pss = self.psum.next()
                self.mm(pss[:, 0:128], kdt[r0:r0 + 64, :], un[r0:r0 + 64, :], [kdt, un], pss)
                self.stt(S_[:], S_[:], egl[:, ch:ch + 1], pss[:, 0:128], ALU.mult, ALU.add, [S_, egl, pss], [S_])
                yield

        yield from rr([prep(0), prep(1), prep(2), prep(3)])
        for pr in range(4):
            yield from rec(pr)
        self.gated_out(OT, X1, Z1, F_AZ + h, 0, h, b)

    def mixer_B(self, l, b, h):
        P = self.P
        t0 = b * 512
        sl = self.slots
        q, f_, vi, Z1, X1, B_, QT, KTl, QE, KE, OT, KK = sl[16:28]
        PROJa = self.a(self.PROJ)
        for row, dst in ((F_BQ + h, q), (F_BF + h, f_), (F_BI + h, vi)):
            P.dma(dst[:], PROJa[row, :, PAD + t0:PAD + t0 + 512], reads=[self.PROJ_T[row][b]], writes=[dst])
        lball, omlb, resetm = self.lball, self.omlb, self.resetm
        self.act(f_[:], f_[:], AF.Sigmoid, [f_], [f_])
        self.ts(f_[:], f_[:], omlb[:, l, h:h + 1], ALU.mult, [f_, omlb, lball], [f_], s2=lball[:, l, h:h + 1], op1=ALU.add)
        self.ts(KK[:], f_[:], -1.0, ALU.mult, [f_], [KK], s2=1.0, op1=ALU.add)
        self.act(f_[:], f_[:], AF.Ln, [f_], [f_])
        P.op("dve", lambda e: e.tensor_tensor_scan(out=B_[:], data0=resetm[:], data1=f_[:], initial=0.0, op0=ALU.mult, op1=ALU.add), [resetm, f_], [B_])
        ebl = self.small.next()
        self.act(ebl[:, 0:8], B_[:, 63::64], AF.Exp, [B_], [ebl])
        self.tt(v3(X1[:], 64), v3(B_[:], 64), bcast_mid(B_[:, 31::64], 64), ALU.subtract, [B_], [X1])
        self.act(QT[:], X1[:], AF.Exp, [X1], [QT])
        self.stt(QT[:], q[:], RS, QT[:], ALU.mult, ALU.mult, [q, QT], [QT])
        self.act(KTl[:], X1[:], AF.Exp, [X1], [KTl], scale=-1.0)
        self.tt(KTl[:], KTl[:], KK[:], ALU.mult, [KTl, KK], [KTl])
        self.act(QE[:], B_[:], AF.Exp, [B_], [QE])
        self.stt(QE[:], q[:], RS, QE[:], ALU.mult, ALU.mult, [q, QE], [QE])
        self.tt(v3(KE[:], 64), bcast_mid(B_[:, 63::64], 64), v3(B_[:], 64), ALU.subtract, [B_], [KE])
        self.act(KE[:], KE[:], AF.Exp, [KE], [KE])
        self.tt(KE[:], KE[:], KK[:], ALU.mult, [KE, KK], [KE])
        yield
        S_ = self.SB[h]
        m = self.m128b
        up_i = self.up_i
        for pr in range(4):
            s_ = slice(pr * 128, (pr + 1) * 128)
            ps = self.psum.next()
            self.tr(ps[:, 0:128], vi[:, s_], [vi], ps)
            vt = m.next()
            self.ecopy(vt[:], ps[:, 0:128], [ps], [vt], eng="act")
            ps = self.psum.next()
            self.tr(ps[:, 0:128], KE[:, s_], [KE], ps)
            ket = m.next()
            self.ecopy(ket[:], ps[:, 0:128], [ps], [ket], eng="dve")
            ps = self.psum.next()
            self.mm(ps[:, 0:128], KTl[:, s_], QT[:, s_], [KTl, QT], ps)
            AT = m.next()
            self.ts(AT[:], ps[:, 0:128], 1e30, ALU.min, [ps], [AT], s2=-1e30, op1=ALU.max)
            self.tt(AT[:], AT[:], up_i[:], ALU.mult, [AT, up_i], [AT])
            for c in range(2):
                r0 = 64 * c
                ch = pr * 2 + c
                cs = slice(ch * 64, ch * 64 + 64)
                pso = self.psum.next()
                self.mm(pso[:, 0:64], S_[:], QE[:, cs], [S_, QE], pso, start=True, stop=False)
                self.mm(pso[:, 0:64], vt[r0:r0 + 64, :], AT[r0:r0 + 64, r0:r0 + 64], [vt, AT], pso, start=False, stop=True)
                self.ecopy(OT[:, cs], pso[:, 0:64], [pso], [OT], eng="act")
                pss = self.psum.next()
                self.mm(pss[:, 0:128], ket[r0:r0 + 64, :], vt[r0:r0 + 64, :], [ket, vt], pss)
                self.stt(S_[:], S_[:], ebl[:, ch:ch + 1], pss[:, 0:128], ALU.mult, ALU.add, [S_, ebl, pss], [S_])
                yield
        self.gated_out(OT, X1, Z1, F_BZ + h, 1, 8 + h, b)

    def mixer_C(self, l, b):
        P = self.P
        t0 = b * 512
        sl = self.slots
        PROJa = self.a(self.PROJ)
        kmem, vmem, ident = self.kmem, self.vmem, self.ident
        m = self.m128
        SC = 256 ** -0.5
        for hc in range(4):
            cqs = []
            for j in range(2):
                row = F_CQ + 2 * hc + j
                st = sl[28 + j]
                P.dma(st[:], PROJa[row, :, PAD + t0:PAD + t0 + 512], reads=[self.PROJ_T[row][b]], writes=[st])
                cb = self.cqb[j]
                self.ecopy(cb[:], st[:], [st], [cb], eng="dve")
                cqs.append(cb)
            o0, o1 = sl[30], sl[31]
            for tq in range(4):
                s_ = slice(tq * 128, (tq + 1) * 128)
                ps = self.psum.next()
                for j in range(2):
                    self.mm(ps[:, 0:NMEM], cqs[j][:, s_], kmem[:, 2 * hc + j, :], [cqs[j], kmem], ps, start=(j == 0), stop=(j == 1))
                mx = self.small.next()
                P.op("dve", lambda e, mx=mx, ps=ps: e.tensor_reduce(out=mx[:, 0:1], in_=ps[:, 0:NMEM], axis=AX.X, op=ALU.max), [ps], [mx])
                self.ts(mx[:, 1:2], mx[:, 0:1], -SC, ALU.mult, [mx], [mx])
                pe_ = self.tp.next()
                self.act(pe_[:, 0:NMEM], ps[:, 0:NMEM], AF.Exp, [ps, mx], [pe_, mx], scale=SC, bias=mx[:, 1:2], accum_out=mx[:, 2:3])
                self.recip(mx[:, 3:4], mx[:, 2:3], [mx], [mx])
                self.ts(pe_[:, 0:NMEM], pe_[:, 0:NMEM], mx[:, 3:4], ALU.mult, [pe_, mx], [pe_])
                pTs = []
                for mt in range(2):
                    ps2 = self.psum.next()
                    self.tr(ps2[:, 0:128], pe_[:, mt * 128:(mt + 1) * 128], [pe_], ps2)
                    pT = self.pTb[mt]
                    self.ecopy(pT[:], ps2[:, 0:128], [ps2], [pT])
                    pTs.append(pT)
                for j, o in ((0, o0), (1, o1)):
                    ps3 = self.psum.next()
                    for mt in range(2):
                        self.mm(ps3[:, 0:128], vmem[:, mt, (2 * hc + j) * 128:(2 * hc + j + 1) * 128], pTs[mt][:], [vmem, pTs[mt]], ps3, start=(mt == 0), stop=(mt == 1))
                    self.ecopy(o[:, s_], ps3[:, 0:128], [ps3], [o])
            for j, o in ((0, o0), (1, o1)):
                row = 16 + 2 * hc + j
                P.dma(self.a(self.OD)[row, :, t0:t0 + 512], o[:], reads=[o], writes=[self.OD_T[row][b]], q="pq")

    def phase3(self, l, b, last):
        P = self.P
        t0 = b * 512
        sl = self.slots
        xts, ys = sl[0:16], sl[16:32]
        hb, big, gsb = self.hb, self.big, self.gsb
        XTv = self.a(self.XT).rearrange("(kt p) s -> p kt s", p=128)
        ODa, PROJa = self.a(self.OD), self.a(self.PROJ)
        w_br, w_out, w1, w2 = self.a(self.w_br), self.a(self.w_out), self.a(self.w1), self.a(self.w2)
        for br in range(3):
            stgs = []
            for kt in range(8):
                row = br * 8 + kt
                st = ys[kt]
                P.dma(st[:], ODa[row, :, t0:t0 + 512], reads=[self.OD_T[row][b]], writes=[st])
                self.ecopy(big[:, kt, :], st[:], [st], [big], eng=("dve" if kt % 2 else "act"))
            for f in range(16):
                wb = self.load_w(w_br[l, br, f], key=("br", br, f), b=b, wb_ap=self.a(self.wb_br)[br, f])
                ps = self.psum.next()
                for kt in range(8):
                    self.mm(ps[:], wb[:, kt, :], big[:, kt, :], [wb, big], ps, start=(kt == 0), stop=(kt == 7))
                g = self.tp.next()
                row = F_G + br * 16 + f
                P.dma(g[:], PROJa[row, :, PAD + t0:PAD + t0 + 512], reads=[self.PROJ_T[row][b]], writes=[g])
                acc = xts[f]
                if br == 0:
                    self.tt(acc[:], ps[:], g[:], ALU.mult, [ps, g], [acc])
                else:
                    self.tt(g[:], ps[:], g[:], ALU.mult, [ps, g], [g])
                    if br == 1:
                        self.tt(acc[:], acc[:], g[:], ALU.add, [acc, g], [acc])
                    else:
                        self.tt(hb[:, f, :], acc[:], g[:], ALU.add, [acc, g], [hb])
        for f in range(16):
            wb = self.load_w(w_out[l, f], key=("out", f), b=b, wb_ap=self.a(self.wb_out)[f])
            ps = self.psum.next()
            for kt in range(KT):
                self.mm(ps[:], wb[:, kt, :], hb[:, kt, :], [wb, hb], ps, start=(kt == 0), stop=(kt == KT - 1))
            self.ecopy(ys[f][:], ps[:], [ps], [ys[f]])
        P.dma(self.slotbuf[:, 0:16, :], XTv[:, :, t0:t0 + 512], reads=[self.XT_T[b]], writes=xts)
        t1, t2 = self.tp.next(), self.tp.next()
        r = self.rstd_from(lambda kt: ys[kt][:], ys, KT, 1.0 / D, [t1, t2])
        for kt in range(KT):
            self.stt(ys[kt][:], ys[kt][:], gsb[:, 1, kt:kt + 1], r[:], ALU.mult, ALU.mult, [ys[kt], gsb, r], [ys[kt]])
            self.tt(xts[kt][:], xts[kt][:], ys[kt][:], ALU.add, [xts[kt], ys[kt]], [xts[kt]])
        t1, t2 = self.tp.next(), self.tp.next()
        r = self.rstd_from(lambda kt: xts[kt][:], xts, KT, 1.0 / D, [t1, t2])
        for kt in range(KT):
            self.stt(hb[:, kt, :], xts[kt][:], gsb[:, 2, kt:kt + 1], r[:], ALU.mult, ALU.mult, [xts[kt], gsb, r], [hb])
        for qtr in range(4):
            for fl in range(16):
                f = qtr * 16 + fl
                wb = self.load_w(w1[l, f], key=("w1", f), b=b, wb_ap=self.a(self.wb_w1)[f])
                ps = self.psum.next()
                for kt in range(KT):
                    self.mm(ps[:], wb[:, kt, :], hb[:, kt, :], [wb, hb], ps, start=(kt == 0), stop=(kt == KT - 1))
                rl = self.tp.next()
                self.act(rl[:], ps[:], AF.Relu, [ps], [rl])
                self.tt(big[:, fl, :], rl[:], rl[:], ALU.mult, [rl], [big])
            for f in range(16):
                wb = self.load_w(w2[l, f, :, qtr * 16:(qtr + 1) * 16, :], key=("w2", f, qtr), b=b, wb_ap=self.a(self.wb_w2)[f, qtr])
                ps = self.psum.next()
                for kt in range(16):
                    self.mm(ps[:], wb[:, kt, :], big[:, kt, :], [wb, big], ps, start=(kt == 0), stop=(kt == 15))
                if qtr == 0:
                    self.ecopy(ys[f][:], ps[:], [ps], [ys[f]])
                else:
                    self.tt(ys[f][:], ys[f][:], ps[:], ALU.add, [ys[f], ps], [ys[f]])
        t1, t2 = self.tp.next(), self.tp.next()
        r = self.rstd_from(lambda kt: ys[kt][:], ys, KT, 1.0 / D, [t1, t2])
        for kt in range(KT):
            self.stt(ys[kt][:], ys[kt][:], gsb[:, 3, kt:kt + 1], r[:], ALU.mult, ALU.mult, [ys[kt], gsb, r], [ys[kt]])
            self.tt(xts[kt][:], xts[kt][:], ys[kt][:], ALU.add, [xts[kt], ys[kt]], [xts[kt]])
        if last:
            ov = self.a(self.outT).rearrange("(kt p) s -> p kt s", p=128)
            P.dma(ov[:, :, t0:t0 + 512], self.slotbuf[:, 0:16, :], reads=xts, writes=[self.OUT_T], q="pq")
        else:
            P.dma(XTv[:, :, t0:t0 + 512], self.slotbuf[:, 0:16, :], reads=xts, writes=[self.XT_T[b]], q="pq")

    def build(self, stop_after=None):
        P, L, NB = self.P, self.L, self.NB
        self.cqb = [P.sb([128, 512], BF16, f"cqb{j}") for j in range(2)]
        self.pTb = [P.sb([128, 128], BF16, f"pTb{j}") for j in range(2)]
        self.smtb = Rot([P.sb([16, 512], F32, f"smt{j}") for j in range(2)])
        import os
        stop = os.environ.get("KSTOP", "")
        order = ["prologue", "p1", "mem", "A", "B", "C", "p3"]
        lim = order.index(stop) if stop in order else 99
        self.prologue()
        for l in range(L):
            self.layer_params(l)
            pairs = [(b0, min(2, NB - b0)) for b0 in range(0, NB, 2)]

            def mixers(b):
                smt = self.smtb.next()
                P.dma(smt[0:16, :], self.a(self.PSM)[:, b * 512:(b + 1) * 512], reads=[self.PSM_T[b]], writes=[smt])
                for h in range(8):
                    gl = []
                    if lim >= 3:
                        gl.append(self.mixer_A(l, b, h, smt))
                    if lim >= 4:
                        gl.append(self.mixer_B(l, b, h))
                    yield from rr(gl)
                if lim >= 5:
                    self.mixer_C(l, b)
                yield

            def chain(bs):
                for b in bs:
                    yield from mixers(b)

            if lim >= 1:
                for _ in self.phase1(l, *pairs[0]):
                    pass
            if lim >= 2:
                self.mem_prep(l)
            for i, (b0, nb) in enumerate(pairs):
                main = chain(range(b0, b0 + nb))
                side = self.phase1(l, *pairs[i + 1]) if (i + 1 < len(pairs) and lim >= 1) else None
                for _ in rrw(main, side, 8):
                    pass
            if lim >= 6:
                for b in range(NB):
                    self.phase3(l, b, last=(l == L - 1))
        st = P.emit()
        P.close()
        return st

NCORES = 4
_CACHE = {}


def _tile_w(w, kt):
    K, F = w.shape
    return np.ascontiguousarray(w.reshape(kt, 128, F // 128, 128).transpose(2, 1, 0, 3))


def _fm_vec(g):
    return np.ascontiguousarray(g.reshape(-1, 128).T)


def prep_weights(inp, L):
    f32 = np.float32
    w_in = np.asarray(inp["w_in"], f32)
    cols_main = np.r_[0:4096, 4112:15376]
    out = {}
    out["w_in"] = np.stack([_tile_w(w_in[l][:, cols_main], 16) for l in range(L)])
    wsm = w_in[:, :, 4096:4112]
    out["w_sm"] = np.ascontiguousarray(wsm.reshape(L, 16, 128, 16).transpose(0, 2, 1, 3))
    out["w_kv"] = np.stack([_tile_w(np.asarray(inp["w_kv_mem"][l], f32), 16) for l in range(L)])
    out["w_br"] = np.stack([np.stack([_tile_w(np.asarray(inp[n][l], f32), 8) for n in ("w_br_a", "w_br_b", "w_br_c")]) for l in range(L)])
    out["w_out"] = np.stack([_tile_w(np.asarray(inp["w_out"][l], f32), 16) for l in range(L)])
    out["w1"] = np.stack([_tile_w(np.asarray(inp["w_mlp_in"][l], f32), 16) for l in range(L)])
    out["w2"] = np.stack([_tile_w(np.asarray(inp["w_mlp_out"][l], f32), 64) for l in range(L)])
    out["gains"] = np.stack([np.stack([_fm_vec(np.asarray(inp[n][l], f32)) for n in ("g_pre_mix", "g_post_mix", "g_pre_mlp", "g_post_mlp", "g_mem")], axis=1) for l in range(L)])
    cw = np.asarray(inp["conv_a"], f32)
    out["convw"] = np.ascontiguousarray(cw.reshape(L, 4, 24, 128).transpose(0, 3, 2, 1))
    hv = np.concatenate([np.asarray(inp["a_log"], f32), np.asarray(inp["dt_bias"], f32)], axis=1)
    out["hvec"] = np.ascontiguousarray(np.broadcast_to(hv[:, None, :], (L, 128, 16)))
    out["gnv"] = np.ascontiguousarray(np.stack([np.asarray(inp["gn_a"], f32), np.asarray(inp["gn_b"], f32)], axis=2))
    lb = np.asarray(inp["lb_raw"], f32)
    out["lbraw"] = np.ascontiguousarray(lb.reshape(L, 8, 128).transpose(2, 1, 0))
    return out


def kernel(**inp):
    x = np.asarray(inp["x"], np.float32)
    mem = np.asarray(inp["mem"], np.float32)
    B, S, _ = x.shape
    L = inp["w_in"].shape[0]
    key = (S, L)
    if key not in _CACHE:
        k = K(S, L)
        k.build()
        _CACHE[key] = k
    k = _CACHE[key]
    w = prep_weights(inp, L)
    in_maps = []
    for c in range(B):
        m = dict(w)
        m["xT"] = np.ascontiguousarray(x[c].T)
        m["memT"] = np.ascontiguousarray(mem[c].T)
        in_maps.append(m)
    res = run_bass_kernel_spmd(k.nc, in_maps, core_ids=list(range(B)))
    out = np.stack([np.ascontiguousarray(res.results[c]["outT"].T) for c in range(B)])
    return out.astype(np.float32)
```

```python
import numpy as np
import concourse.bass as bass
import concourse.mybir as mybir
from contextlib import ExitStack

F32 = mybir.dt.float32
BF16 = mybir.dt.bfloat16
I32 = mybir.dt.int32
AF = mybir.ActivationFunctionType
ALU = mybir.AluOpType
AX = mybir.AxisListType

COMPUTE = ("pe", "dve", "act", "pool")
DMAQ = ("sp", "pq")
NDMASEM = {"sp": 24, "pq": 12}


class T:
    __slots__ = ("t", "name", "w", "r", "rd")

    def __init__(self, t, name=""):
        self.t = t
        self.name = name
        self.w = None
        self.r = {}
        self.rd = []

    def __getitem__(self, idx):
        return self.t[idx]


class Ins:
    __slots__ = ("eng", "fn", "inc", "idx", "deps", "dma_sem", "dma_val", "ticket")

    def __init__(self, eng, fn, inc):
        self.eng = eng
        self.fn = fn
        self.inc = inc
        self.deps = []
        self.dma_sem = None
        self.dma_val = 0
        self.ticket = None


class Prog:
    def __init__(self, nc):
        self.nc = nc
        self.stack = ExitStack()
        self.all = []
        self.per = {e: [] for e in COMPUTE + DMAQ}
        self.nalloc = 0

    def sb(self, shape, dt=F32, name=None):
        self.nalloc += 1
        name = name or f"sb{self.nalloc}"
        t = self.stack.enter_context(self.nc.sbuf_tensor(name, list(shape), dt))
        return T(t, name)

    def ps(self, shape, dt=F32, name=None):
        self.nalloc += 1
        name = name or f"ps{self.nalloc}"
        t = self.stack.enter_context(self.nc.psum_tensor(name, list(shape), dt))
        return T(t, name)

    def dram(self, name, shape, dt=F32, kind="Internal"):
        t = self.nc.dram_tensor(name, list(shape), dt, kind=kind)
        return T(t, name)

    def view(self, t, name=""):
        return T(t.t if isinstance(t, T) else t, name)

    def op(self, eng, fn, reads=(), writes=(), inc=True):
        ins = Ins(eng, fn, inc)
        isdma = eng in DMAQ
        deps = {}

        def add(d, kind):
            if d is None or d is ins:
                return
            if d.eng == eng and not isdma:
                if kind != "raw" or eng == "pe":
                    return
            deps[id(d)] = d

        for t in reads:
            add(t.w, "raw")
        for t in writes:
            add(t.w, "waw")
            for r in t.r.values():
                add(r, "war")
            for r in t.rd:
                add(r, "war")
        for t in reads:
            if isdma:
                t.rd.append(ins)
            else:
                t.r[eng] = ins
        for t in writes:
            t.w = ins
            t.r = {}
            t.rd = []
        ins.deps = list(deps.values())
        ins.idx = len(self.per[eng])
        self.per[eng].append(ins)
        self.all.append(ins)
        return ins

    def dma(self, out_ap, in_ap, reads=(), writes=(), q="sp", **kw):
        return self.op(q, lambda e: e.dma_start(out=out_ap, in_=in_ap, **kw), reads, writes)

    def emit(self):
        nc = self.nc
        engobj = {"pe": nc.tensor, "dve": nc.vector, "act": nc.scalar, "pool": nc.gpsimd,
                  "sp": nc.sync, "pq": nc.gpsimd}
        sems = {e: self.stack.enter_context(nc.semaphore(f"s_{e}")) for e in COMPUTE}
        dsem = {q: [self.stack.enter_context(nc.semaphore(f"d_{q}{i}")) for i in range(NDMASEM[q])]
                for q in DMAQ}
        for e in COMPUTE:
            if self.per[e]:
                self.per[e][-1].inc = True
        for e in COMPUTE:
            c = 0
            lst = self.per[e]
            for ins in lst:
                if ins.inc:
                    c += 1
                    ins.ticket = c
            nxt = None
            for ins in reversed(lst):
                if ins.inc:
                    nxt = ins.ticket
                else:
                    ins.ticket = nxt
        dcount = {q: [0] * NDMASEM[q] for q in DMAQ}
        dprev = {q: [None] * NDMASEM[q] for q in DMAQ}
        for q in DMAQ:
            for i, ins in enumerate(self.per[q]):
                k = i % NDMASEM[q]
                dcount[q][k] += 16
                ins.dma_sem = dsem[q][k]
                ins.dma_val = dcount[q][k]
                ins.ticket = (q, k)
        issue_eng = {"pe": "pe", "dve": "dve", "act": "act", "pool": "pool", "sp": "sp", "pq": "pool"}
        seen = {e: {} for e in ("pe", "dve", "act", "pool", "sp")}
        nwait = 0
        for ins in self.all:
            ie = issue_eng[ins.eng]
            eo = engobj[ins.eng]
            sn = seen[ie]
            waits = []
            for d in ins.deps:
                if d.eng in DMAQ:
                    waits.append((d.dma_sem, d.dma_val))
                else:
                    waits.append((sems[d.eng], d.ticket))
            if ins.eng in DMAQ:
                q, k = ins.ticket
                prev = ins.dma_val - 16
                if prev > 0:
                    waits.append((ins.dma_sem, prev))
            for s, v in waits:
                key = id(s)
                if sn.get(key, 0) >= v:
                    continue
                sn[key] = v
                eo.wait_ge(s, v)
                nwait += 1
            r = ins.fn(eo)
            if ins.eng in DMAQ:
                r.then_inc(ins.dma_sem, 16)
            elif ins.inc:
                r.then_inc(sems[ins.eng], 1)
        for q in DMAQ:
            for k in range(NDMASEM[q]):
                if dcount[q][k] > 0:
                    nc.sync.wait_ge(dsem[q][k], dcount[q][k])
        for e in COMPUTE:
            lst = self.per[e]
            if lst:
                nc.sync.wait_ge(sems[e], lst[-1].ticket)
        self.stats = {e: len(self.per[e]) for e in self.per}
        self.stats["waits"] = nwait
        return self.stats

    def close(self):
        self.stack.close()
from concourse.bass_utils import run_bass_kernel_spmd

D = 2048
KT = 16
NMEM = 256
EPS = 1e-6
F_AQ, F_AK, F_AV, F_AZ, F_BQ, F_BF, F_BI, F_BZ, F_CQ, F_G = 0, 8, 16, 24, 32, 40, 48, 56, 64, 72
PAD = 3
RS = 128 ** -0.5


def bcast_mid(ap2, n_inner):
    a = ap2.ap
    return bass.AP(ap2.tensor, ap2.offset, [list(a[0]), list(a[1]), [0, n_inner]])


def bcast_outer(ap2, n_outer):
    a = ap2.ap
    return bass.AP(ap2.tensor, ap2.offset, [list(a[0]), [0, n_outer], list(a[1])])


def v3(ap, inner):
    return ap.rearrange("p (a b) -> p a b", b=inner)


def rr(gens):
    gens = list(gens)
    while gens:
        for g in list(gens):
            try:
                next(g)
                yield
            except StopIteration:
                gens.remove(g)


def rrw(main, side, ratio):
    main_done = False
    side_done = side is None
    while not (main_done and side_done):
        if not main_done:
            for _ in range(ratio):
                try:
                    next(main)
                except StopIteration:
                    main_done = True
                    break
        if not side_done:
            try:
                next(side)
            except StopIteration:
                side_done = True
        yield


class Rot:
    def __init__(self, items):
        self.items = items
        self.i = 0

    def next(self):
        x = self.items[self.i % len(self.items)]
        self.i += 1
        return x


class K:
    def __init__(self, S, L):
        self.S, self.L = S, L
        self.NB = S // 512
        nc = bass.Bass("TRN2", target_bir_lowering=False)
        self.nc = nc
        P = self.P = Prog(nc)
        NB = self.NB
        ext = lambda n, s: P.dram(n, s, F32, kind="ExternalInput")
        self.xT_in = ext("xT", [D, S])
        self.memT = ext("memT", [D, NMEM])
        self.w_in = ext("w_in", [L, 120, 128, KT, 128])
        self.w_sm = ext("w_sm", [L, 128, KT, 16])
        self.w_kv = ext("w_kv", [L, 16, 128, KT, 128])
        self.w_br = ext("w_br", [L, 3, 16, 128, 8, 128])
        self.w_out = ext("w_out", [L, 16, 128, KT, 128])
        self.w1 = ext("w1", [L, 64, 128, KT, 128])
        self.w2 = ext("w2", [L, 16, 128, 64, 128])
        self.gains = ext("gains", [L, 128, 5, KT])
        self.convw = ext("convw", [L, 128, 24, 4])
        self.hvec = ext("hvec", [L, 128, 16])
        self.gnv = ext("gnv", [L, 128, 2])
        self.lbraw = ext("lbraw", [128, 8, L])
        self.outT = P.dram("outT", [D, S], F32, kind="ExternalOutput")
        self.XT = P.dram("XT", [D, S], F32)
        self.PROJ = P.dram("PROJ", [120, 128, S + PAD], F32)
        self.PSM = P.dram("PSM", [16, S], F32)
        self.OD = P.dram("OD", [24, 128, S], F32)
        self.wb_in = P.dram("wb_in", [120, 128, KT, 128], BF16)
        self.wb_br = P.dram("wb_br", [3, 16, 128, 8, 128], BF16)
        self.wb_out = P.dram("wb_out", [16, 128, KT, 128], BF16)
        self.wb_w1 = P.dram("wb_w1", [64, 128, KT, 128], BF16)
        self.wb_w2 = P.dram("wb_w2", [16, 4, 128, KT, 128], BF16)
        self.WB_T = {}
        self.XT_T = [T(None, f"XT{b}") for b in range(NB)]
        self.PROJ_T = [[T(None, f"PJ{f}_{b}") for b in range(NB)] for f in range(120)]
        self.PSM_T = [T(None, f"PSM{b}") for b in range(NB)]
        self.OD_T = [[T(None, f"OD{f}_{b}") for b in range(NB)] for f in range(24)]
        self.WT_ = T(None, "weights")
        self.OUT_T = T(None, "out")

        self.ones = P.sb([128, 128], F32, "ones")
        self.ident = P.sb([128, 128], F32, "ident")
        self.nlow_s = P.sb([128, 128], F32, "nlow_s")
        self.nup_s = P.sb([128, 128], F32, "nup_s")
        self.up_i = P.sb([128, 128], F32, "up_i")
        self.resetm = P.sb([128, 512], F32, "resetm")
        self.zero3 = P.sb([128, 4], F32, "zero3")
        self.epsb = P.sb([128, 1], F32, "epsb")
        self.oneb = P.sb([128, 1], F32, "oneb")
        ones, ident, nlow_s, nup_s, up_i, resetm, zero3 = self.ones, self.ident, self.nlow_s, self.nup_s, self.up_i, self.resetm, self.zero3
        P.op("pool", lambda e: e.memset(ones[:], 1.0), [], [ones])
        self.ones_bf = P.sb([128, 128], BF16, "ones_bf")
        P.op("pool", lambda e: e.memset(self.ones_bf[:], 1.0), [], [self.ones_bf])
        P.op("pool", lambda e: e.memset(zero3[:], 0.0), [], [zero3])
        P.op("pool", lambda e: e.memset(self.epsb[:], EPS), [], [self.epsb])
        P.op("pool", lambda e: e.memset(self.oneb[:], 1.0), [], [self.oneb])
        P.op("pool", lambda e: e.memset(resetm[:], 1.0), [], [resetm])
        P.op("pool", lambda e: e.memset(resetm[:, ::64], 0.0), [resetm], [resetm])
        P.op("pool", lambda e: e.affine_select(out=ident[:], in_=ones[:], pattern=[[-1, 128]], compare_op=ALU.is_equal, fill=0.0, base=0, channel_multiplier=1), [ones], [ident])
        P.op("pool", lambda e: e.memset(nlow_s[:], -1.0), [], [nlow_s])
        P.op("pool", lambda e: e.memset(nup_s[:], -1.0), [], [nup_s])
        P.op("pool", lambda e: e.affine_select(out=nlow_s[:], in_=nlow_s[:], pattern=[[-1, 128]], compare_op=ALU.is_ge, fill=0.0, base=-1, channel_multiplier=1), [nlow_s], [nlow_s])
        P.op("pool", lambda e: e.memset(nlow_s[64:128, 0:64], 0.0), [nlow_s], [nlow_s])
        P.op("pool", lambda e: e.affine_select(out=nup_s[:], in_=nup_s[:], pattern=[[1, 128]], compare_op=ALU.is_ge, fill=0.0, base=-1, channel_multiplier=-1), [nup_s], [nup_s])
        P.op("pool", lambda e: e.memset(nup_s[0:64, 64:128], 0.0), [nup_s], [nup_s])
        P.op("pool", lambda e: e.affine_select(out=up_i[:], in_=ones[:], pattern=[[1, 128]], compare_op=ALU.is_ge, fill=0.0, base=0, channel_multiplier=-1), [ones], [up_i])
        P.op("pool", lambda e: e.memset(up_i[0:64, 64:128], 0.0), [up_i], [up_i])

        self.psum = Rot([P.ps([128, 512], F32, f"psb{i}") for i in range(8)])
        self.wstage = Rot([P.sb([128, 8, 128], F32, f"wst{i}") for i in range(2)])
        self.xc = Rot([P.sb([128, 512], F32, f"xc{i}") for i in range(4)])
        self.r1 = P.sb([128, 512], F32, "r1")
        self.wbf = Rot([P.sb([128, 16, 128], BF16, f"wbf{i}") for i in range(4)])
        self.slotbuf = P.sb([128, 32, 512], F32, "slots")
        self.slots = [T(self.slotbuf[:, i, :], f"slot{i}") for i in range(32)]
        self.hb = P.sb([128, KT, 512], BF16, "hb")
        self.big = P.sb([128, KT, 512], BF16, "big")
        self.stg = Rot([P.sb([128, 512], F32, f"stg{i}") for i in range(4)])
        self.tp = Rot([P.sb([128, 512], F32, f"tp{i}") for i in range(5)])
        self.m128 = Rot([P.sb([128, 128], F32, f"m{i}") for i in range(35)])
        self.m128b = Rot([P.sb([128, 128], F32, f"mb{i}") for i in range(10)])
        self.small = Rot([P.sb([128, 8], F32, f"sm{i}") for i in range(32)])
        self.raw = Rot([P.sb([128, 516], F32, f"raw{i}") for i in range(3)])
        self.kmem = P.sb([128, 8, NMEM], BF16, "kmem")
        self.vmem = P.sb([128, 2, 1024], BF16, "vmem")
        self.gsb = P.sb([128, 5, KT], F32, "gsb")
        self.cwsb = P.sb([128, 24, 4], F32, "cwsb")
        self.hv = P.sb([128, 16], F32, "hv")
        self.negA = P.sb([128, 8], F32, "negA")
        self.gn = P.sb([128, 2], F32, "gn")
        self.lbr = P.sb([128, 8, L], F32, "lbr")
        self.lbe = P.sb([128, 8, L], F32, "lbe")
        self.lbs = P.sb([128, 8], F32, "lbs")
        self.lball = P.sb([128, L, 8], F32, "lball")
        self.omlb = P.sb([128, L, 8], F32, "omlb")
        self.wsmst = P.sb([128, KT, 16], F32, "wsmst")
        self.wsmb = P.sb([128, KT, 16], BF16, "wsmb")
        self.SA = [P.sb([128, 128], F32, f"SA{h}") for h in range(8)]
        self.SB = [P.sb([128, 128], F32, f"SB{h}") for h in range(8)]
        self.evq = 0

    def a(self, t):
        return t.t.ap()

    def mm(self, ps_ap, lhsT, rhs, reads, ps_t, start=True, stop=True, inc=None):
        return self.P.op("pe", lambda e: e.matmul(ps_ap, lhsT, rhs, start=start, stop=stop), reads, [ps_t], inc=(stop if inc is None else inc))

    def tr(self, ps_ap, in_ap, reads, ps_t):
        ident = self.ident
        return self.P.op("pe", lambda e: e.transpose(ps_ap, in_ap, ident[:]), reads + [ident], [ps_t], inc=True)

    def act(self, out, in_, func, reads, writes, **kw):
        return self.P.op("act", lambda e: e.activation(out=out, in_=in_, func=func, **kw), reads, writes)

    def ecopy(self, out, in_, reads, writes, eng=None):
        self.evq += 1
        if eng is None:
            eng = "act" if self.evq % 2 else "dve"
        if eng == "act":
            return self.P.op("act", lambda e: e.copy(out=out, in_=in_), reads, writes)
        return self.P.op("dve", lambda e: e.tensor_copy(out=out, in_=in_), reads, writes)

    def tt(self, out, in0, in1, op, reads, writes):
        return self.P.op("dve", lambda e: e.tensor_tensor(out=out, in0=in0, in1=in1, op=op), reads, writes)

    def ts(self, out, in0, s1, op0, reads, writes, s2=None, op1=None):
        if op1 is None:
            return self.P.op("dve", lambda e: e.tensor_scalar(out=out, in0=in0, scalar1=s1, scalar2=None, op0=op0), reads, writes)
        return self.P.op("dve", lambda e: e.tensor_scalar(out=out, in0=in0, scalar1=s1, scalar2=s2, op0=op0, op1=op1), reads, writes)

    def stt(self, out, in0, scalar, in1, op0, op1, reads, writes):
        return self.P.op("dve", lambda e: e.scalar_tensor_tensor(out=out, in0=in0, scalar=scalar, in1=in1, op0=op0, op1=op1), reads, writes)

    def recip(self, out, in_, reads, writes):
        return self.P.op("dve", lambda e: e.reciprocal(out=out, in_=in_), reads, writes)

    def rstd_from(self, src_fn, src_reads, nkt, scale, sqs, n=512):
        ps = self.psum.next()
        for kt in range(nkt):
            sq = sqs[kt % 2] if nkt > 1 else sqs[0]
            sqb = sq[:].bitcast(BF16)[:, 0:n]
            self.act(sqb, src_fn(kt), AF.Square, src_reads, [sq])
            self.mm(ps[:, 0:n], self.ones_bf[:], sqb, [self.ones_bf, sq], ps, start=(kt == 0), stop=(kt == nkt - 1), inc=True)
        r = sqs[0]
        self.act(r[:, 0:n], ps[:, 0:n], AF.Sqrt, [ps, self.epsb], [r], scale=scale, bias=self.epsb[:])
        self.recip(r[:, 0:n], r[:, 0:n], [r], [r])
        return r

    def load_w(self, w_ap, key=None, b=0, wb_ap=None):
        k = w_ap.shape[1]
        wb = self.wbf.next()
        if key is not None:
            wt = self.WB_T.setdefault(key, T(None, str(key)))
            if b > 0:
                self.P.dma(wb[:, 0:k, :], wb_ap, reads=[wt], writes=[wb])
                return wb
        for k0 in range(0, k, 8):
            st = self.wstage.next()
            self.P.dma(st[:, 0:8, :], w_ap[:, k0:k0 + 8, :], reads=[self.WT_], writes=[st])
            self.ecopy(wb[:, k0:k0 + 8, :], st[:, 0:8, :], [st], [wb])
        if key is not None:
            self.P.dma(wb_ap, wb[:, 0:k, :], reads=[wb], writes=[wt], q="pq")
        return wb

    def diag_cols(self, src, scratch):
        ident = self.ident
        self.tt(v3(scratch[:], 128), v3(src[:], 128), bcast_outer(ident[:, :], 4), ALU.mult, [src, ident], [scratch])
        c = self.small.next()
        self.P.op("dve", lambda e: e.tensor_reduce(out=c[:, 0:4], in_=v3(scratch[:], 128), axis=AX.X, op=ALU.add), [scratch], [c])
        return c

    def prologue(self):
        P, L, NB = self.P, self.L, self.NB
        PROJa, XTa, xTa = self.a(self.PROJ), self.a(self.XT), self.a(self.xT_in)
        for f in range(24):
            P.dma(PROJa[f, :, 0:PAD], self.zero3[:, 0:PAD], reads=[self.zero3], writes=[self.PROJ_T[f][0]], q="pq")
        for b in range(NB):
            P.dma(XTa[:, b * 512:(b + 1) * 512], xTa[:, b * 512:(b + 1) * 512], reads=[self.WT_], writes=[self.XT_T[b]], q="pq")
        lbr, lbe, lbs, lball, omlb = self.lbr, self.lbe, self.lbs, self.lball, self.omlb
        P.dma(lbr[:], self.a(self.lbraw), reads=[self.WT_], writes=[lbr])
        self.act(lbe[:], lbr[:], AF.Exp, [lbr], [lbe])
        P.op("dve", lambda e: e.tensor_reduce(out=lbs[:], in_=lbe[:], axis=AX.X, op=ALU.add), [lbe], [lbs])
        self.recip(lbs[:], lbs[:], [lbs], [lbs])
        P.op("dve", lambda e: e.memset(lball[:, 0, :], 0.0), [], [lball])
        for l in range(1, L):
            tl = self.small.next()
            self.tt(tl[:], lbe[:, :, l], lbs[:], ALU.mult, [lbe, lbs], [tl])
            self.tt(lball[:, l, :], lball[:, l - 1, :], tl[:], ALU.add, [lball, tl], [lball])
        self.ts(omlb[:], lball[:], -1.0, ALU.mult, [lball], [omlb], s2=1.0, op1=ALU.add)

    def layer_params(self, l):
        P = self.P
        P.dma(self.gsb[:], self.a(self.gains)[l], reads=[self.WT_], writes=[self.gsb])
        P.dma(self.cwsb[:], self.a(self.convw)[l], reads=[self.WT_], writes=[self.cwsb])
        P.dma(self.hv[:], self.a(self.hvec)[l], reads=[self.WT_], writes=[self.hv])
        P.dma(self.gn[:], self.a(self.gnv)[l], reads=[self.WT_], writes=[self.gn])
        P.dma(self.wsmst[:], self.a(self.w_sm)[l], reads=[self.WT_], writes=[self.wsmst])
        wsmb, wsmst, negA, hv = self.wsmb, self.wsmst, self.negA, self.hv
        P.op("pool", lambda e: e.tensor_copy(out=wsmb[:], in_=wsmst[:]), [wsmst], [wsmb])
        self.act(negA[:], hv[:, 0:8], AF.Exp, [hv], [negA])
        self.ts(negA[:], negA[:], -1.0, ALU.mult, [negA], [negA])
        for h in range(8):
            P.op("pool", lambda e, h=h: e.memset(self.SA[h][:], 0.0), [], [self.SA[h]])
            P.op("pool", lambda e, h=h: e.memset(self.SB[h][:], 0.0), [], [self.SB[h]])

    def phase1(self, l, b0, nb):
        P = self.P
        gsb = self.gsb
        acts = [self.hb, self.big][:nb]
        XTa = self.a(self.XT)
        r1 = self.r1
        for i in range(nb):
            b = b0 + i
            hb = acts[i]
            cs = slice(b * 512, (b + 1) * 512)
            ps = self.psum.next()
            for kt in range(KT):
                xc = self.xc.next()
                P.dma(xc[:], XTa[kt * 128:(kt + 1) * 128, cs], reads=[self.XT_T[b]], writes=[xc])
                self.act(xc[:], xc[:], AF.Square, [xc], [xc])
                self.mm(ps[:], self.ones[:], xc[:], [self.ones, xc], ps, start=(kt == 0), stop=(kt == KT - 1), inc=True)
            self.act(r1[:], ps[:], AF.Sqrt, [ps, self.epsb], [r1], scale=1.0 / D, bias=self.epsb[:])
            self.recip(r1[:], r1[:], [r1], [r1])
            yield
            for kt in range(KT):
                xc = self.xc.next()
                P.dma(xc[:], XTa[kt * 128:(kt + 1) * 128, cs], reads=[self.XT_T[b]], writes=[xc])
                self.stt(hb[:, kt, :], xc[:], gsb[:, 0, kt:kt + 1], r1[:], ALU.mult, ALU.mult, [xc, gsb, r1], [hb])
                if kt % 4 == 3:
                    yield
        PROJa = self.a(self.PROJ)
        w_in = self.a(self.w_in)
        wb_in = self.a(self.wb_in)
        for f in range(120):
            wb = self.load_w(w_in[l, f], key=("in", f), b=b0, wb_ap=wb_in[f])
            for i in range(nb):
                b = b0 + i
                hb = acts[i]
                ps = self.psum.next()
                for kt in range(KT):
                    self.mm(ps[:], wb[:, kt, :], hb[:, kt, :], [wb, hb], ps, start=(kt == 0), stop=(kt == KT - 1))
                s = self.stg.next()
                if F_AZ <= f < F_BF or F_BZ <= f < F_CQ:
                    self.act(s[:], ps[:], AF.Silu, [ps], [s])
                elif f >= F_G:
                    self.act(s[:], ps[:], AF.Sigmoid, [ps], [s])
                else:
                    self.ecopy(s[:], ps[:], [ps], [s], eng="dve")
                P.dma(PROJa[f, :, PAD + b * 512: PAD + (b + 1) * 512], s[:], reads=[s], writes=[self.PROJ_T[f][b]], q="pq")
            yield
        wsmb = self.wsmb
        for i in range(nb):
            b = b0 + i
            hb = acts[i]
            ps = self.psum.next()
            for kt in range(KT):
                self.mm(ps[0:16, :], wsmb[:, kt, :], hb[:, kt, :], [wsmb, hb], ps, start=(kt == 0), stop=(kt == KT - 1))
            s = self.stg.next()
            self.ecopy(s[0:16, :], ps[0:16, :], [ps], [s], eng="dve")
            P.dma(self.a(self.PSM)[:, b * 512:(b + 1) * 512], s[0:16, :], reads=[s], writes=[self.PSM_T[b]], q="pq")

    def mem_prep(self, l):
        P = self.P
        sl = self.slots
        mx = sl[0:16]
        gsb, hm = self.gsb, self.big
        memv = self.a(self.memT).rearrange("(kt p) s -> p kt s", p=128)
        P.dma(self.slotbuf[:, 0:16, 0:NMEM], memv, reads=[self.WT_], writes=mx)
        r = self.rstd_from(lambda kt: mx[kt][:, 0:NMEM], mx, KT, 1.0 / D, [sl[16], sl[17]], n=NMEM)
        for kt in range(KT):
            self.stt(hm[:, kt, 0:NMEM], mx[kt][:, 0:NMEM], gsb[:, 4, kt:kt + 1], r[:, 0:NMEM], ALU.mult, ALU.mult, [mx[kt], gsb, r], [hm])
        w_kv = self.a(self.w_kv)
        kmem, vmem = self.kmem, self.vmem
        for f in range(16):
            wb = self.load_w(w_kv[l, f])
            if f < 8:
                ps = self.psum.next()
                for kt in range(KT):
                    self.mm(ps[:, 0:NMEM], wb[:, kt, :], hm[:, kt, 0:NMEM], [wb, hm], ps, start=(kt == 0), stop=(kt == KT - 1))
                self.ecopy(kmem[:, f, :], ps[:, 0:NMEM], [ps], [kmem])
            else:
                for mt in range(2):
                    ps = self.psum.next()
                    for kt in range(KT):
                        self.mm(ps[:, 0:128], hm[:, kt, mt * 128:(mt + 1) * 128], wb[:, kt, :], [wb, hm], ps, start=(kt == 0), stop=(kt == KT - 1))
                    self.ecopy(vmem[:, mt, (f - 8) * 128:(f - 7) * 128], ps[:, 0:128], [ps], [vmem])

    def gated_out(self, oT, X1, Z1, zrow, gcol, orow, b):
        P = self.P
        t0 = b * 512
        r = self.rstd_from(lambda kt: oT[:], [oT], 1, 1.0 / 128, [X1])
        P.dma(Z1[:], self.a(self.PROJ)[zrow, :, PAD + t0:PAD + t0 + 512], reads=[self.PROJ_T[zrow][b]], writes=[Z1])
        self.stt(oT[:], oT[:], self.gn[:, gcol:gcol + 1], r[:], ALU.mult, ALU.mult, [oT, self.gn, r], [oT])
        self.tt(oT[:], oT[:], Z1[:], ALU.mult, [oT, Z1], [oT])
        P.dma(self.a(self.OD)[orow, :, t0:t0 + 512], oT[:], reads=[oT], writes=[self.OD_T[orow][b]], q="pq")

    def mixer_A(self, l, b, h, smt):
        P = self.P
        t0 = b * 512
        sl = self.slots
        qc, kc, vc, X1, B1, L1, G1, EG, RK, QD, C1, NBU, E1, OT, Z1 = sl[0:15]
        import os
        self.ka = float(os.environ.get('KA', '99'))
        ones, ident, cw = self.ones, self.ident, self.cwsb
        PROJa = self.a(self.PROJ)
        for f0, c in ((F_AQ, qc), (F_AK, kc), (F_AV, vc)):
            f = f0 + h
            raw = self.raw.next()
            rd = [self.PROJ_T[f][b]] + ([self.PROJ_T[f][b - 1]] if b > 0 else [])
            P.dma(raw[:, 0:515], PROJa[f, :, t0:t0 + 515], reads=rd, writes=[raw])
            self.ts(c[:], raw[:, 0:512], cw[:, f, 0:1], ALU.mult, [raw, cw], [c])
            for k in range(1, 4):
                self.stt(c[:], raw[:, k:k + 512], cw[:, f, k:k + 1], c[:], ALU.mult, ALU.add, [raw, cw, c], [c])
            self.act(c[:], c[:], AF.Silu, [c], [c])
        if self.ka < 2:
            return
        for c, mul in ((qc, RS), (kc, 1.0)):
            r = self.rstd_from(lambda kt, c=c: c[:], [c], 1, 1.0, [X1])
            self.stt(c[:], c[:], mul, r[:], ALU.mult, ALU.mult, [c, r], [c])
        if self.ka < 3:
            return
        qn, kn = qc, kc
        hv, negA = self.hv, self.negA
        sm1 = self.tp.next()
        self.ts(sm1[0:16, :], smt[0:16, :], ident[0:16, h:h + 1], ALU.mult, [smt, ident], [sm1])
        ps = self.psum.next()
        self.mm(ps[:], ones[0:16, :], sm1[0:16, :], [ones, sm1], ps)
        self.act(B1[:], ps[:], AF.Sigmoid, [ps], [B1])
        sm2 = self.tp.next()
        self.ts(sm2[0:16, :], smt[0:16, :], ident[0:16, 8 + h:9 + h], ALU.mult, [smt, ident], [sm2])
        ps = self.psum.next()
        self.mm(ps[:], ones[0:16, :], sm2[0:16, :], [ones, sm2], ps)
        self.act(L1[:], ps[:], AF.Exp, [ps, hv], [L1], bias=hv[:, 8 + h:9 + h])
        self.act(L1[:], L1[:], AF.Ln, [L1, self.oneb], [L1], bias=self.oneb[:])
        self.ts(L1[:], L1[:], negA[:, h:h + 1], ALU.mult, [L1, negA], [L1])
        if self.ka < 4:
            return
        resetm = self.resetm
        P.op("dve", lambda e: e.tensor_tensor_scan(out=G1[:], data0=resetm[:], data1=L1[:], initial=0.0, op0=ALU.mult, op1=ALU.add), [resetm, L1], [G1])
        self.act(EG[:], G1[:], AF.Exp, [G1], [EG])
        egl = self.small.next()
        self.act(egl[:, 0:8], G1[:, 63::64], AF.Exp, [G1], [egl])
        self.tt(v3(RK[:], 64), bcast_mid(G1[:, 63::64], 64), v3(G1[:], 64), ALU.subtract, [G1], [RK])
        self.act(RK[:], RK[:], AF.Exp, [RK], [RK])
        if self.ka < 5:
            return
        self.tt(QD[:], qn[:], EG[:], ALU.mult, [qn, EG], [QD])
        self.tt(C1[:], B1[:], EG[:], ALU.mult, [B1, EG], [C1])
        gamcol = self.diag_cols(G1, X1)
        ngamcol = self.small.next()
        self.ts(ngamcol[:, 0:4], gamcol[:, 0:4], -1.0, ALU.mult, [gamcol], [ngamcol])
        betacol = self.diag_cols(B1, X1)
        c1col = self.diag_cols(C1, X1)
        rkcol = self.diag_cols(RK, X1)
        nup_s, nlow_s, up_i = self.nup_s, self.nlow_s, self.up_i
        self.tt(v3(NBU[:], 128), v3(B1[:], 128), bcast_outer(nup_s[:, :], 4), ALU.mult, [B1, nup_s], [NBU])
        if self.ka < 6:
            return
        for pr in range(4):
            s_ = slice(pr * 128, (pr + 1) * 128)
            self.act(E1[:, s_], G1[:, s_], AF.Exp, [G1, gamcol], [E1], scale=-1.0, bias=gamcol[:, pr:pr + 1])
            self.act(X1[:, s_], G1[:, s_], AF.Exp, [G1, ngamcol], [X1], scale=1.0, bias=ngamcol[:, pr:pr + 1])
        self.tt(E1[:], E1[:], X1[:], ALU.min, [E1, X1], [E1])
        if self.ka < 7:
            return
        if self.ka < 7.1:
            return
        yield
        S_ = self.SA[h]
        m = self.m128
        res = {}

        def prep(pr):
            s_ = slice(pr * 128, (pr + 1) * 128)
            ps = self.psum.next()
            self.mm(ps[:, 0:128], kn[:, s_], kn[:, s_], [kn], ps)
            M1 = m.next()
            self.tt(M1[:], ps[:, 0:128], E1[:, s_], ALU.mult, [ps, E1], [M1])
            Pm, Qm, Tt = m.next(), m.next(), m.next()
            self.stt(Pm[:], M1[:], betacol[:, pr:pr + 1], nlow_s[:], ALU.mult, ALU.mult, [M1, betacol, nlow_s], [Pm])
            self.tt(Qm[:], M1[:], NBU[:, s_], ALU.mult, [M1, NBU], [Qm])
            self.tt(Tt[:], Qm[:], ident[:], ALU.add, [Qm, ident], [Tt])
            yield
            for k in range(1, 6):
                ps = self.psum.next()
                self.mm(ps[:, 0:128], Qm[:], Pm[:], [Qm, Pm], ps)
                if k < 5:
                    psq = self.psum.next()
                    self.mm(psq[:, 0:128], Pm[:], Qm[:], [Qm, Pm], psq)
                Pn = m.next()
                self.ecopy(Pn[:], ps[:, 0:128], [ps], [Pn], eng="act")
                if k < 5:
                    Qn = m.next()
                    self.ecopy(Qn[:], psq[:, 0:128], [psq], [Qn], eng="dve")
                yield
                ps2 = self.psum.next()
                self.mm(ps2[:, 0:128], Pn[:], Tt[:], [Pn, Tt], ps2)
                Tn = m.next()
                self.tt(Tn[:], ps2[:, 0:128], Tt[:], ALU.add, [ps2, Tt], [Tn])
                Pm, Tt = Pn, Tn
                if k < 5:
                    Qm = Qn
                yield
            ps = self.psum.next()
            self.tr(ps[:, 0:128], kn[:, s_], [kn], ps)
            kbg, kdt = m.next(), m.next()
            self.ts(kbg[:], ps[:, 0:128], c1col[:, pr:pr + 1], ALU.mult, [ps, c1col], [kbg])
            self.ts(kdt[:], ps[:, 0:128], rkcol[:, pr:pr + 1], ALU.mult, [ps, rkcol], [kdt])
            ps = self.psum.next()
            self.tr(ps[:, 0:128], vc[:, s_], [vc], ps)
            vb = m.next()
            self.ts(vb[:], ps[:, 0:128], betacol[:, pr:pr + 1], ALU.mult, [ps, betacol], [vb])
            yield
            ps = self.psum.next()
            self.mm(ps[:, 0:128], Tt[:], vb[:], [Tt, vb], ps)
            U = m.next()
            self.ecopy(U[:], ps[:, 0:128], [ps], [U], eng="act")
            ps = self.psum.next()
            self.mm(ps[:, 0:128], kbg[:], Tt[:], [kbg, Tt], ps)
            WTt = m.next()
            self.ecopy(WTt[:], ps[:, 0:128], [ps], [WTt], eng="dve")
            yield
            ps = self.psum.next()
            self.mm(ps[:, 0:128], kn[:, s_], qn[:, s_], [kn, qn], ps)
            AT = m.next()
            self.tt(AT[:], ps[:, 0:128], E1[:, s_], ALU.mult, [ps, E1], [AT])
            self.tt(AT[:], AT[:], up_i[:], ALU.mult, [AT, up_i], [AT])
            res[pr] = (kdt, U, WTt, AT)
            yield

        def rec(pr):
            kdt, U, WTt, AT = res[pr]
            un = m.next()
            for c in range(2):
                r0 = 64 * c
                ch = pr * 2 + c
                cs = slice(ch * 64, ch * 64 + 64)
                ps = self.psum.next()
                self.mm(ps[:, 0:128], WTt[:], S_[:], [WTt, S_], ps)
                self.tt(un[r0:r0 + 64, :], U[r0:r0 + 64, :], ps[r0:r0 + 64, 0:128], ALU.subtract, [U, ps], [un])
                yield
                pso = self.psum.next()
                self.mm(pso[:, 0:64], S_[:], QD[:, cs], [S_, QD], pso, start=True, stop=False)
                self.mm(pso[:, 0:64], un[r0:r0 + 64, :], AT[r0:r0 + 64, r0:r0 + 64], [un, AT], pso, start=False, stop=True)
                self.ecopy(OT[:, cs], pso[:, 0:64], [pso], [OT], eng="act")
                pss = self.psum.next()
                self.mm(pss[:, 0:128], kdt[r0:r0 + 64, :], un[r0:r0 + 64, :], [kdt, un], pss)
                self.stt(S_[:], S_[:], egl[:, ch:ch + 1], pss[:, 0:128], ALU.mult, ALU.add, [S_, egl, pss], [S_])
                yield

        yield from rr([prep(0), prep(1), prep(2), prep(3)])
        for pr in range(4):
            yield from rec(pr)
        self.gated_out(OT, X1, Z1, F_AZ + h, 0, h, b)

    def mixer_B(self, l, b, h):
        P = self.P
        t0 = b * 512
        sl = self.slots
        q, f_, vi, Z1, X1, B_, QT, KTl, QE, KE, OT, KK = sl[16:28]
        PROJa = self.a(self.PROJ)
        for row, dst in ((F_BQ + h, q), (F_BF + h, f_), (F_BI + h, vi)):
            P.dma(dst[:], PROJa[row, :, PAD + t0:PAD + t0 + 512], reads=[self.PROJ_T[row][b]], writes=[dst])
        lball, omlb, resetm = self.lball, self.omlb, self.resetm
        self.act(f_[:], f_[:], AF.Sigmoid, [f_], [f_])
        self.ts(f_[:], f_[:], omlb[:, l, h:h + 1], ALU.mult, [f_, omlb, lball], [f_], s2=lball[:, l, h:h + 1], op1=ALU.add)
        self.ts(KK[:], f_[:], -1.0, ALU.mult, [f_], [KK], s2=1.0, op1=ALU.add)
        self.act(f_[:], f_[:], AF.Ln, [f_], [f_])
        P.op("dve", lambda e: e.tensor_tensor_scan(out=B_[:], data0=resetm[:], data1=f_[:], initial=0.0, op0=ALU.mult, op1=ALU.add), [resetm, f_], [B_])
        ebl = self.small.next()
        self.act(ebl[:, 0:8], B_[:, 63::64], AF.Exp, [B_], [ebl])
        self.tt(v3(X1[:], 64), v3(B_[:], 64), bcast_mid(B_[:, 31::64], 64), ALU.subtract, [B_], [X1])
        self.act(QT[:], X1[:], AF.Exp, [X1], [QT])
        self.stt(QT[:], q[:], RS, QT[:], ALU.mult, ALU.mult, [q, QT], [QT])
        self.act(KTl[:], X1[:], AF.Exp, [X1], [KTl], scale=-1.0)
        self.tt(KTl[:], KTl[:], KK[:], ALU.mult, [KTl, KK], [KTl])
        self.act(QE[:], B_[:], AF.Exp, [B_], [QE])
        self.stt(QE[:], q[:], RS, QE[:], ALU.mult, ALU.mult, [q, QE], [QE])
        self.tt(v3(KE[:], 64), bcast_mid(B_[:, 63::64], 64), v3(B_[:], 64), ALU.subtract, [B_], [KE])
        self.act(KE[:], KE[:], AF.Exp, [KE], [KE])
        self.tt(KE[:], KE[:], KK[:], ALU.mult, [KE, KK], [KE])
        yield
        S_ = self.SB[h]
        m = self.m128b
        up_i = self.up_i
        for pr in range(4):
            s_ = slice(pr * 128, (pr + 1) * 128)
            ps = self.psum.next()
            self.tr(ps[:, 0:128], vi[:, s_], [vi], ps)
            vt = m.next()
            self.ecopy(vt[:], ps[:, 0:128], [ps], [vt], eng="act")
            ps = self.psum.next()
            self.tr(ps[:, 0:128], KE[:, s_], [KE], ps)
            ket = m.next()
            self.ecopy(ket[:], ps[:, 0:128], [ps], [ket], eng="dve")
            ps = self.psum.next()
            self.mm(ps[:, 0:128], KTl[:, s_], QT[:, s_], [KTl, QT], ps)
            AT = m.next()
            self.ts(AT[:], ps[:, 0:128], 1e30, ALU.min, [ps], [AT], s2=-1e30, op1=ALU.max)
            self.tt(AT[:], AT[:], up_i[:], ALU.mult, [AT, up_i], [AT])
            for c in range(2):
                r0 = 64 * c
                ch = pr * 2 + c
                cs = slice(ch * 64, ch * 64 + 64)
                pso = self.psum.next()
                self.mm(pso[:, 0:64], S_[:], QE[:, cs], [S_, QE], pso, start=True, stop=False)
                self.mm(pso[:, 0:64], vt[r0:r0 + 64, :], AT[r0:r0 + 64, r0:r0 + 64], [vt, AT], pso, start=False, stop=True)
                self.ecopy(OT[:, cs], pso[:, 0:64], [pso], [OT], eng="act")
                pss = self.psum.next()
                self.mm(pss[:, 0:128], ket[r0:r0 + 64, :], vt[r0:r0 + 64, :], [ket, vt], pss)
                self.stt(S_[:], S_[:], ebl[:, ch:ch + 1], pss[:, 0:128], ALU.mult, ALU.add, [S_, ebl, pss], [S_])
                yield
        self.gated_out(OT, X1, Z1, F_BZ + h, 1, 8 + h, b)

    def mixer_C(self, l, b):
        P = self.P
        t0 = b * 512
        sl = self.slots
        PROJa = self.a(self.PROJ)
        kmem, vmem, ident = self.kmem, self.vmem, self.ident
        m = self.m128
        SC = 256 ** -0.5
        for hc in range(4):
            cqs = []
            for j in range(2):
                row = F_CQ + 2 * hc + j
                st = sl[28 + j]
                P.dma(st[:], PROJa[row, :, PAD + t0:PAD + t0 + 512], reads=[self.PROJ_T[row][b]], writes=[st])
                cb = self.cqb[j]
                self.ecopy(cb[:], st[:], [st], [cb], eng="dve")
                cqs.append(cb)
            o0, o1 = sl[30], sl[31]
            for tq in range(4):
                s_ = slice(tq * 128, (tq + 1) * 128)
                ps = self.psum.next()
                for j in range(2):
                    self.mm(ps[:, 0:NMEM], cqs[j][:, s_], kmem[:, 2 * hc + j, :], [cqs[j], kmem], ps, start=(j == 0), stop=(j == 1))
                mx = self.small.next()
                P.op("dve", lambda e, mx=mx, ps=ps: e.tensor_reduce(out=mx[:, 0:1], in_=ps[:, 0:NMEM], axis=AX.X, op=ALU.max), [ps], [mx])
                self.ts(mx[:, 1:2], mx[:, 0:1], -SC, ALU.mult, [mx], [mx])
                pe_ = self.tp.next()
                self.act(pe_[:, 0:NMEM], ps[:, 0:NMEM], AF.Exp, [ps, mx], [pe_, mx], scale=SC, bias=mx[:, 1:2], accum_out=mx[:, 2:3])
                self.recip(mx[:, 3:4], mx[:, 2:3], [mx], [mx])
                self.ts(pe_[:, 0:NMEM], pe_[:, 0:NMEM], mx[:, 3:4], ALU.mult, [pe_, mx], [pe_])
                pTs = []
                for mt in range(2):
                    ps2 = self.psum.next()
                    self.tr(ps2[:, 0:128], pe_[:, mt * 128:(mt + 1) * 128], [pe_], ps2)
                    pT = self.pTb[mt]
                    self.ecopy(pT[:], ps2[:, 0:128], [ps2], [pT])
                    pTs.append(pT)
                for j, o in ((0, o0), (1, o1)):
                    ps3 = self.psum.next()
                    for mt in range(2):
                        self.mm(ps3[:, 0:128], vmem[:, mt, (2 * hc + j) * 128:(2 * hc + j + 1) * 128], pTs[mt][:], [vmem, pTs[mt]], ps3, start=(mt == 0), stop=(mt == 1))
                    self.ecopy(o[:, s_], ps3[:, 0:128], [ps3], [o])
            for j, o in ((0, o0), (1, o1)):
                row = 16 + 2 * hc + j
                P.dma(self.a(self.OD)[row, :, t0:t0 + 512], o[:], reads=[o], writes=[self.OD_T[row][b]], q="pq")

    def phase3(self, l, b, last):
        P = self.P
        t0 = b * 512
        sl = self.slots
        xts, ys = sl[0:16], sl[16:32]
        hb, big, gsb = self.hb, self.big, self.gsb
        XTv = self.a(self.XT).rearrange("(kt p) s -> p kt s", p=128)
        ODa, PROJa = self.a(self.OD), self.a(self.PROJ)
        w_br, w_out, w1, w2 = self.a(self.w_br), self.a(self.w_out), self.a(self.w1), self.a(self.w2)
        for br in range(3):
            stgs = []
            for kt in range(8):
                row = br * 8 + kt
                st = ys[kt]
                P.dma(st[:], ODa[row, :, t0:t0 + 512], reads=[self.OD_T[row][b]], writes=[st])
                self.ecopy(big[:, kt, :], st[:], [st], [big], eng=("dve" if kt % 2 else "act"))
            for f in range(16):
                wb = self.load_w(w_br[l, br, f], key=("br", br, f), b=b, wb_ap=self.a(self.wb_br)[br, f])
                ps = self.psum.next()
                for kt in range(8):
                    self.mm(ps[:], wb[:, kt, :], big[:, kt, :], [wb, big], ps, start=(kt == 0), stop=(kt == 7))
                g = self.tp.next()
                row = F_G + br * 16 + f
                P.dma(g[:], PROJa[row, :, PAD + t0:PAD + t0 + 512], reads=[self.PROJ_T[row][b]], writes=[g])
                acc = xts[f]
                if br == 0:
                    self.tt(acc[:], ps[:], g[:], ALU.mult, [ps, g], [acc])
                else:
                    self.tt(g[:], ps[:], g[:], ALU.mult, [ps, g], [g])
                    if br == 1:
                        self.tt(acc[:], acc[:], g[:], ALU.add, [acc, g], [acc])
                    else:
                        self.tt(hb[:, f, :], acc[:], g[:], ALU.add, [acc, g], [hb])
        for f in range(16):
            wb = self.load_w(w_out[l, f], key=("out", f), b=b, wb_ap=self.a(self.wb_out)[f])
            ps = self.psum.next()
            for kt in range(KT):
                self.mm(ps[:], wb[:, kt, :], hb[:, kt, :], [wb, hb], ps, start=(kt == 0), stop=(kt == KT - 1))
            self.ecopy(ys[f][:], ps[:], [ps], [ys[f]])
        P.dma(self.slotbuf[:, 0:16, :], XTv[:, :, t0:t0 + 512], reads=[self.XT_T[b]], writes=xts)
        t1, t2 = self.tp.next(), self.tp.next()
        r = self.rstd_from(lambda kt: ys[kt][:], ys, KT, 1.0 / D, [t1, t2])
        for kt in range(KT):
            self.stt(ys[kt][:], ys[kt][:], gsb[:, 1, kt:kt + 1], r[:], ALU.mult, ALU.mult, [ys[kt], gsb, r], [ys[kt]])
            self.tt(xts[kt][:], xts[kt][:], ys[kt][:], ALU.add, [xts[kt], ys[kt]], [xts[kt]])
        t1, t2 = self.tp.next(), self.tp.next()
        r = self.rstd_from(lambda kt: xts[kt][:], xts, KT, 1.0 / D, [t1, t2])
        for kt in range(KT):
            self.stt(hb[:, kt, :], xts[kt][:], gsb[:, 2, kt:kt + 1], r[:], ALU.mult, ALU.mult, [xts[kt], gsb, r], [hb])
        for qtr in range(4):
            for fl in range(16):
                f = qtr * 16 + fl
                wb = self.load_w(w1[l, f], key=("w1", f), b=b, wb_ap=self.a(self.wb_w1)[f])
                ps = self.psum.next()
                for kt in range(KT):
                    self.mm(ps[:], wb[:, kt, :], hb[:, kt, :], [wb, hb], ps, start=(kt == 0), stop=(kt == KT - 1))
                rl = self.tp.next()
                self.act(rl[:], ps[:], AF.Relu, [ps], [rl])
                self.tt(big[:, fl, :], rl[:], rl[:], ALU.mult, [rl], [big])
            for f in range(16):
                wb = self.load_w(w2[l, f, :, qtr * 16:(qtr + 1) * 16, :], key=("w2", f, qtr), b=b, wb_ap=self.a(self.wb_w2)[f, qtr])
                ps = self.psum.next()
                for kt in range(16):
                    self.mm(ps[:], wb[:, kt, :], big[:, kt, :], [wb, big], ps, start=(kt == 0), stop=(kt == 15))
                if qtr == 0:
                    self.ecopy(ys[f][:], ps[:], [ps], [ys[f]])
                else:
                    self.tt(ys[f][:], ys[f][:], ps[:], ALU.add, [ys[f], ps], [ys[f]])
        t1, t2 = self.tp.next(), self.tp.next()
        r = self.rstd_from(lambda kt: ys[kt][:], ys, KT, 1.0 / D, [t1, t2])
        for kt in range(KT):
            self.stt(ys[kt][:], ys[kt][:], gsb[:, 3, kt:kt + 1], r[:], ALU.mult, ALU.mult, [ys[kt], gsb, r], [ys[kt]])
            self.tt(xts[kt][:], xts[kt][:], ys[kt][:], ALU.add, [xts[kt], ys[kt]], [xts[kt]])
        if last:
            ov = self.a(self.outT).rearrange("(kt p) s -> p kt s", p=128)
            P.dma(ov[:, :, t0:t0 + 512], self.slotbuf[:, 0:16, :], reads=xts, writes=[self.OUT_T], q="pq")
        else:
            P.dma(XTv[:, :, t0:t0 + 512], self.slotbuf[:, 0:16, :], reads=xts, writes=[self.XT_T[b]], q="pq")

    def build(self, stop_after=None):
        P, L, NB = self.P, self.L, self.NB
        self.cqb = [P.sb([128, 512], BF16, f"cqb{j}") for j in range(2)]
        self.pTb = [P.sb([128, 128], BF16, f"pTb{j}") for j in range(2)]
        self.smtb = Rot([P.sb([16, 512], F32, f"smt{j}") for j in range(2)])
        import os
        stop = os.environ.get("KSTOP", "")
        order = ["prologue", "p1", "mem", "A", "B", "C", "p3"]
        lim = order.index(stop) if stop in order else 99
        self.prologue()
        for l in range(L):
            self.layer_params(l)
            pairs = [(b0, min(2, NB - b0)) for b0 in range(0, NB, 2)]

            def mixers(b):
                smt = self.smtb.next()
                P.dma(smt[0:16, :], self.a(self.PSM)[:, b * 512:(b + 1) * 512], reads=[self.PSM_T[b]], writes=[smt])
                for h in range(8):
                    gl = []
                    if lim >= 3:
                        gl.append(self.mixer_A(l, b, h, smt))
                    if lim >= 4:
                        gl.append(self.mixer_B(l, b, h))
                    yield from rr(gl)
                if lim >= 5:
                    self.mixer_C(l, b)
                yield

            def chain(bs):
                for b in bs:
                    yield from mixers(b)

            if lim >= 1:
                for _ in self.phase1(l, *pairs[0]):
                    pass
            if lim >= 2:
                self.mem_prep(l)
            for i, (b0, nb) in enumerate(pairs):
                main = chain(range(b0, b0 + nb))
                side = self.phase1(l, *pairs[i + 1]) if (i + 1 < len(pairs) and lim >= 1) else None
                for _ in rrw(main, side, 8):
                    pass
            if lim >= 6:
                for b in range(NB):
                    self.phase3(l, b, last=(l == L - 1))
        st = P.emit()
        P.close()
        return st

NCORES = 4
_CACHE = {}


def _tile_w(w, kt):
    K, F = w.shape
    return np.ascontiguousarray(w.reshape(kt, 128, F // 128, 128).transpose(2, 1, 0, 3))


def _fm_vec(g):
    return np.ascontiguousarray(g.reshape(-1, 128).T)


def prep_weights(inp, L):
    f32 = np.float32
    w_in = np.asarray(inp["w_in"], f32)
    cols_main = np.r_[0:4096, 4112:15376]
    out = {}
    out["w_in"] = np.stack([_tile_w(w_in[l][:, cols_main], 16) for l in range(L)])
    wsm = w_in[:, :, 4096:4112]
    out["w_sm"] = np.ascontiguousarray(wsm.reshape(L, 16, 128, 16).transpose(0, 2, 1, 3))
    out["w_kv"] = np.stack([_tile_w(np.asarray(inp["w_kv_mem"][l], f32), 16) for l in range(L)])
    out["w_br"] = np.stack([np.stack([_tile_w(np.asarray(inp[n][l], f32), 8) for n in ("w_br_a", "w_br_b", "w_br_c")]) for l in range(L)])
    out["w_out"] = np.stack([_tile_w(np.asarray(inp["w_out"][l], f32), 16) for l in range(L)])
    out["w1"] = np.stack([_tile_w(np.asarray(inp["w_mlp_in"][l], f32), 16) for l in range(L)])
    out["w2"] = np.stack([_tile_w(np.asarray(inp["w_mlp_out"][l], f32), 64) for l in range(L)])
    out["gains"] = np.stack([np.stack([_fm_vec(np.asarray(inp[n][l], f32)) for n in ("g_pre_mix", "g_post_mix", "g_pre_mlp", "g_post_mlp", "g_mem")], axis=1) for l in range(L)])
    cw = np.asarray(inp["conv_a"], f32)
    out["convw"] = np.ascontiguousarray(cw.reshape(L, 4, 24, 128).transpose(0, 3, 2, 1))
    hv = np.concatenate([np.asarray(inp["a_log"], f32), np.asarray(inp["dt_bias"], f32)], axis=1)
    out["hvec"] = np.ascontiguousarray(np.broadcast_to(hv[:, None, :], (L, 128, 16)))
    out["gnv"] = np.ascontiguousarray(np.stack([np.asarray(inp["gn_a"], f32), np.asarray(inp["gn_b"], f32)], axis=2))
    lb = np.asarray(inp["lb_raw"], f32)
    out["lbraw"] = np.ascontiguousarray(lb.reshape(L, 8, 128).transpose(2, 1, 0))
    return out


def kernel(**inp):
    x = np.asarray(inp["x"], np.float32)
    mem = np.asarray(inp["mem"], np.float32)
    B, S, _ = x.shape
    L = inp["w_in"].shape[0]
    key = (S, L)
    if key not in _CACHE:
        k = K(S, L)
        k.build()
        _CACHE[key] = k
    k = _CACHE[key]
    w = prep_weights(inp, L)
    in_maps = []
    for c in range(B):
        m = dict(w)
        m["xT"] = np.ascontiguousarray(x[c].T)
        m["memT"] = np.ascontiguousarray(mem[c].T)
        in_maps.append(m)
    res = run_bass_kernel_spmd(k.nc, in_maps, core_ids=list(range(B)))
    out = np.stack([np.ascontiguousarray(res.results[c]["outT"].T) for c in range(B)])
    return out.astype(np.float32)
```
